# Optimizing a Trainium2 kernel written in Bass

```python
import math
import jax, jax.numpy as jnp
from jax import lax
import numpy as np

D_MODEL = 1024
BATCH = 4
SEQ = 4096
DEPTH = 2

GRID_W = 64
N_HEADS = 16
N_KV_HEADS = 4
HEAD_DIM = D_MODEL // N_HEADS
GROUP = N_HEADS // N_KV_HEADS
Q_BLOCK = 128
ROPE_THETA = 10000.0
ATTN_QKV = N_HEADS * HEAD_DIM + 2 * N_KV_HEADS * HEAD_DIM
GLA_HEADS = 4
GLA_DK = D_MODEL // 2 // GLA_HEADS
GLA_DV = D_MODEL // GLA_HEADS
GLA_QK = GLA_HEADS * GLA_DK
GLA_V = GLA_HEADS * GLA_DV
GATE_RANK = 16
GATE_TAU = 16.0
GLA_CHUNK = 64
GLA_IN = 2 * GLA_QK + 2 * GLA_V + 2 * GATE_RANK
D_FF = 4 * D_MODEL
N_MIXERS = 2
N_ATTN = (DEPTH + 1) // 2
N_GLA = DEPTH // 2
EPS = 1e-6

kernel_name = 'hybrid_gqa_axialrope_bigla_sqrelu'


def rmsnorm(x, g):
    x32 = x.astype(jnp.float32)
    y = x32 * lax.rsqrt(jnp.mean(x32 * x32, axis=-1, keepdims=True) + EPS)
    return (y * g.astype(jnp.float32)).astype(x.dtype)


def rope_angles(pos, dim):
    inv = ROPE_THETA ** (-jnp.arange(0, dim, 2, dtype=jnp.float32) / dim)
    ang = pos.astype(jnp.float32)[:, None] * inv[None, :]
    return jnp.cos(ang), jnp.sin(ang)


def apply_rot(x, cos, sin):
    half = x.shape[-1] // 2
    x32 = x.astype(jnp.float32)
    x1, x2 = x32[..., :half], x32[..., half:]
    c = cos[None, :, None, :]
    s = sin[None, :, None, :]
    return jnp.concatenate([x1 * c - x2 * s, x2 * c + x1 * s], axis=-1).astype(x.dtype)


def axial_rope(x, rope):
    cr, sr, cc, sc = rope
    half = HEAD_DIM // 2
    return jnp.concatenate([apply_rot(x[..., :half], cr, sr), apply_rot(x[..., half:], cc, sc)], axis=-1)


def attention_mixer(h, w_qkv, q_g, k_g, w_o, rope):
    B, S, _ = h.shape
    qkv = h @ w_qkv
    q, k, v = jnp.split(qkv, [N_HEADS * HEAD_DIM, (N_HEADS + N_KV_HEADS) * HEAD_DIM], axis=-1)
    q = rmsnorm(q.reshape(B, S, N_HEADS, HEAD_DIM), q_g)
    k = rmsnorm(k.reshape(B, S, N_KV_HEADS, HEAD_DIM), k_g)
    v = v.reshape(B, S, N_KV_HEADS, HEAD_DIM)
    q = axial_rope(q, rope)
    k = axial_rope(k, rope)
    n_blk = S // Q_BLOCK
    qb = jnp.moveaxis(q.reshape(B, n_blk, Q_BLOCK, N_KV_HEADS, GROUP, HEAD_DIM), 1, 0)
    scale = 1.0 / math.sqrt(HEAD_DIM)

    def block(qblk):
        s = jnp.einsum('bqhgd,bkhd->bhgqk', qblk, k, preferred_element_type=jnp.float32) * scale
        p = jax.nn.softmax(s, axis=-1).astype(v.dtype)
        return jnp.einsum('bhgqk,bkhd->bqhgd', p, v)

    o = lax.map(block, qb)
    o = jnp.moveaxis(o, 0, 1).reshape(B, S, N_HEADS * HEAD_DIM)
    return o @ w_o


def gla_chunked(q, k, v, logg):
    B, S, H, DK = q.shape
    DV = v.shape[-1]
    N = S // GLA_CHUNK
    to_c = lambda t: t.reshape(B, N, GLA_CHUNK, H, t.shape[-1]).transpose(0, 3, 1, 2, 4)
    q, k, v, logg = to_c(q), to_c(k), to_c(v), to_c(logg)
    b = jnp.cumsum(logg, axis=3)
    qe = q * jnp.exp(b)
    ke = k * jnp.exp(-b)
    mask = jnp.tril(jnp.ones((GLA_CHUNK, GLA_CHUNK), dtype=bool))
    a = jnp.einsum('bhncd,bhnjd->bhncj', qe, ke)
    a = jnp.where(mask, a, 0.0)
    o_intra = jnp.einsum('bhncj,bhnjv->bhncv', a, v)
    b_last = b[..., -1:, :]
    kd = k * jnp.exp(b_last - b)
    chunk_kv = jnp.einsum('bhncd,bhncv->bhndv', kd, v)
    decay = jnp.exp(b_last[..., 0, :])

    def step(state, inp):
        dec, ckv = inp
        return dec[..., None] * state + ckv, state

    init = jnp.zeros((B, H, DK, DV), jnp.float32)
    _, s_before = lax.scan(step, init, (jnp.moveaxis(decay, 2, 0), jnp.moveaxis(chunk_kv, 2, 0)))
    s_before = jnp.moveaxis(s_before, 0, 2)
    o = o_intra + jnp.einsum('bhncd,bhndv->bhncv', qe, s_before)
    return o.transpose(0, 2, 3, 1, 4).reshape(B, S, H, DV)


def gla_mixer(h, w_in, w_gate_up, b_gate, out_g, w_o):
    B, S, _ = h.shape
    f32 = jnp.float32
    proj = h @ w_in
    q, k, v, r, z = jnp.split(proj, [GLA_QK, 2 * GLA_QK, 2 * GLA_QK + GLA_V, 2 * GLA_QK + 2 * GLA_V], axis=-1)
    q = q.astype(f32).reshape(B, S, GLA_HEADS, GLA_DK) * (GLA_DK ** -0.5)
    k = k.astype(f32).reshape(B, S, GLA_HEADS, GLA_DK)
    v = v.astype(f32).reshape(B, S, GLA_HEADS, GLA_DV)
    z = z.astype(f32).reshape(B, S, 2, GATE_RANK)
    logit = jnp.einsum('bsir,ire->bsie', z, w_gate_up.astype(f32)) + b_gate.astype(f32)
    logg = (jax.nn.log_sigmoid(logit) / GATE_TAU).reshape(B, S, 2, GLA_HEADS, GLA_DK)
    o_f = gla_chunked(q, k, v, logg[:, :, 0])
    flip = lambda t: jnp.flip(t, axis=1)
    o_b = flip(gla_chunked(flip(q), flip(k), flip(v), flip(logg[:, :, 1])))
    o = rmsnorm(o_f + o_b, out_g).reshape(B, S, GLA_V)
    o = o * jax.nn.silu(r.astype(f32))
    return o.astype(h.dtype) @ w_o


def sq_relu_mlp(h, w_in, w_out):
    return jnp.square(jax.nn.relu(h @ w_in)) @ w_out


def setup_inputs(seed: int = 0) -> dict:
    key = jax.random.key(seed)
    ks = jax.random.split(key, 16)
    nrm = lambda k, shape, fan_in: jax.random.normal(k, shape, jnp.float32) * (fan_in ** -0.5)
    gain = lambda k, shape: 1.0 + 0.05 * jax.random.normal(k, shape, jnp.float32)
    return {
        'x': jax.random.normal(ks[0], (BATCH, SEQ, D_MODEL), jnp.float32),
        'norm_mix': gain(ks[1], (DEPTH, D_MODEL)),
        'norm_mlp': gain(ks[2], (DEPTH, D_MODEL)),
        'attn_w_qkv': nrm(ks[3], (N_ATTN, D_MODEL, ATTN_QKV), D_MODEL),
        'attn_q_norm': gain(ks[4], (N_ATTN, HEAD_DIM)),
        'attn_k_norm': gain(ks[5], (N_ATTN, HEAD_DIM)),
        'attn_w_o': nrm(ks[6], (N_ATTN, N_HEADS * HEAD_DIM, D_MODEL), N_HEADS * HEAD_DIM),
        'gla_w_in': nrm(ks[7], (N_GLA, D_MODEL, GLA_IN), D_MODEL),
        'gla_w_gate_up': nrm(ks[8], (N_GLA, 2, GATE_RANK, GLA_QK), GATE_RANK),
        'gla_b_gate': 0.1 * jax.random.normal(ks[9], (N_GLA, 2, GLA_QK), jnp.float32),
        'gla_out_norm': gain(ks[10], (N_GLA, GLA_DV)),
        'gla_w_o': nrm(ks[11], (N_GLA, GLA_V, D_MODEL), GLA_V),
        'mlp_w_in': nrm(ks[12], (DEPTH, D_MODEL, D_FF), D_MODEL),
        'mlp_w_out': nrm(ks[13], (DEPTH, D_FF, D_MODEL), D_FF),
        'final_norm': gain(ks[14], (D_MODEL,)),
    }


def reference(x, norm_mix, norm_mlp, attn_w_qkv, attn_q_norm, attn_k_norm, attn_w_o,
              gla_w_in, gla_w_gate_up, gla_b_gate, gla_out_norm, gla_w_o,
              mlp_w_in, mlp_w_out, final_norm):
    S = x.shape[1]
    rows = S // GRID_W
    row = jnp.repeat(jnp.arange(rows, dtype=jnp.int32), GRID_W)
    col = jnp.tile(jnp.arange(GRID_W, dtype=jnp.int32), rows)
    cr, sr = rope_angles(row, HEAD_DIM // 2)
    cc, sc = rope_angles(col, HEAD_DIM // 2)
    rope = (cr, sr, cc, sc)
    h = x
    for i in range(DEPTH):
        j = i // N_MIXERS
        hn = rmsnorm(h, norm_mix[i])
        if i % N_MIXERS == 0:
            h = h + attention_mixer(hn, attn_w_qkv[j], attn_q_norm[j], attn_k_norm[j], attn_w_o[j], rope)
        else:
            h = h + gla_mixer(hn, gla_w_in[j], gla_w_gate_up[j], gla_b_gate[j], gla_out_norm[j], gla_w_o[j])
        h = h + sq_relu_mlp(rmsnorm(h, norm_mlp[i]), mlp_w_in[i], mlp_w_out[i])
    return rmsnorm(h, final_norm)
```

```python
import math
from functools import reduce

import numpy as np
import concourse.bass as bass
import concourse.mybir as mybir
from concourse.bass_utils import run_bass_kernel_spmd

F32 = mybir.dt.float32
BF16 = mybir.dt.bfloat16
I32 = mybir.dt.int32
AF = mybir.ActivationFunctionType
ALU = mybir.AluOpType
AX = mybir.AxisListType

NCORES = 8
D = 1024
T = 2048
TA = 4096
EPS = 1e-6
HO = [0, 4, 1, 5, 2, 6, 3, 7, 8, 12, 9, 13, 10, 14, 11, 15]
ENGS = ("pe", "act", "dve", "pool", "sp")
NDSEM = 12
SBUF_BYTES = 212800


class Buf:
    __slots__ = ("w", "r")

    def __init__(self):
        self.w = None
        self.r = {}


class Tile:
    def __init__(self, ap):
        self.ap = ap
        self.b = Buf()


class Prog:
    def __init__(self):
        self.lists = {e: [] for e in ENGS}
        self.count = {}
        self.waited = {e: {} for e in ENGS}
        self.dma_i = {"sp": 0, "pool": 0}

    def _deps(self, eng, reads, writes):
        toks = {}

        def add(t):
            if t is None:
                return
            k, v = t
            if eng == "pe" and k == "pe":
                return
            if toks.get(k, 0) < v:
                toks[k] = v

        for b in reads:
            add(b.w)
        for b in writes:
            add(b.w)
            for k, v in b.r.items():
                add((k, v))
        for k, v in toks.items():
            if self.waited[eng].get(k, 0) < v:
                self.lists[eng].append(("w", k, v))
                self.waited[eng][k] = v

    def _mark(self, tok, reads, writes):
        k, v = tok
        for b in reads:
            if b.r.get(k, 0) < v:
                b.r[k] = v
        for b in writes:
            b.w = tok
            b.r = {}

    def op(self, eng, fn, reads=(), writes=()):
        self._deps(eng, reads, writes)
        self.count[eng] = self.count.get(eng, 0) + 1
        tok = (eng, self.count[eng])
        self.lists[eng].append(("o", fn, eng, 1))
        self._mark(tok, reads, writes)
        return tok

    def dma(self, q, fn, reads=(), writes=()):
        self._deps(q, reads, writes)
        i = self.dma_i[q]
        self.dma_i[q] = i + 1
        key = "d%s%d" % (q, i % NDSEM)
        self.count[key] = self.count.get(key, 0) + 16
        tok = (key, self.count[key])
        self.lists[q].append(("o", fn, key, 16))
        self._mark(tok, reads, writes)
        return tok

    def custom(self, eng, fn, key, reads=(), writes=()):
        self._deps(eng, reads, writes)
        self.count[key] = self.count.get(key, 0) + 1
        tok = (key, self.count[key])
        self.lists[eng].append(("c", fn, key, 1))
        self._mark(tok, reads, writes)
        return tok

    def barrier(self):
        for e in ENGS:
            for k, v in self.count.items():
                if k == e and e != "pe":
                    pass
                if self.waited[e].get(k, 0) < v:
                    self.lists[e].append(("w", k, v))
                    self.waited[e][k] = v


def _prod(s):
    return reduce(lambda a, b: a * b, s, 1)


def build_program(stop_after=None, only=None):
    nc = bass.Bass("TRN2", target_bir_lowering=False)

    def din(name, shape):
        return nc.dram_tensor(name, list(shape), F32, kind="ExternalInput").ap()

    xT_d = din("xT", [128, 8, TA])
    pos_d = din("pos", [128, 64])
    gains_d = din("gains", [128, 40])
    qkg_d = din("qkg", [128, 128])
    wq_d = din("wq", [128, 8, 1024])
    wkv_d = din("wkv", [128, 8, 512])
    wo_d = din("wo", [128, 8, 1024])
    win_d = din("win", [2, 8, 128, 8, 512])
    wout_d = din("wout", [2, 8, 128, 4, 1024])
    wg_d = din("wg", [2, 128, 8, 1536])
    wz_d = din("wz", [128, 8, 32])
    wup_d = din("wup", [33, 2, 512])
    og_d = din("og", [128, 2])
    wo2_d = din("wo2", [2, 128, 4, 1024])
    mex_d = din("mex", [128, 2])
    y_d = nc.dram_tensor("y", [128, 8, T], F32, kind="ExternalOutput").ap()
    ib_t = [nc.dram_tensor("ib%d" % g, [128, 512], F32) for g in range(2)]
    ob_t = [nc.dram_tensor("ob%d" % g, [256, 512], F32) for g in range(2)]

    P = Prog()

    with (
        nc.sbuf_tensor("arena", [128, SBUF_BYTES // 4], F32) as A,
        nc.psum_tensor("psum", [128, 4096], F32) as PS,
    ):
        top = [0]

        def alloc(dtype, *shape):
            esz = 4 if dtype in (F32, I32) else 2
            nbytes = (_prod(shape) * esz + 63) // 64 * 64
            off = top[0]
            top[0] += nbytes
            assert top[0] <= SBUF_BYTES, ("SBUF overflow", top[0])
            ap = A[:, off // 4:(off + nbytes) // 4]
            if dtype != F32:
                ap = ap.bitcast(dtype)
            ap = ap[:, 0:_prod(shape)]
            if len(shape) > 1:
                names = ["a%d" % i for i in range(len(shape))]
                ap = ap.rearrange("p (%s) -> p %s" % (" ".join(names), " ".join(names)),
                                  **{n: s for n, s in zip(names[:-1], shape[:-1])})
            return Tile(ap)

        def psbank(b, dtype=F32):
            ap = PS[:, b * 512:(b + 1) * 512]
            if dtype != F32:
                ap = ap.bitcast(dtype)
            return Tile(ap)

        def f_act(out, in_, func, scale=1.0, bias=None):
            if bias is None:
                return lambda e: e.activation(out=out, in_=in_, func=func, scale=scale)
            return lambda e: e.activation(out=out, in_=in_, func=func, scale=scale, bias=bias)

        def f_tt(out, in0, in1, op):
            return lambda e: e.tensor_tensor(out=out, in0=in0, in1=in1, op=op)

        def f_stt(out, in0, scalar, in1, op0, op1):
            return lambda e: e.scalar_tensor_tensor(out=out, in0=in0, scalar=scalar, in1=in1, op0=op0, op1=op1)

        def f_ts(out, in0, s1, op0, s2=None, op1=None):
            if op1 is None:
                return lambda e: e.tensor_scalar(out=out, in0=in0, scalar1=s1, scalar2=None, op0=op0)
            return lambda e: e.tensor_scalar(out=out, in0=in0, scalar1=s1, scalar2=s2, op0=op0, op1=op1)

        def f_copy(out, in_):
            return lambda e: e.tensor_copy(out=out, in_=in_)

        def f_mm(group):
            def fn(e):
                inst = None
                for (o, l, r, st, sp) in group:
                    inst = e.matmul(o, l, r, start=st, stop=sp)
                return inst
            return fn

        def f_tr(group):
            def fn(e):
                inst = None
                for (o, i, idn) in group:
                    inst = e.transpose(o, i, idn)
                return inst
            return fn

        def f_dma(out, in_, cast=False):
            if cast:
                return lambda e: e.dma_start(out=out, in_=in_, max_dma_last_dim=4096)
            return lambda e: e.dma_start(out=out, in_=in_)

        def acc_group(out, pairs):
            n = len(pairs)
            return [(out, l, r, i == 0, i == n - 1) for i, (l, r) in enumerate(pairs)]

        hT = alloc(F32, 8, T)
        hT_b = [[Buf() for _ in range(4)] for _ in range(8)]
        hT_all = [b for row in hT_b for b in row]
        onesf = alloc(F32, 128)
        UT = alloc(F32, 128)
        LT = alloc(F32, 128)
        SL = alloc(F32, 128)
        SU = alloc(F32, 128)
        IDF = alloc(F32, 128)
        ident = alloc(BF16, 128)
        onesb = alloc(BF16, 128)
        gains = alloc(F32, 5, 8)
        qkg = alloc(F32, 2, 64)
        og = alloc(F32, 2)
        mex = alloc(F32, 2)
        pos = alloc(F32, 64)
        tab = alloc(F32, 2, 32, 2, 16)
        persist_top = top[0]

        P.dma("sp", f_dma(hT.ap, xT_d[:, :, 0:T]), writes=hT_all)
        P.dma("sp", f_dma(gains.ap, gains_d.rearrange("p (a b) -> p a b", a=5)), writes=[gains.b])
        P.dma("sp", f_dma(qkg.ap, qkg_d.rearrange("p (a b) -> p a b", a=2)), writes=[qkg.b])
        P.dma("sp", f_dma(og.ap, og_d), writes=[og.b])
        P.dma("sp", f_dma(mex.ap, mex_d), writes=[mex.b])
        P.dma("sp", f_dma(pos.ap, pos_d), writes=[pos.b])

        P.op("pool", lambda e: e.memset(onesf.ap, 1.0), writes=[onesf.b])
        P.op("pool", lambda e: e.memset(onesb.ap, 1.0), writes=[onesb.b])

        def mk_mask(dst, cm, step, cmp):
            P.op("pool", lambda e: e.affine_select(out=dst.ap, in_=onesf.ap, pattern=[[step, 128]],
                                                   compare_op=cmp, fill=0.0, base=0, channel_multiplier=cm),
                 reads=[onesf.b], writes=[dst.b])

        mk_mask(UT, -1, 1, ALU.is_ge)
        mk_mask(LT, 1, -1, ALU.is_ge)
        mk_mask(SL, 1, -1, ALU.is_gt)
        mk_mask(SU, -1, 1, ALU.is_gt)
        mk_mask(IDF, 1, -1, ALU.is_equal)
        P.op("dve", f_copy(ident.ap, IDF.ap), reads=[IDF.b], writes=[ident.b])

        m0 = top[0]
        invf = alloc(F32, 16)
        ang = alloc(F32, 64, 16)
        u = alloc(F32, 2, 1024)
        ki = alloc(I32, 2048)
        kf_ = alloc(F32, 2048)
        fr = alloc(F32, 2048)
        ng = alloc(F32, 2048)
        for f in range(16):
            val = float(10000.0 ** (-(2.0 * f) / 32.0))
            P.op("pool", (lambda f=f, val=val: (lambda e: e.memset(invf.ap[:, f:f + 1], val)))(), writes=[invf.b])
        P.op("dve", f_tt(ang.ap, pos.ap.unsqueeze(2).broadcast_to([128, 64, 16]),
                         invf.ap.unsqueeze(1).broadcast_to([128, 64, 16]), ALU.mult),
             reads=[pos.b, invf.b], writes=[ang.b])
        angf = ang.ap.rearrange("p a b -> p (a b)")
        inv2pi = float(1.0 / (2.0 * math.pi))
        P.op("dve", f_ts(u.ap[:, 0, :], angf, inv2pi, ALU.mult, 0.5, ALU.add), reads=[ang.b], writes=[u.b])
        P.op("dve", f_ts(u.ap[:, 1, :], angf, inv2pi, ALU.mult, 0.75, ALU.add), reads=[ang.b], writes=[u.b])
        uf = u.ap.rearrange("p a b -> p (a b)")
        P.op("dve", f_copy(ki.ap, uf), reads=[u.b], writes=[ki.b])
        P.op("dve", f_copy(kf_.ap, ki.ap), reads=[ki.b], writes=[kf_.b])
        P.op("dve", f_tt(fr.ap, uf, kf_.ap, ALU.subtract), reads=[u.b, kf_.b], writes=[fr.b])
        P.op("dve", lambda e: e.tensor_single_scalar(out=ng.ap, in_=fr.ap, scalar=0.0, op=ALU.is_lt),
             reads=[fr.b], writes=[ng.b])
        P.op("dve", f_tt(fr.ap, fr.ap, ng.ap, ALU.add), reads=[fr.b, ng.b], writes=[fr.b])
        P.op("dve", f_ts(fr.ap, fr.ap, float(2.0 * math.pi), ALU.mult, float(-math.pi), ALU.add),
             reads=[fr.b], writes=[fr.b])
        P.op("dve", f_ts(fr.ap, fr.ap, -3.141592, ALU.max, 3.141592, ALU.min), reads=[fr.b], writes=[fr.b])
        P.op("act", f_act(tab.ap.rearrange("p a b c d -> p (a b c d)"), fr.ap, AF.Sin), reads=[fr.b], writes=[tab.b])
        P.barrier()
        top[0] = m0

        def rmsnorm_group(xsrc, xbufs, gidx, dst, dstbufs, n, ps_ss, lnv, rstd, sq=None, sqbufs=None):
            if sq is None:
                sq, sqbufs = dst, dstbufs
            P.op("act", f_act(sq, xsrc, AF.Square), reads=xbufs, writes=sqbufs)
            P.op("pe", f_mm(acc_group(ps_ss.ap[:, 0:n], [(onesb.ap, sq[:, ck, :]) for ck in range(8)])),
                 reads=sqbufs + [onesb.b], writes=[ps_ss.b])
            P.op("act", f_act(lnv.ap[:, 0:n], ps_ss.ap[:, 0:n], AF.Ln, scale=1.0 / D, bias=EPS),
                 reads=[ps_ss.b], writes=[lnv.b])
            P.op("act", f_act(rstd.ap[:, 0:n], lnv.ap[:, 0:n], AF.Exp, scale=-0.5), reads=[lnv.b], writes=[rstd.b])
            for ck in range(8):
                P.op("dve", f_stt(dst[:, ck, :], xsrc[:, ck, :], gains.ap[:, gidx, ck:ck + 1], rstd.ap[:, 0:n],
                                  ALU.mult, ALU.mult),
                     reads=xbufs + [rstd.b, gains.b], writes=dstbufs)

        def rope(src, dst, H, i, tmps):
            sv = src.ap.rearrange("p (h r x f) -> p h r x f", h=H, r=2, x=2)
            dv = dst.ap.rearrange("p (h r x f) -> p h r x f", h=H, r=2, x=2)
            x1, x2 = sv[:, :, :, 0, :], sv[:, :, :, 1, :]
            sn = tab.ap[:, 0, i, :, :].unsqueeze(1).broadcast_to([128, H, 2, 16])
            cs = tab.ap[:, 1, i, :, :].unsqueeze(1).broadcast_to([128, H, 2, 16])
            t1, t2, t3, t4 = [t.ap[:, 0:H * 32].rearrange("p (h r f) -> p h r f", h=H, r=2) for t in tmps]
            tb = [t.b for t in tmps]
            P.op("dve", f_tt(t1, x1, cs, ALU.mult), reads=[src.b, tab.b], writes=[tb[0]])
            P.op("dve", f_tt(t2, x2, sn, ALU.mult), reads=[src.b, tab.b], writes=[tb[1]])
            P.op("dve", f_tt(dv[:, :, :, 0, :], t1, t2, ALU.subtract), reads=[tb[0], tb[1]], writes=[dst.b])
            P.op("dve", f_tt(t3, x2, cs, ALU.mult), reads=[src.b, tab.b], writes=[tb[2]])
            P.op("dve", f_tt(t4, x1, sn, ALU.mult), reads=[src.b, tab.b], writes=[tb[3]])
            P.op("dve", f_tt(dv[:, :, :, 1, :], t3, t4, ALU.add), reads=[tb[2], tb[3]], writes=[dst.b])

        def headnorm(srcf, H, gi, sqt, sst, lt, rt):
            v = srcf.ap.rearrange("p (h d) -> p h d", h=H)
            sqv = sqt.ap[:, 0:H * 64].rearrange("p (h d) -> p h d", h=H)
            P.op("dve", f_tt(sqt.ap[:, 0:H * 64], srcf.ap, srcf.ap, ALU.mult), reads=[srcf.b], writes=[sqt.b])
            P.op("dve", lambda e: e.tensor_reduce(out=sst.ap[:, 0:H], in_=sqv, axis=AX.X, op=ALU.add),
                 reads=[sqt.b], writes=[sst.b])
            P.op("act", f_act(lt.ap[:, 0:H], sst.ap[:, 0:H], AF.Ln, scale=1.0 / 64.0, bias=EPS),
                 reads=[sst.b], writes=[lt.b])
            P.op("act", f_act(rt.ap[:, 0:H], lt.ap[:, 0:H], AF.Exp, scale=-0.5), reads=[lt.b], writes=[rt.b])
            P.op("dve", f_tt(v, v, rt.ap[:, 0:H].unsqueeze(2).broadcast_to([128, H, 64]), ALU.mult),
                 reads=[srcf.b, rt.b], writes=[srcf.b])
            P.op("dve", f_tt(v, v, qkg.ap[:, gi, :].unsqueeze(1).broadcast_to([128, H, 64]), ALU.mult),
                 reads=[srcf.b, qkg.b], writes=[srcf.b])

        def attention_phase():
            m_phase = top[0]
            QT = alloc(BF16, 8, T)
            KT = alloc(BF16, 2, TA)
            VA = alloc(BF16, 32, 4, 128)
            QT_b = [Buf() for _ in range(16)]
            KT_b = [Buf() for _ in range(32)]
            VA_b = [Buf() for _ in range(32)]
            m_a = top[0]
            wq = alloc(BF16, 8, 1024)
            wkv = alloc(BF16, 8, 512)
            xo = alloc(F32, 8, 256)
            hn = alloc(BF16, 8, 256)
            lnv = alloc(F32, 256)
            rstd = alloc(F32, 256)
            kf = alloc(F32, 256)
            qf = alloc(F32, 512)
            sqt = alloc(F32, 512)
            sst = alloc(F32, 8)
            lt = alloc(F32, 8)
            rt = alloc(F32, 8)
            rtm = [alloc(F32, 256) for _ in range(4)]
            Kr = alloc(BF16, 256)
            Qr = alloc(BF16, 1024)
            ps_ss, ps_kv, ps_q, ps_kt, ps_qt = psbank(0), psbank(1), psbank(2), psbank(3, BF16), psbank(4, BF16)

            P.dma("pool", f_dma(wq.ap, wq_d, True), writes=[wq.b])
            P.dma("pool", f_dma(wkv.ap, wkv_d, True), writes=[wkv.b])
            P.op("pool", lambda e: e.memset(VA.ap.rearrange("p a b c -> p (a b c)"), 1.0), writes=VA_b)

            for gi in range(16):
                own = gi < 8
                if own:
                    xsrc = hT.ap[:, :, gi * 256:(gi + 1) * 256]
                    xb = [hT_b[oc][gi // 2] for oc in range(8)]
                else:
                    P.dma("sp", f_dma(xo.ap, xT_d[:, :, gi * 256:(gi + 1) * 256]), writes=[xo.b])
                    xsrc, xb = xo.ap, [xo.b]
                rmsnorm_group(xsrc, xb, 0, hn.ap, [hn.b], 256, ps_ss, lnv, rstd)
                for j in range(2):
                    i = gi * 2 + j
                    js = slice(j * 128, (j + 1) * 128)
                    ts_ = slice(i * 128, (i + 1) * 128)
                    P.op("pe", f_mm(acc_group(ps_kv.ap, [(hn.ap[:, ck, js], wkv.ap[:, ck, :]) for ck in range(8)])),
                         reads=[hn.b, wkv.b], writes=[ps_kv.b])
                    P.op("act", f_copy_act(VA.ap[:, i, :, 0:64],
                                           ps_kv.ap[:, 256:512].rearrange("p (m d) -> p m d", m=4)),
                         reads=[ps_kv.b], writes=[VA_b[i]])
                    P.op("act", f_copy_act(kf.ap, ps_kv.ap[:, 0:256]), reads=[ps_kv.b], writes=[kf.b])
                    headnorm(kf, 4, 1, sqt, sst, lt, rt)
                    rope(kf, Kr, 4, i, rtm)
                    P.op("pe", f_tr([(ps_kt.ap[:, pi * 128:(pi + 1) * 128], Kr.ap[:, pi * 128:(pi + 1) * 128], ident.ap)
                                     for pi in range(2)]),
                         reads=[Kr.b, ident.b], writes=[ps_kt.b])
                    P.op("act", f_copy_act(KT.ap[:, :, ts_], ps_kt.ap[:, 0:256].rearrange("p (a b) -> p a b", a=2)),
                         reads=[ps_kt.b], writes=[KT_b[i]])
                    if own:
                        for half in range(2):
                            P.op("pe", f_mm(acc_group(ps_q.ap, [(hn.ap[:, ck, js], wq.ap[:, ck, half * 512:(half + 1) * 512])
                                                               for ck in range(8)])),
                                 reads=[hn.b, wq.b], writes=[ps_q.b])
                            P.op("act", f_copy_act(qf.ap, ps_q.ap), reads=[ps_q.b], writes=[qf.b])
                            headnorm(qf, 8, 0, sqt, sst, lt, rt)
                            qdst = Tile(Qr.ap[:, half * 512:(half + 1) * 512])
                            qdst.b = Qr.b
                            rope(qf, qdst, 8, i, rtm)
                        P.op("pe", f_tr([(ps_qt.ap[:, b * 128:(b + 1) * 128], Qr.ap[:, b * 128:(b + 1) * 128], ident.ap)
                                         for b in range(8)]),
                             reads=[Qr.b, ident.b], writes=[ps_qt.b])
                        P.op("act", f_copy_act(QT.ap[:, :, ts_], ps_qt.ap.rearrange("p (a b) -> p a b", a=8)),
                             reads=[ps_qt.b], writes=[QT_b[i]])
            P.barrier()
            top[0] = m_a
            wo = alloc(BF16, 8, 1024)
            oT = alloc(BF16, 8, 512)
            Pb = [[alloc(BF16, 512) for _ in range(2)] for _ in range(3)]
            tl = alloc(F32, 512)
            rr = alloc(F32, 512)
            nb = alloc(F32, 512)
            S = [[psbank(0), psbank(1)], [psbank(2), psbank(3)]]
            O = [[psbank(4), psbank(5)], [psbank(6), psbank(7)]]
            P.dma("pool", f_dma(wo.ap, wo_d, True), writes=[wo.b])
            it = 0
            for qt in range(4):
                qs = slice(qt * 512, (qt + 1) * 512)
                qb = QT_b[qt * 4:(qt + 1) * 4]
                for b in range(8):
                    pi = b // 4
                    Oa, Ob = O[b % 2]
                    for kt in range(32):
                        ks = slice(kt * 128, (kt + 1) * 128)
                        sa, sb = S[it % 2]
                        pa, pb = Pb[it % 3]
                        it += 1
                        P.op("pe", f_mm([(sa.ap, KT.ap[0:64, pi, ks], QT.ap[0:64, b, qs], True, True),
                                         (sb.ap, KT.ap[64:128, pi, ks], QT.ap[64:128, b, qs], True, True)]),
                             reads=[KT_b[kt]] + qb, writes=[sa.b, sb.b])
                        P.op("act", f_act(pa.ap, sa.ap, AF.Exp, scale=0.125), reads=[sa.b], writes=[pa.b])
                        P.op("act", f_act(pb.ap, sb.ap, AF.Exp, scale=0.125), reads=[sb.b], writes=[pb.b])
                        P.op("pe", f_mm([(Oa.ap, VA.ap[:, kt, 2 * pi, :], pa.ap, kt == 0, kt == 31),
                                         (Ob.ap, VA.ap[:, kt, 2 * pi + 1, :], pb.ap, kt == 0, kt == 31)]),
                             reads=[VA_b[kt], pa.b, pb.b], writes=[Oa.b, Ob.b])
                    P.op("act", f_act(tl.ap[0:64, :], Oa.ap[64:128, :], AF.Ln), reads=[Oa.b], writes=[tl.b])
                    P.op("act", f_act(rr.ap[0:64, :], tl.ap[0:64, :], AF.Exp, scale=-1.0), reads=[tl.b], writes=[rr.b])
                    P.op("dve", f_tt(oT.ap[0:64, b, :], Oa.ap[0:64, :], rr.ap[0:64, :], ALU.mult),
                         reads=[Oa.b, rr.b], writes=[oT.b])
                    P.op("act", f_act(tl.ap[64:128, :], Ob.ap[64:128, :], AF.Ln), reads=[Ob.b], writes=[tl.b])
                    P.op("act", f_act(rr.ap[64:128, :], tl.ap[64:128, :], AF.Exp, scale=-1.0), reads=[tl.b], writes=[rr.b])
                    P.op("act", f_copy_act(nb.ap[64:128, :], Ob.ap[0:64, :]), reads=[Ob.b], writes=[nb.b])
                    P.op("dve", f_tt(oT.ap[64:128, b, :], nb.ap[64:128, :], rr.ap[64:128, :], ALU.mult),
                         reads=[nb.b, rr.b], writes=[oT.b])
                for oc in range(8):
                    ps = S[oc % 2][0]
                    P.op("pe", f_mm(acc_group(ps.ap, [(wo.ap[:, b, oc * 128:(oc + 1) * 128], oT.ap[:, b, :])
                                                      for b in range(8)])),
                         reads=[wo.b, oT.b], writes=[ps.b])
                    P.op("dve", f_tt(hT.ap[:, oc, qs], hT.ap[:, oc, qs], ps.ap, ALU.add),
                         reads=[ps.b, hT_b[oc][qt]], writes=[hT_b[oc][qt]])
            P.barrier()
            top[0] = m_phase

        def f_copy_act(out, in_):
            return lambda e: e.activation(out=out, in_=in_, func=AF.Copy)

        def mlp_phase(layer, gidx):
            m_phase = top[0]
            hn = alloc(BF16, 8, T)
            hn_b = [Buf() for _ in range(4)]
            lnv = alloc(F32, 512)
            rstd = alloc(F32, 512)
            h1 = [alloc(BF16, 4, T) for _ in range(2)]
            h1_b = [[Buf() for _ in range(4)] for _ in range(2)]
            win = [alloc(BF16, 8, 512) for _ in range(2)]
            wout = [alloc(BF16, 4, 1024) for _ in range(2)]
            rl = [alloc(F32, 512) for _ in range(2)]
            psI = [psbank(b) for b in range(4)]
            psO = [psbank(b) for b in range(4, 8)]
            for tt in range(4):
                tsl = slice(tt * 512, (tt + 1) * 512)
                rmsnorm_group(hT.ap[:, :, tsl], [hT_b[oc][tt] for oc in range(8)], gidx,
                              hn.ap[:, :, tsl], [hn_b[tt]], 512, psI[tt % 4], lnv, rstd)
            cnt = [0, 0]

            def load(G):
                P.dma("pool", f_dma(win[G % 2].ap, win_d[layer, G], True), writes=[win[G % 2].b])
                P.dma("pool", f_dma(wout[G % 2].ap, wout_d[layer, G], True), writes=[wout[G % 2].b])

            def stage_in(G):
                w = win[G % 2]
                for tt in range(4):
                    tsl = slice(tt * 512, (tt + 1) * 512)
                    for fb in range(4):
                        ps = psI[cnt[0] % 4]
                        r_ = rl[cnt[0] % 2]
                        cnt[0] += 1
                        P.op("pe", f_mm(acc_group(ps.ap, [(w.ap[:, ck, fb * 128:(fb + 1) * 128], hn.ap[:, ck, tsl])
                                                          for ck in range(8)])),
                             reads=[w.b, hn_b[tt]], writes=[ps.b])
                        P.op("act", f_act(r_.ap, ps.ap, AF.Relu), reads=[ps.b], writes=[r_.b])
                        P.op("act", f_act(h1[G % 2].ap[:, fb, tsl], r_.ap, AF.Square), reads=[r_.b],
                             writes=[h1_b[G % 2][tt]])

            def stage_out(G):
                w = wout[G % 2]
                for oc in range(8):
                    for tt in range(4):
                        tsl = slice(tt * 512, (tt + 1) * 512)
                        ps = psO[cnt[1] % 4]
                        cnt[1] += 1
                        P.op("pe", f_mm(acc_group(ps.ap, [(w.ap[:, fb, oc * 128:(oc + 1) * 128], h1[G % 2].ap[:, fb, tsl])
                                                          for fb in range(4)])),
                             reads=[w.b, h1_b[G % 2][tt]], writes=[ps.b])
                        P.op("dve", f_tt(hT.ap[:, oc, tsl], hT.ap[:, oc, tsl], ps.ap, ALU.add),
                             reads=[ps.b, hT_b[oc][tt]], writes=[hT_b[oc][tt]])

            load(0)
            load(1)
            stage_in(0)
            for G in range(8):
                if G + 1 < 8:
                    stage_in(G + 1)
                stage_out(G)
                if G + 2 < 8:
                    load(G + 2)
            P.barrier()
            top[0] = m_phase

        def gla_phase():
            m_phase = top[0]
            hn = alloc(BF16, 8, T)
            hn_b = [Buf() for _ in range(4)]
            zT = alloc(F32, T)
            wup = alloc(F32, 2, 512)
            wz = alloc(BF16, 8, 32)
            wg = alloc(BF16, 8, 1536)
            wo2 = alloc(BF16, 4, 1024)
            oX = alloc(BF16, 4, T)
            oX_b = [Buf() for _ in range(16)]
            Sf = alloc(F32, 2, 256)
            Sb = alloc(BF16, 2, 256)
            rx = alloc(F32, 2, 512)
            NP_ = 2
            e1 = [alloc(F32, 256) for _ in range(NP_)]
            lg = [alloc(F32, 256) for _ in range(NP_)]
            E1 = [alloc(F32, 2, 128) for _ in range(NP_)]
            E2 = [alloc(F32, 2, 128) for _ in range(NP_)]
            E3 = [alloc(F32, 256) for _ in range(NP_)]
            qeT = [alloc(BF16, 2, 128) for _ in range(NP_)]
            keT = [alloc(BF16, 2, 128) for _ in range(NP_)]
            kd = [alloc(BF16, 256) for _ in range(NP_)]
            vt = [alloc(BF16, 512) for _ in range(NP_)]
            aTm = [alloc(BF16, 2, 128) for _ in range(NP_)]
            ot = alloc(F32, 4, 128)
            osq = alloc(BF16, 4, 128)
            lnv = alloc(F32, 512)
            rstd = alloc(F32, 512)
            sg = alloc(F32, 512)
            on = alloc(F32, 4, 128)
            ps_qk, ps_kl, ps_v, ps_cr, ps_a, ps_o, ps_ckv, ps_r = [psbank(b) for b in range(8)]

            P.dma("pool", f_dma(wz.ap, wz_d, True), writes=[wz.b])
            P.dma("sp", f_dma(wup.ap[0:33], wup_d), writes=[wup.b])
            for tt in range(4):
                tsl = slice(tt * 512, (tt + 1) * 512)
                rmsnorm_group(hT.ap[:, :, tsl], [hT_b[oc][tt] for oc in range(8)], 2,
                              hn.ap[:, :, tsl], [hn_b[tt]], 512, ps_qk, lnv, rstd)
            P.op("pool", lambda e: e.memset(zT.ap[32:33, :], 1.0), writes=[zT.b])
            for tt in range(4):
                tsl = slice(tt * 512, (tt + 1) * 512)
                P.op("pe", f_mm(acc_group(ps_v.ap[0:32, :], [(wz.ap[:, ck, :], hn.ap[:, ck, tsl]) for ck in range(8)])),
                     reads=[wz.b, hn_b[tt]], writes=[ps_v.b])
                P.op("act", f_copy_act(zT.ap[0:32, tsl], ps_v.ap[0:32, :]), reads=[ps_v.b], writes=[zT.b])

            QS = float(128.0 ** -0.5)

            def one_pass(g, d, order, final):
                MI = UT if d == 0 else LT
                MS = SL if d == 0 else SU
                lastc = 127 if d == 0 else 0
                for n_, i in enumerate(order):
                    p_ = n_ % NP_
                    ts_ = slice(i * 128, (i + 1) * 128)
                    hb = hn_b[i // 4]
                    grp = []
                    for blk in range(4):
                        grp += acc_group(ps_qk.ap[:, blk * 128:(blk + 1) * 128],
                                         [(wg.ap[:, ck, blk * 128:(blk + 1) * 128], hn.ap[:, ck, ts_]) for ck in range(8)])
                    P.op("pe", f_mm(grp), reads=[wg.b, hb], writes=[ps_qk.b])
                    P.op("pe", f_mm(acc_group(ps_kl.ap[:, 0:256], [(hn.ap[:, ck, ts_], wg.ap[:, ck, 256:512])
                                                                   for ck in range(8)])
                                    + [(ps_kl.ap[:, 256:512], zT.ap[0:33, ts_], wup.ap[0:33, d, g * 256:(g + 1) * 256],
                                        True, True)]),
                         reads=[wg.b, hb, zT.b, wup.b], writes=[ps_kl.b])
                    P.op("pe", f_mm(acc_group(ps_v.ap, [(hn.ap[:, ck, ts_], wg.ap[:, ck, 512:1024]) for ck in range(8)])),
                         reads=[wg.b, hb], writes=[ps_v.b])
                    P.op("act", f_act(e1[p_].ap, ps_kl.ap[:, 256:512], AF.Exp, scale=-1.0), reads=[ps_kl.b], writes=[e1[p_].b])
                    P.op("act", f_act(lg[p_].ap, e1[p_].ap, AF.Ln, bias=1.0), reads=[e1[p_].b], writes=[lg[p_].b])
                    P.op("act", f_copy_act(vt[p_].ap, ps_v.ap), reads=[ps_v.b], writes=[vt[p_].b])
                    P.op("pe", f_mm([(ps_cr.ap[:, h * 128:(h + 1) * 128], lg[p_].ap[:, h * 128:(h + 1) * 128], MI.ap, True, True)
                                     for h in range(2)]
                                    + [(ps_cr.ap[:, 256:512], MS.ap, lg[p_].ap, True, True)]),
                         reads=[lg[p_].b, MI.b, MS.b], writes=[ps_cr.b])
                    csv = ps_cr.ap[:, 0:256].rearrange("p (h c) -> p h c", h=2)
                    P.op("act", f_act(E1[p_].ap, csv, AF.Exp, scale=-1.0 / 16.0), reads=[ps_cr.b], writes=[E1[p_].b])
                    P.op("act", f_act(E2[p_].ap, csv, AF.Exp, scale=1.0 / 16.0), reads=[ps_cr.b], writes=[E2[p_].b])
                    P.op("act", f_act(E3[p_].ap, ps_cr.ap[:, 256:512], AF.Exp, scale=-1.0 / 16.0),
                         reads=[ps_cr.b], writes=[E3[p_].b])
                    qkv = ps_qk.ap.rearrange("p (b c) -> p b c", b=4)
                    P.op("dve", f_stt(qeT[p_].ap, qkv[:, 0:2, :], QS, E1[p_].ap, ALU.mult, ALU.mult),
                         reads=[ps_qk.b, E1[p_].b], writes=[qeT[p_].b])
                    P.op("dve", f_tt(keT[p_].ap, qkv[:, 2:4, :], E2[p_].ap, ALU.mult),
                         reads=[ps_qk.b, E2[p_].b], writes=[keT[p_].b])
                    P.op("dve", f_tt(kd[p_].ap, ps_kl.ap[:, 0:256], E3[p_].ap, ALU.mult),
                         reads=[ps_kl.b, E3[p_].b], writes=[kd[p_].b])
                    P.op("pe", f_mm([(ps_a.ap[:, h * 128:(h + 1) * 128], keT[p_].ap[:, h, :], qeT[p_].ap[:, h, :], True, True)
                                     for h in range(2)]),
                         reads=[keT[p_].b, qeT[p_].b], writes=[ps_a.b])
                    P.op("dve", f_tt(aTm[p_].ap, ps_a.ap[:, 0:256].rearrange("p (h c) -> p h c", h=2),
                                     MI.ap.unsqueeze(1).broadcast_to([128, 2, 128]), ALU.mult),
                         reads=[ps_a.b, MI.b], writes=[aTm[p_].b])
                    grp = []
                    for h in range(2):
                        for vb in range(2):
                            blk = h * 2 + vb
                            o_ = ps_o.ap[:, blk * 128:(blk + 1) * 128]
                            grp.append((o_, vt[p_].ap[:, h * 256 + vb * 128:h * 256 + (vb + 1) * 128], aTm[p_].ap[:, h, :],
                                        True, False))
                            grp.append((o_, Sb.ap[:, h, vb * 128:(vb + 1) * 128], qeT[p_].ap[:, h, :], False, True))
                    P.op("pe", f_mm(grp), reads=[vt[p_].b, aTm[p_].b, Sb.b, qeT[p_].b], writes=[ps_o.b])
                    P.op("pe", f_mm([(ps_ckv.ap[:, h * 256:(h + 1) * 256], kd[p_].ap[:, h * 128:(h + 1) * 128],
                                      vt[p_].ap[:, h * 256:(h + 1) * 256], True, True) for h in range(2)]),
                         reads=[kd[p_].b, vt[p_].b], writes=[ps_ckv.b])
                    for h in range(2):
                        P.op("dve", f_stt(Sf.ap[:, h, :], Sf.ap[:, h, :], E1[p_].ap[:, h, lastc:lastc + 1],
                                          ps_ckv.ap[:, h * 256:(h + 1) * 256], ALU.mult, ALU.add),
                             reads=[Sf.b, E1[p_].b, ps_ckv.b], writes=[Sf.b])
                    P.op("act", f_copy_act(Sb.ap, Sf.ap), reads=[Sf.b], writes=[Sb.b])
                    ov = ps_o.ap.rearrange("p (b c) -> p b c", b=4)
                    if not final:
                        P.op("act", f_copy_act(oX.ap[:, :, ts_], ov), reads=[ps_o.b], writes=[oX_b[i]])
                        continue
                    P.op("dve", f_tt(ot.ap, ov, oX.ap[:, :, ts_], ALU.add), reads=[ps_o.b, oX_b[i]], writes=[ot.b])
                    P.op("act", f_act(osq.ap, ot.ap, AF.Square), reads=[ot.b], writes=[osq.b])
                    grp = []
                    for h in range(2):
                        grp += acc_group(ps_a.ap[:, 256 + h * 128:256 + (h + 1) * 128],
                                         [(onesb.ap, osq.ap[:, h * 2 + vb, :]) for vb in range(2)])
                    P.op("pe", f_mm(grp), reads=[osq.b, onesb.b], writes=[ps_a.b])
                    P.op("act", f_act(lnv.ap[:, 0:256], ps_a.ap[:, 256:512], AF.Ln, scale=1.0 / 256.0, bias=EPS),
                         reads=[ps_a.b], writes=[lnv.b])
                    P.op("act", f_act(rstd.ap[:, 0:256], lnv.ap[:, 0:256], AF.Exp, scale=-0.5), reads=[lnv.b], writes=[rstd.b])
                    grp = []
                    for blk in range(4):
                        grp += acc_group(ps_r.ap[:, blk * 128:(blk + 1) * 128],
                                         [(wg.ap[:, ck, 1024 + blk * 128:1024 + (blk + 1) * 128], hn.ap[:, ck, ts_])
                                          for ck in range(8)])
                    P.op("pe", f_mm(grp), reads=[wg.b, hb], writes=[ps_r.b])
                    P.op("act", f_act(sg.ap, ps_r.ap, AF.Exp, scale=-1.0), reads=[ps_r.b], writes=[sg.b])
                    P.op("act", f_act(sg.ap, sg.ap, AF.Ln, bias=1.0), reads=[sg.b], writes=[sg.b])
                    P.op("act", f_act(sg.ap, sg.ap, AF.Exp, scale=-1.0), reads=[sg.b], writes=[sg.b])
                    for blk in range(4):
                        h, vb = blk // 2, blk % 2
                        P.op("dve", f_stt(on.ap[:, blk, :], ot.ap[:, blk, :], og.ap[:, vb:vb + 1],
                                          rstd.ap[:, h * 128:(h + 1) * 128], ALU.mult, ALU.mult),
                             reads=[ot.b, og.b, rstd.b], writes=[on.b])
                    P.op("dve", f_tt(sg.ap, ps_r.ap, sg.ap, ALU.mult), reads=[ps_r.b, sg.b], writes=[sg.b])
                    P.op("dve", f_tt(oX.ap[:, :, ts_], on.ap, sg.ap.rearrange("p (b c) -> p b c", b=4), ALU.mult),
                         reads=[on.b, sg.b], writes=[oX_b[i]])

            for g in range(2):
                P.dma("pool", f_dma(wg.ap, wg_d[g], True), writes=[wg.b])
                P.dma("pool", f_dma(wo2.ap, wo2_d[g], True), writes=[wo2.b])
                P.op("dve", lambda e: e.memset(Sf.ap.rearrange("p a b -> p (a b)"), 0.0), writes=[Sf.b])
                P.op("pool", lambda e: e.memset(Sb.ap.rearrange("p a b -> p (a b)"), 0.0), writes=[Sb.b])
                one_pass(g, 0, list(range(16)), False)
                ibb, obb = Buf(), Buf()
                sfl = Sf.ap.rearrange("p a b -> p (a b)")
                P.dma("pool", f_dma(ib_t[g].ap(), sfl), reads=[Sf.b], writes=[ibb])
                P.custom("pool", (lambda g=g: (lambda e: e.collective_compute(
                    "AllGather", ALU.bypass, replica_groups=[[0, 1], [2, 3], [4, 5], [6, 7]],
                    ins=[ib_t[g].ap().opt()], outs=[ob_t[g].ap().opt()])))(), "cc%d" % g, reads=[ibb], writes=[obb])
                P.dma("pool", f_dma(rx.ap, ob_t[g].ap().rearrange("(r p) c -> p r c", p=128)), reads=[obb], writes=[rx.b])
                P.op("dve", f_ts(sfl, rx.ap[:, 0, :], mex.ap[:, 0:1], ALU.mult), reads=[rx.b, mex.b], writes=[Sf.b])
                P.op("dve", f_stt(sfl, rx.ap[:, 1, :], mex.ap[:, 1:2], sfl, ALU.mult, ALU.add),
                     reads=[rx.b, mex.b, Sf.b], writes=[Sf.b])
                P.op("act", f_copy_act(Sb.ap, Sf.ap), reads=[Sf.b], writes=[Sb.b])
                one_pass(g, 1, list(range(15, -1, -1)), True)
                k = 0
                for oc in range(8):
                    for tt in range(4):
                        tsl = slice(tt * 512, (tt + 1) * 512)
                        ps = [ps_qk, ps_kl, ps_v, ps_cr][k % 4]
                        k += 1
                        P.op("pe", f_mm(acc_group(ps.ap, [(wo2.ap[:, blk, oc * 128:(oc + 1) * 128], oX.ap[:, blk, tsl])
                                                          for blk in range(4)])),
                             reads=[wo2.b] + oX_b[tt * 4:(tt + 1) * 4], writes=[ps.b])
                        P.op("dve", f_tt(hT.ap[:, oc, tsl], hT.ap[:, oc, tsl], ps.ap, ALU.add),
                             reads=[ps.b, hT_b[oc][tt]], writes=[hT_b[oc][tt]])
                P.barrier()
            top[0] = m_phase

        def output_phase(do_norm):
            m_phase = top[0]
            lnv = alloc(F32, 512)
            rstd = alloc(F32, 512)
            sq = alloc(BF16, 8, 512)
            yo = [alloc(F32, 8, 512) for _ in range(2)]
            toks = []
            for tt in range(4):
                tsl = slice(tt * 512, (tt + 1) * 512)
                xb = [hT_b[oc][tt] for oc in range(8)]
                if do_norm:
                    y = yo[tt % 2]
                    import os
                    mode = os.environ.get("KDBG_OUT", "")
                    if mode == "B":
                        P.op("dve", f_copy(y.ap, hT.ap[:, :, tsl]), reads=xb, writes=[y.b])
                    else:
                        rmsnorm_group(hT.ap[:, :, tsl], xb, 4, y.ap, [y.b], 512, psbank(tt % 4), lnv, rstd, sq=sq.ap, sqbufs=[sq.b])
                    if mode == "A":
                        toks.append(P.dma("sp", f_dma(y_d[:, :, tsl], hT.ap[:, :, tsl]), reads=xb + [y.b]))
                    else:
                        toks.append(P.dma("sp", f_dma(y_d[:, :, tsl], y.ap), reads=[y.b]))
                else:
                    toks.append(P.dma("sp", f_dma(y_d[:, :, tsl], hT.ap[:, :, tsl]), reads=xb))
            top[0] = m_phase

        phases = [("attn", attention_phase), ("mlp1", lambda: mlp_phase(0, 1)), ("gla", gla_phase),
                  ("mlp2", lambda: mlp_phase(1, 3))]
        done = False
        if only is not None:
            for name, fn in phases:
                if name in only:
                    fn()
            output_phase("final" in only)
            done = True
        else:
            for name, fn in phases:
                fn()
                if stop_after == name:
                    output_phase(False)
                    done = True
                    break
        if not done:
            output_phase(True)
        P.barrier()

        import contextlib
        with contextlib.ExitStack() as es:
            sems = {}
            for k in P.count.keys():
                sems[k] = es.enter_context(nc.semaphore("s_" + k))
            block = es.enter_context(nc.Block())

            def replay(name, e):
                for item in P.lists[name]:
                    if item[0] == "w":
                        e.wait_ge(sems[item[1]], item[2])
                    else:
                        inst = item[1](e)
                        if item[0] == "c":
                            inst.then_inc(sems[item[2]])
                        else:
                            inst.then_inc(sems[item[2]], item[3])

            @block.tensor
            def _(e):
                replay("pe", e)

            @block.scalar
            def _(e):
                replay("act", e)

            @block.vector
            def _(e):
                replay("dve", e)

            @block.gpsimd
            def _(e):
                replay("pool", e)

            @block.sync
            def _(e):
                replay("sp", e)
    return nc


def _fm(w):
    K, N = w.shape
    return np.ascontiguousarray(w.reshape(K // 128, 128, N).transpose(1, 0, 2))


def _prep(inputs):
    f = lambda a: np.asarray(a, dtype=np.float32)
    x = f(inputs["x"])
    norm_mix, norm_mlp, final_norm = f(inputs["norm_mix"]), f(inputs["norm_mlp"]), f(inputs["final_norm"])
    wqkv = f(inputs["attn_w_qkv"])[0]
    qn, kn = f(inputs["attn_q_norm"])[0], f(inputs["attn_k_norm"])[0]
    wo = f(inputs["attn_w_o"])[0]
    gw = f(inputs["gla_w_in"])[0]
    gup, gb = f(inputs["gla_w_gate_up"])[0], f(inputs["gla_b_gate"])[0]
    gon, gwo = f(inputs["gla_out_norm"])[0], f(inputs["gla_w_o"])[0]
    mwi, mwo = f(inputs["mlp_w_in"]), f(inputs["mlp_w_out"])

    gains = np.stack([norm_mix[0], norm_mlp[0], norm_mix[1], norm_mlp[1], final_norm], 0)
    gains_l = np.ascontiguousarray(gains.reshape(5, 8, 128).transpose(2, 0, 1)).reshape(128, 40)
    qkg_l = np.ascontiguousarray(np.broadcast_to(np.concatenate([qn, kn])[None, :], (128, 128)))
    qcols = np.concatenate([np.arange(h * 64, (h + 1) * 64) for h in HO])
    wq_l = _fm(wqkv[:, :1024][:, qcols])
    wkv_l = _fm(wqkv[:, 1024:1536])
    wo_l = _fm(wo[qcols, :])
    win_l = np.stack([np.stack([_fm(mwi[l][:, G * 512:(G + 1) * 512]) for G in range(8)]) for l in range(2)])
    wout_l = np.stack([np.stack([_fm(mwo[l][G * 512:(G + 1) * 512, :]) for G in range(8)]) for l in range(2)])
    wg_l = []
    for g in range(2):
        cols = np.concatenate([np.arange(g * 256, (g + 1) * 256), 512 + np.arange(g * 256, (g + 1) * 256),
                               1024 + np.arange(g * 512, (g + 1) * 512), 2048 + np.arange(g * 512, (g + 1) * 512)])
        wg_l.append(_fm(gw[:, cols]))
    wg_l = np.stack(wg_l)
    og_l = np.ascontiguousarray(gon.reshape(2, 128).T)
    wo2_l = np.stack([_fm(gwo[g * 512:(g + 1) * 512, :]) for g in range(2)])

    maps = []
    idxs = []
    for c in range(NCORES):
        b, s = c // 2, c % 2
        if s == 0:
            own = np.arange(0, T)
            other = np.arange(T, TA)
            dirs = (0, 1)
        else:
            own = np.arange(TA - 1, T - 1, -1)
            other = np.arange(0, T)
            dirs = (1, 0)
        idx = np.concatenate([own, other])
        idxs.append(own)
        xT_l = np.ascontiguousarray(x[b][idx].T.reshape(8, 128, TA).transpose(1, 0, 2))
        pr = (idx // 64).astype(np.float32)
        pc = (idx % 64).astype(np.float32)
        pos_l = np.ascontiguousarray(np.stack([pr.reshape(32, 128).T, pc.reshape(32, 128).T], -1)).reshape(128, 64)
        zc = np.concatenate([3072 + dirs[0] * 16 + np.arange(16), 3072 + dirs[1] * 16 + np.arange(16)])
        wz_l = _fm(gw[:, zc])
        wup_l = np.zeros((33, 2, 512), np.float32)
        wup_l[0:16, 0] = gup[dirs[0]]
        wup_l[16:32, 1] = gup[dirs[1]]
        wup_l[32, 0] = gb[dirs[0]]
        wup_l[32, 1] = gb[dirs[1]]
        mex_l = np.zeros((128, 2), np.float32)
        mex_l[:, 1 - s] = 1.0
        maps.append({
            "xT": xT_l, "pos": pos_l, "gains": gains_l, "qkg": qkg_l, "wq": wq_l, "wkv": wkv_l, "wo": wo_l,
            "win": win_l, "wout": wout_l, "wg": wg_l, "wz": wz_l, "wup": wup_l, "og": og_l, "wo2": wo2_l,
            "mex": mex_l,
        })
    return maps, idxs


_NC_CACHE = {}


def run(inputs, stop_after=None, trace=False, only=None):
    maps, idxs = _prep(inputs)
    key = (stop_after, only)
    if key not in _NC_CACHE:
        _NC_CACHE[key] = build_program(stop_after, only)
    nc = _NC_CACHE[key]
    res = run_bass_kernel_spmd(nc, maps, core_ids=list(range(NCORES)), trace=trace)
    out = np.empty((4, TA, D), np.float32)
    for c in range(NCORES):
        y = np.asarray(res.results[c]["y"])
        out[c // 2, idxs[c], :] = y.transpose(2, 1, 0).reshape(T, D)
    return out, res


def kernel(**inputs):
    h2, _ = run(inputs, stop_after="mlp1")
    inputs2 = dict(inputs)
    inputs2["x"] = h2
    out, _ = run(inputs2, only=("gla", "mlp2", "final"))
    return out
```

```python
import math
from functools import reduce

import numpy as np
import concourse.bass as bass
import concourse.mybir as mybir
from concourse.bass_utils import run_bass_kernel_spmd

F32 = mybir.dt.float32
BF16 = mybir.dt.bfloat16
I32 = mybir.dt.int32
AF = mybir.ActivationFunctionType
ALU = mybir.AluOpType
AX = mybir.AxisListType

NCORES = 8
D = 1024
T = 2048
TA = 4096
EPS = 1e-6
HO = [0, 4, 1, 5, 2, 6, 3, 7, 8, 12, 9, 13, 10, 14, 11, 15]
ENGS = ("pe", "act", "dve", "pool", "sp")
NDSEM = 12
SBUF_BYTES = 212800


class Buf:
    __slots__ = ("w", "r")

    def __init__(self):
        self.w = None
        self.r = {}


class Tile:
    def __init__(self, ap):
        self.ap = ap
        self.b = Buf()


class Prog:
    def __init__(self):
        self.lists = {e: [] for e in ENGS}
        self.count = {}
        self.waited = {e: {} for e in ENGS}
        self.dma_i = {"sp": 0, "pool": 0}

    def _deps(self, eng, reads, writes):
        toks = {}

        def add(t):
            if t is None:
                return
            k, v = t
            if eng == "pe" and k == "pe":
                return
            if toks.get(k, 0) < v:
                toks[k] = v

        for b in reads:
            add(b.w)
        for b in writes:
            add(b.w)
            for k, v in b.r.items():
                add((k, v))
        for k, v in toks.items():
            if self.waited[eng].get(k, 0) < v:
                self.lists[eng].append(("w", k, v))
                self.waited[eng][k] = v

    def _mark(self, tok, reads, writes):
        k, v = tok
        for b in reads:
            if b.r.get(k, 0) < v:
                b.r[k] = v
        for b in writes:
            b.w = tok
            b.r = {}

    def op(self, eng, fn, reads=(), writes=()):
        self._deps(eng, reads, writes)
        self.count[eng] = self.count.get(eng, 0) + 1
        tok = (eng, self.count[eng])
        self.lists[eng].append(("o", fn, eng, 1))
        self._mark(tok, reads, writes)
        return tok

    def dma(self, q, fn, reads=(), writes=()):
        self._deps(q, reads, writes)
        i = self.dma_i[q]
        self.dma_i[q] = i + 1
        key = "d%s%d" % (q, i % NDSEM)
        self.count[key] = self.count.get(key, 0) + 16
        tok = (key, self.count[key])
        self.lists[q].append(("o", fn, key, 16))
        self._mark(tok, reads, writes)
        return tok

    def custom(self, eng, fn, key, reads=(), writes=()):
        self._deps(eng, reads, writes)
        self.count[key] = self.count.get(key, 0) + 1
        tok = (key, self.count[key])
        self.lists[eng].append(("c", fn, key, 1))
        self._mark(tok, reads, writes)
        return tok

    def barrier(self):
        for e in ENGS:
            for k, v in self.count.items():
                if k == e and e != "pe":
                    pass
                if self.waited[e].get(k, 0) < v:
                    self.lists[e].append(("w", k, v))
                    self.waited[e][k] = v


def _prod(s):
    return reduce(lambda a, b: a * b, s, 1)


def build_program(stop_after=None, only=None):
    nc = bass.Bass("TRN2", target_bir_lowering=False)

    def din(name, shape):
        return nc.dram_tensor(name, list(shape), F32, kind="ExternalInput").ap()

    xT_d = din("xT", [128, 8, TA])
    pos_d = din("pos", [128, 64])
    gains_d = din("gains", [128, 40])
    qkg_d = din("qkg", [128, 128])
    wq_d = din("wq", [128, 8, 1024])
    wkv_d = din("wkv", [128, 8, 512])
    wo_d = din("wo", [128, 8, 1024])
    win_d = din("win", [2, 8, 128, 8, 512])
    wout_d = din("wout", [2, 8, 128, 4, 1024])
    wg_d = din("wg", [2, 128, 8, 1536])
    wz_d = din("wz", [128, 8, 32])
    wup_d = din("wup", [33, 2, 512])
    og_d = din("og", [128, 2])
    wo2_d = din("wo2", [2, 128, 4, 1024])
    mex_d = din("mex", [128, 2])
    y_d = nc.dram_tensor("y", [128, 8, T], F32, kind="ExternalOutput").ap()
    ib_t = [nc.dram_tensor("ib%d" % g, [128, 512], F32) for g in range(2)]
    ob_t = [nc.dram_tensor("ob%d" % g, [256, 512], F32) for g in range(2)]

    P = Prog()

    with (
        nc.sbuf_tensor("arena", [128, SBUF_BYTES // 4], F32) as A,
        nc.psum_tensor("psum", [128, 4096], F32) as PS,
    ):
        top = [0]

        def alloc(dtype, *shape):
            esz = 4 if dtype in (F32, I32) else 2
            nbytes = (_prod(shape) * esz + 63) // 64 * 64
            off = top[0]
            top[0] += nbytes
            assert top[0] <= SBUF_BYTES, ("SBUF overflow", top[0])
            ap = A[:, off // 4:(off + nbytes) // 4]
            if dtype != F32:
                ap = ap.bitcast(dtype)
            ap = ap[:, 0:_prod(shape)]
            if len(shape) > 1:
                names = ["a%d" % i for i in range(len(shape))]
                ap = ap.rearrange("p (%s) -> p %s" % (" ".join(names), " ".join(names)),
                                  **{n: s for n, s in zip(names[:-1], shape[:-1])})
            return Tile(ap)

        def psbank(b, dtype=F32):
            ap = PS[:, b * 512:(b + 1) * 512]
            if dtype != F32:
                ap = ap.bitcast(dtype)
            return Tile(ap)

        def f_act(out, in_, func, scale=1.0, bias=None):
            if bias is None:
                return lambda e: e.activation(out=out, in_=in_, func=func, scale=scale)
            return lambda e: e.activation(out=out, in_=in_, func=func, scale=scale, bias=bias)

        def f_tt(out, in0, in1, op):
            return lambda e: e.tensor_tensor(out=out, in0=in0, in1=in1, op=op)

        def f_stt(out, in0, scalar, in1, op0, op1):
            return lambda e: e.scalar_tensor_tensor(out=out, in0=in0, scalar=scalar, in1=in1, op0=op0, op1=op1)

        def f_ts(out, in0, s1, op0, s2=None, op1=None):
            if op1 is None:
                return lambda e: e.tensor_scalar(out=out, in0=in0, scalar1=s1, scalar2=None, op0=op0)
            return lambda e: e.tensor_scalar(out=out, in0=in0, scalar1=s1, scalar2=s2, op0=op0, op1=op1)

        def f_copy(out, in_):
            return lambda e: e.tensor_copy(out=out, in_=in_)

        def f_mm(group):
            def fn(e):
                inst = None
                for (o, l, r, st, sp) in group:
                    inst = e.matmul(o, l, r, start=st, stop=sp)
                return inst
            return fn

        def f_tr(group):
            def fn(e):
                inst = None
                for (o, i, idn) in group:
                    inst = e.transpose(o, i, idn)
                return inst
            return fn

        def f_dma(out, in_, cast=False):
            if cast:
                return lambda e: e.dma_start(out=out, in_=in_, max_dma_last_dim=4096)
            return lambda e: e.dma_start(out=out, in_=in_)

        def acc_group(out, pairs):
            n = len(pairs)
            return [(out, l, r, i == 0, i == n - 1) for i, (l, r) in enumerate(pairs)]

        hT = alloc(F32, 8, T)
        hT_b = [[Buf() for _ in range(4)] for _ in range(8)]
        hT_all = [b for row in hT_b for b in row]
        onesf = alloc(F32, 128)
        UT = alloc(F32, 128)
        LT = alloc(F32, 128)
        SL = alloc(F32, 128)
        SU = alloc(F32, 128)
        IDF = alloc(F32, 128)
        ident = alloc(BF16, 128)
        onesb = alloc(BF16, 128)
        gains = alloc(F32, 5, 8)
        qkg = alloc(F32, 2, 64)
        og = alloc(F32, 2)
        mex = alloc(F32, 2)
        pos = alloc(F32, 64)
        tab = alloc(F32, 2, 32, 2, 16)
        persist_top = top[0]

        P.dma("sp", f_dma(hT.ap, xT_d[:, :, 0:T]), writes=hT_all)
        P.dma("sp", f_dma(gains.ap, gains_d.rearrange("p (a b) -> p a b", a=5)), writes=[gains.b])
        P.dma("sp", f_dma(qkg.ap, qkg_d.rearrange("p (a b) -> p a b", a=2)), writes=[qkg.b])
        P.dma("sp", f_dma(og.ap, og_d), writes=[og.b])
        P.dma("sp", f_dma(mex.ap, mex_d), writes=[mex.b])
        P.dma("sp", f_dma(pos.ap, pos_d), writes=[pos.b])

        P.op("pool", lambda e: e.memset(onesf.ap, 1.0), writes=[onesf.b])
        P.op("pool", lambda e: e.memset(onesb.ap, 1.0), writes=[onesb.b])

        def mk_mask(dst, cm, step, cmp):
            P.op("pool", lambda e: e.affine_select(out=dst.ap, in_=onesf.ap, pattern=[[step, 128]],
                                                   compare_op=cmp, fill=0.0, base=0, channel_multiplier=cm),
                 reads=[onesf.b], writes=[dst.b])

        mk_mask(UT, -1, 1, ALU.is_ge)
        mk_mask(LT, 1, -1, ALU.is_ge)
        mk_mask(SL, 1, -1, ALU.is_gt)
        mk_mask(SU, -1, 1, ALU.is_gt)
        mk_mask(IDF, 1, -1, ALU.is_equal)
        P.op("dve", f_copy(ident.ap, IDF.ap), reads=[IDF.b], writes=[ident.b])

        m0 = top[0]
        invf = alloc(F32, 16)
        ang = alloc(F32, 64, 16)
        u = alloc(F32, 2, 1024)
        ki = alloc(I32, 2048)
        kf_ = alloc(F32, 2048)
        fr = alloc(F32, 2048)
        ng = alloc(F32, 2048)
        for f in range(16):
            val = float(10000.0 ** (-(2.0 * f) / 32.0))
            P.op("pool", (lambda f=f, val=val: (lambda e: e.memset(invf.ap[:, f:f + 1], val)))(), writes=[invf.b])
        P.op("dve", f_tt(ang.ap, pos.ap.unsqueeze(2).broadcast_to([128, 64, 16]),
                         invf.ap.unsqueeze(1).broadcast_to([128, 64, 16]), ALU.mult),
             reads=[pos.b, invf.b], writes=[ang.b])
        angf = ang.ap.rearrange("p a b -> p (a b)")
        inv2pi = float(1.0 / (2.0 * math.pi))
        P.op("dve", f_ts(u.ap[:, 0, :], angf, inv2pi, ALU.mult, 0.5, ALU.add), reads=[ang.b], writes=[u.b])
        P.op("dve", f_ts(u.ap[:, 1, :], angf, inv2pi, ALU.mult, 0.75, ALU.add), reads=[ang.b], writes=[u.b])
        uf = u.ap.rearrange("p a b -> p (a b)")
        P.op("dve", f_copy(ki.ap, uf), reads=[u.b], writes=[ki.b])
        P.op("dve", f_copy(kf_.ap, ki.ap), reads=[ki.b], writes=[kf_.b])
        P.op("dve", f_tt(fr.ap, uf, kf_.ap, ALU.subtract), reads=[u.b, kf_.b], writes=[fr.b])
        P.op("dve", lambda e: e.tensor_single_scalar(out=ng.ap, in_=fr.ap, scalar=0.0, op=ALU.is_lt),
             reads=[fr.b], writes=[ng.b])
        P.op("dve", f_tt(fr.ap, fr.ap, ng.ap, ALU.add), reads=[fr.b, ng.b], writes=[fr.b])
        P.op("dve", f_ts(fr.ap, fr.ap, float(2.0 * math.pi), ALU.mult, float(-math.pi), ALU.add),
             reads=[fr.b], writes=[fr.b])
        P.op("dve", f_ts(fr.ap, fr.ap, -3.141592, ALU.max, 3.141592, ALU.min), reads=[fr.b], writes=[fr.b])
        P.op("act", f_act(tab.ap.rearrange("p a b c d -> p (a b c d)"), fr.ap, AF.Sin), reads=[fr.b], writes=[tab.b])
        P.barrier()
        top[0] = m0

        def rmsnorm_group(xsrc, xbufs, gidx, dst, dstbufs, n, ps_ss, lnv, rstd, sq=None, sqbufs=None):
            if sq is None:
                sq, sqbufs = dst, dstbufs
            P.op("act", f_act(sq, xsrc, AF.Square), reads=xbufs, writes=sqbufs)
            P.op("pe", f_mm(acc_group(ps_ss.ap[:, 0:n], [(onesb.ap, sq[:, ck, :]) for ck in range(8)])),
                 reads=sqbufs + [onesb.b], writes=[ps_ss.b])
            P.op("act", f_act(lnv.ap[:, 0:n], ps_ss.ap[:, 0:n], AF.Ln, scale=1.0 / D, bias=EPS),
                 reads=[ps_ss.b], writes=[lnv.b])
            P.op("act", f_act(rstd.ap[:, 0:n], lnv.ap[:, 0:n], AF.Exp, scale=-0.5), reads=[lnv.b], writes=[rstd.b])
            for ck in range(8):
                P.op("dve", f_stt(dst[:, ck, :], xsrc[:, ck, :], gains.ap[:, gidx, ck:ck + 1], rstd.ap[:, 0:n],
                                  ALU.mult, ALU.mult),
                     reads=xbufs + [rstd.b, gains.b], writes=dstbufs)

        def rope(src, dst, H, i, tmps):
            sv = src.ap.rearrange("p (h r x f) -> p h r x f", h=H, r=2, x=2)
            dv = dst.ap.rearrange("p (h r x f) -> p h r x f", h=H, r=2, x=2)
            x1, x2 = sv[:, :, :, 0, :], sv[:, :, :, 1, :]
            sn = tab.ap[:, 0, i, :, :].unsqueeze(1).broadcast_to([128, H, 2, 16])
            cs = tab.ap[:, 1, i, :, :].unsqueeze(1).broadcast_to([128, H, 2, 16])
            t1, t2, t3, t4 = [t.ap[:, 0:H * 32].rearrange("p (h r f) -> p h r f", h=H, r=2) for t in tmps]
            tb = [t.b for t in tmps]
            P.op("dve", f_tt(t1, x1, cs, ALU.mult), reads=[src.b, tab.b], writes=[tb[0]])
            P.op("dve", f_tt(t2, x2, sn, ALU.mult), reads=[src.b, tab.b], writes=[tb[1]])
            P.op("dve", f_tt(dv[:, :, :, 0, :], t1, t2, ALU.subtract), reads=[tb[0], tb[1]], writes=[dst.b])
            P.op("dve", f_tt(t3, x2, cs, ALU.mult), reads=[src.b, tab.b], writes=[tb[2]])
            P.op("dve", f_tt(t4, x1, sn, ALU.mult), reads=[src.b, tab.b], writes=[tb[3]])
            P.op("dve", f_tt(dv[:, :, :, 1, :], t3, t4, ALU.add), reads=[tb[2], tb[3]], writes=[dst.b])

        def headnorm(srcf, H, gi, sqt, sst, lt, rt):
            v = srcf.ap.rearrange("p (h d) -> p h d", h=H)
            sqv = sqt.ap[:, 0:H * 64].rearrange("p (h d) -> p h d", h=H)
            P.op("dve", f_tt(sqt.ap[:, 0:H * 64], srcf.ap, srcf.ap, ALU.mult), reads=[srcf.b], writes=[sqt.b])
            P.op("dve", lambda e: e.tensor_reduce(out=sst.ap[:, 0:H], in_=sqv, axis=AX.X, op=ALU.add),
                 reads=[sqt.b], writes=[sst.b])
            P.op("act", f_act(lt.ap[:, 0:H], sst.ap[:, 0:H], AF.Ln, scale=1.0 / 64.0, bias=EPS),
                 reads=[sst.b], writes=[lt.b])
            P.op("act", f_act(rt.ap[:, 0:H], lt.ap[:, 0:H], AF.Exp, scale=-0.5), reads=[lt.b], writes=[rt.b])
            P.op("dve", f_tt(v, v, rt.ap[:, 0:H].unsqueeze(2).broadcast_to([128, H, 64]), ALU.mult),
                 reads=[srcf.b, rt.b], writes=[srcf.b])
            P.op("dve", f_tt(v, v, qkg.ap[:, gi, :].unsqueeze(1).broadcast_to([128, H, 64]), ALU.mult),
                 reads=[srcf.b, qkg.b], writes=[srcf.b])

        def attention_phase():
            m_phase = top[0]
            QT = alloc(BF16, 8, T)
            KT = alloc(BF16, 2, TA)
            VA = alloc(BF16, 32, 4, 128)
            QT_b = [Buf() for _ in range(16)]
            KT_b = [Buf() for _ in range(32)]
            VA_b = [Buf() for _ in range(32)]
            m_a = top[0]
            wq = alloc(BF16, 8, 1024)
            wkv = alloc(BF16, 8, 512)
            xo = alloc(F32, 8, 256)
            hn = alloc(BF16, 8, 256)
            lnv = alloc(F32, 256)
            rstd = alloc(F32, 256)
            kf = alloc(F32, 256)
            qf = alloc(F32, 512)
            sqt = alloc(F32, 512)
            sst = alloc(F32, 8)
            lt = alloc(F32, 8)
            rt = alloc(F32, 8)
            rtm = [alloc(F32, 256) for _ in range(4)]
            Kr = alloc(BF16, 256)
            Qr = alloc(BF16, 1024)
            ps_ss, ps_kv, ps_q, ps_kt, ps_qt = psbank(0), psbank(1), psbank(2), psbank(3, BF16), psbank(4, BF16)

            P.dma("pool", f_dma(wq.ap, wq_d, True), writes=[wq.b])
            P.dma("pool", f_dma(wkv.ap, wkv_d, True), writes=[wkv.b])
            P.op("pool", lambda e: e.memset(VA.ap.rearrange("p a b c -> p (a b c)"), 1.0), writes=VA_b)

            for gi in range(16):
                own = gi < 8
                if own:
                    xsrc = hT.ap[:, :, gi * 256:(gi + 1) * 256]
                    xb = [hT_b[oc][gi // 2] for oc in range(8)]
                else:
                    P.dma("sp", f_dma(xo.ap, xT_d[:, :, gi * 256:(gi + 1) * 256]), writes=[xo.b])
                    xsrc, xb = xo.ap, [xo.b]
                rmsnorm_group(xsrc, xb, 0, hn.ap, [hn.b], 256, ps_ss, lnv, rstd)
                for j in range(2):
                    i = gi * 2 + j
                    js = slice(j * 128, (j + 1) * 128)
                    ts_ = slice(i * 128, (i + 1) * 128)
                    P.op("pe", f_mm(acc_group(ps_kv.ap, [(hn.ap[:, ck, js], wkv.ap[:, ck, :]) for ck in range(8)])),
                         reads=[hn.b, wkv.b], writes=[ps_kv.b])
                    P.op("act", f_copy_act(VA.ap[:, i, :, 0:64],
                                           ps_kv.ap[:, 256:512].rearrange("p (m d) -> p m d", m=4)),
                         reads=[ps_kv.b], writes=[VA_b[i]])
                    P.op("act", f_copy_act(kf.ap, ps_kv.ap[:, 0:256]), reads=[ps_kv.b], writes=[kf.b])
                    headnorm(kf, 4, 1, sqt, sst, lt, rt)
                    rope(kf, Kr, 4, i, rtm)
                    P.op("pe", f_tr([(ps_kt.ap[:, pi * 128:(pi + 1) * 128], Kr.ap[:, pi * 128:(pi + 1) * 128], ident.ap)
                                     for pi in range(2)]),
                         reads=[Kr.b, ident.b], writes=[ps_kt.b])
                    P.op("act", f_copy_act(KT.ap[:, :, ts_], ps_kt.ap[:, 0:256].rearrange("p (a b) -> p a b", a=2)),
                         reads=[ps_kt.b], writes=[KT_b[i]])
                    if own:
                        for half in range(2):
                            P.op("pe", f_mm(acc_group(ps_q.ap, [(hn.ap[:, ck, js], wq.ap[:, ck, half * 512:(half + 1) * 512])
                                                               for ck in range(8)])),
                                 reads=[hn.b, wq.b], writes=[ps_q.b])
                            P.op("act", f_copy_act(qf.ap, ps_q.ap), reads=[ps_q.b], writes=[qf.b])
                            headnorm(qf, 8, 0, sqt, sst, lt, rt)
                            qdst = Tile(Qr.ap[:, half * 512:(half + 1) * 512])
                            qdst.b = Qr.b
                            rope(qf, qdst, 8, i, rtm)
                        P.op("pe", f_tr([(ps_qt.ap[:, b * 128:(b + 1) * 128], Qr.ap[:, b * 128:(b + 1) * 128], ident.ap)
                                         for b in range(8)]),
                             reads=[Qr.b, ident.b], writes=[ps_qt.b])
                        P.op("act", f_copy_act(QT.ap[:, :, ts_], ps_qt.ap.rearrange("p (a b) -> p a b", a=8)),
                             reads=[ps_qt.b], writes=[QT_b[i]])
            P.barrier()
            top[0] = m_a
            wo = alloc(BF16, 8, 1024)
            oT = alloc(BF16, 8, 512)
            Pb = [alloc(BF16, 1024) for _ in range(3)]
            tl = alloc(F32, 512)
            rr = alloc(F32, 512)
            nb = alloc(F32, 512)
            S = [Tile(PS[:, 0:1024]), Tile(PS[:, 1024:2048])]
            O = [[psbank(4), psbank(5)], [psbank(6), psbank(7)]]
            P.dma("pool", f_dma(wo.ap, wo_d, True), writes=[wo.b])
            it = 0
            for qt in range(4):
                qs = slice(qt * 512, (qt + 1) * 512)
                qb = QT_b[qt * 4:(qt + 1) * 4]
                for b in range(8):
                    pi = b // 4
                    Oa, Ob = O[b % 2]
                    for kt in range(32):
                        ks = slice(kt * 128, (kt + 1) * 128)
                        s2 = S[it % 2]
                        p2 = Pb[it % 3]
                        it += 1
                        P.op("pe", f_mm([(s2.ap[:, 0:512], KT.ap[0:64, pi, ks], QT.ap[0:64, b, qs], True, True),
                                         (s2.ap[:, 512:1024], KT.ap[64:128, pi, ks], QT.ap[64:128, b, qs], True, True)]),
                             reads=[KT_b[kt]] + qb, writes=[s2.b])
                        P.op("act", f_act(p2.ap, s2.ap, AF.Exp, scale=0.125), reads=[s2.b], writes=[p2.b])
                        P.op("pe", f_mm([(Oa.ap, VA.ap[:, kt, 2 * pi, :], p2.ap[:, 0:512], kt == 0, kt == 31),
                                         (Ob.ap, VA.ap[:, kt, 2 * pi + 1, :], p2.ap[:, 512:1024], kt == 0, kt == 31)]),
                             reads=[VA_b[kt], p2.b], writes=[Oa.b, Ob.b])
                    P.op("act", f_act(tl.ap[0:64, :], Oa.ap[64:128, :], AF.Ln), reads=[Oa.b], writes=[tl.b])
                    P.op("act", f_act(rr.ap[0:64, :], tl.ap[0:64, :], AF.Exp, scale=-1.0), reads=[tl.b], writes=[rr.b])
                    P.op("dve", f_tt(oT.ap[0:64, b, :], Oa.ap[0:64, :], rr.ap[0:64, :], ALU.mult),
                         reads=[Oa.b, rr.b], writes=[oT.b])
                    P.op("act", f_act(tl.ap[64:128, :], Ob.ap[64:128, :], AF.Ln), reads=[Ob.b], writes=[tl.b])
                    P.op("act", f_act(rr.ap[64:128, :], tl.ap[64:128, :], AF.Exp, scale=-1.0), reads=[tl.b], writes=[rr.b])
                    P.op("act", f_copy_act(nb.ap[64:128, :], Ob.ap[0:64, :]), reads=[Ob.b], writes=[nb.b])
                    P.op("dve", f_tt(oT.ap[64:128, b, :], nb.ap[64:128, :], rr.ap[64:128, :], ALU.mult),
                         reads=[nb.b, rr.b], writes=[oT.b])
                for oc in range(8):
                    ps = S[oc % 2]
                    P.op("pe", f_mm(acc_group(ps.ap[:, 0:512], [(wo.ap[:, b, oc * 128:(oc + 1) * 128], oT.ap[:, b, :])
                                                               for b in range(8)])),
                         reads=[wo.b, oT.b], writes=[ps.b])
                    P.op("dve", f_tt(hT.ap[:, oc, qs], hT.ap[:, oc, qs], ps.ap[:, 0:512], ALU.add),
                         reads=[ps.b, hT_b[oc][qt]], writes=[hT_b[oc][qt]])
            P.barrier()
            top[0] = m_phase

        def f_copy_act(out, in_):
            return lambda e: e.activation(out=out, in_=in_, func=AF.Copy)

        def mlp_phase(layer, gidx):
            m_phase = top[0]
            hn = alloc(BF16, 8, T)
            hn_b = [Buf() for _ in range(4)]
            lnv = alloc(F32, 512)
            rstd = alloc(F32, 512)
            h1 = [alloc(BF16, 4, T) for _ in range(2)]
            h1_b = [[Buf() for _ in range(4)] for _ in range(2)]
            win = [alloc(BF16, 8, 512) for _ in range(2)]
            wout = [alloc(BF16, 4, 1024) for _ in range(2)]
            rl = [alloc(F32, 512) for _ in range(2)]
            psI = [psbank(b) for b in range(4)]
            psO = [psbank(b) for b in range(4, 8)]
            for tt in range(4):
                tsl = slice(tt * 512, (tt + 1) * 512)
                rmsnorm_group(hT.ap[:, :, tsl], [hT_b[oc][tt] for oc in range(8)], gidx,
                              hn.ap[:, :, tsl], [hn_b[tt]], 512, psI[tt % 4], lnv, rstd)
            cnt = [0, 0]

            def load(G):
                P.dma("pool", f_dma(win[G % 2].ap, win_d[layer, G], True), writes=[win[G % 2].b])
                P.dma("pool", f_dma(wout[G % 2].ap, wout_d[layer, G], True), writes=[wout[G % 2].b])

            def stage_in(G):
                w = win[G % 2]
                for tt in range(4):
                    tsl = slice(tt * 512, (tt + 1) * 512)
                    for fb in range(4):
                        ps = psI[cnt[0] % 4]
                        r_ = rl[cnt[0] % 2]
                        cnt[0] += 1
                        P.op("pe", f_mm(acc_group(ps.ap, [(w.ap[:, ck, fb * 128:(fb + 1) * 128], hn.ap[:, ck, tsl])
                                                          for ck in range(8)])),
                             reads=[w.b, hn_b[tt]], writes=[ps.b])
                        P.op("act", f_act(r_.ap, ps.ap, AF.Relu), reads=[ps.b], writes=[r_.b])
                        P.op("act", f_act(h1[G % 2].ap[:, fb, tsl], r_.ap, AF.Square), reads=[r_.b],
                             writes=[h1_b[G % 2][tt]])

            def stage_out(G):
                w = wout[G % 2]
                for oc in range(8):
                    for tt in range(4):
                        tsl = slice(tt * 512, (tt + 1) * 512)
                        ps = psO[cnt[1] % 4]
                        cnt[1] += 1
                        P.op("pe", f_mm(acc_group(ps.ap, [(w.ap[:, fb, oc * 128:(oc + 1) * 128], h1[G % 2].ap[:, fb, tsl])
                                                          for fb in range(4)])),
                             reads=[w.b, h1_b[G % 2][tt]], writes=[ps.b])
                        P.op("dve", f_tt(hT.ap[:, oc, tsl], hT.ap[:, oc, tsl], ps.ap, ALU.add),
                             reads=[ps.b, hT_b[oc][tt]], writes=[hT_b[oc][tt]])

            load(0)
            load(1)
            stage_in(0)
            for G in range(8):
                if G + 1 < 8:
                    stage_in(G + 1)
                stage_out(G)
                if G + 2 < 8:
                    load(G + 2)
            P.barrier()
            top[0] = m_phase

        def gla_phase():
            m_phase = top[0]
            hn = alloc(BF16, 8, T)
            hn_b = [Buf() for _ in range(4)]
            zT = alloc(F32, T)
            wup = alloc(F32, 2, 512)
            wz = alloc(BF16, 8, 32)
            wg = alloc(BF16, 8, 1536)
            wo2 = alloc(BF16, 4, 1024)
            oX = alloc(BF16, 4, T)
            oX_b = [Buf() for _ in range(16)]
            Sf = alloc(F32, 2, 256)
            Sb = alloc(BF16, 2, 256)
            rx = alloc(F32, 2, 512)
            NP_ = 2
            lg = [alloc(F32, 256) for _ in range(NP_)]
            E1 = [alloc(F32, 2, 128) for _ in range(NP_)]
            E2 = [alloc(F32, 2, 128) for _ in range(NP_)]
            E3 = [alloc(F32, 256) for _ in range(NP_)]
            qeT = [alloc(BF16, 2, 128) for _ in range(NP_)]
            keT = [alloc(BF16, 2, 128) for _ in range(NP_)]
            kd = [alloc(BF16, 256) for _ in range(NP_)]
            vt = [alloc(BF16, 512) for _ in range(NP_)]
            aTm = [alloc(BF16, 2, 128) for _ in range(NP_)]
            ot = alloc(F32, 4, 128)
            osq = alloc(BF16, 4, 128)
            lnv = alloc(F32, 512)
            rstd = alloc(F32, 512)
            sg = alloc(F32, 512)
            on = alloc(F32, 4, 128)
            ps_qk, ps_kl, ps_v, ps_cr, ps_a, ps_o, ps_ckv, ps_r = [psbank(b) for b in range(8)]
            ps_klb = ps_kl.ap.bitcast(BF16)
            qkb = [alloc(BF16, 512) for _ in range(NP_)]

            P.dma("pool", f_dma(wz.ap, wz_d, True), writes=[wz.b])
            P.dma("sp", f_dma(wup.ap[0:33], wup_d), writes=[wup.b])
            for tt in range(4):
                tsl = slice(tt * 512, (tt + 1) * 512)
                rmsnorm_group(hT.ap[:, :, tsl], [hT_b[oc][tt] for oc in range(8)], 2,
                              hn.ap[:, :, tsl], [hn_b[tt]], 512, ps_qk, lnv, rstd)
            P.op("pool", lambda e: e.memset(zT.ap[32:33, :], 1.0), writes=[zT.b])
            for tt in range(4):
                tsl = slice(tt * 512, (tt + 1) * 512)
                P.op("pe", f_mm(acc_group(ps_v.ap[0:32, :], [(wz.ap[:, ck, :], hn.ap[:, ck, tsl]) for ck in range(8)])),
                     reads=[wz.b, hn_b[tt]], writes=[ps_v.b])
                P.op("act", f_copy_act(zT.ap[0:32, tsl], ps_v.ap[0:32, :]), reads=[ps_v.b], writes=[zT.b])

            QS = float(128.0 ** -0.5)

            def one_pass(g, d, order, final):
                MI = UT if d == 0 else LT
                MS = SL if d == 0 else SU
                lastc = 127 if d == 0 else 0
                for n_, i in enumerate(order):
                    p_ = n_ % NP_
                    ts_ = slice(i * 128, (i + 1) * 128)
                    hb = hn_b[i // 4]
                    P.op("pe", f_mm(acc_group(ps_qk.ap, [(hn.ap[:, ck, ts_], wg.ap[:, ck, 0:512]) for ck in range(8)])),
                         reads=[wg.b, hb], writes=[ps_qk.b])
                    P.op("act", f_copy_act(qkb[p_].ap, ps_qk.ap), reads=[ps_qk.b], writes=[qkb[p_].b])
                    P.op("pe", f_mm([(ps_kl.ap[:, 256:512], zT.ap[0:33, ts_], wup.ap[0:33, d, g * 256:(g + 1) * 256],
                                      True, True)]),
                         reads=[zT.b, wup.b], writes=[ps_kl.b])
                    P.op("pe", f_tr([(ps_klb[:, blk * 128:(blk + 1) * 128], qkb[p_].ap[:, blk * 128:(blk + 1) * 128], ident.ap)
                                     for blk in range(4)]),
                         reads=[qkb[p_].b, ident.b], writes=[ps_kl.b])
                    P.op("pe", f_mm(acc_group(ps_v.ap, [(hn.ap[:, ck, ts_], wg.ap[:, ck, 512:1024]) for ck in range(8)])),
                         reads=[wg.b, hb], writes=[ps_v.b])
                    P.op("act", f_act(lg[p_].ap, ps_kl.ap[:, 256:512], AF.Exp, scale=-1.0), reads=[ps_kl.b], writes=[lg[p_].b])
                    P.op("act", f_act(lg[p_].ap, lg[p_].ap, AF.Ln, bias=1.0), reads=[lg[p_].b], writes=[lg[p_].b])
                    P.op("act", f_copy_act(vt[p_].ap, ps_v.ap), reads=[ps_v.b], writes=[vt[p_].b])
                    P.op("pe", f_mm([(ps_cr.ap[:, h * 128:(h + 1) * 128], lg[p_].ap[:, h * 128:(h + 1) * 128], MI.ap, True, True)
                                     for h in range(2)]
                                    + [(ps_cr.ap[:, 256:512], MS.ap, lg[p_].ap, True, True)]),
                         reads=[lg[p_].b, MI.b, MS.b], writes=[ps_cr.b])
                    csv = ps_cr.ap[:, 0:256].rearrange("p (h c) -> p h c", h=2)
                    P.op("act", f_act(E1[p_].ap, csv, AF.Exp, scale=-1.0 / 16.0), reads=[ps_cr.b], writes=[E1[p_].b])
                    P.op("act", f_act(E2[p_].ap, csv, AF.Exp, scale=1.0 / 16.0), reads=[ps_cr.b], writes=[E2[p_].b])
                    P.op("act", f_act(E3[p_].ap, ps_cr.ap[:, 256:512], AF.Exp, scale=-1.0 / 16.0),
                         reads=[ps_cr.b], writes=[E3[p_].b])
                    qkv = ps_klb[:, 0:512].rearrange("p (b c) -> p b c", b=4)
                    P.op("dve", f_stt(qeT[p_].ap, qkv[:, 0:2, :], QS, E1[p_].ap, ALU.mult, ALU.mult),
                         reads=[ps_kl.b, E1[p_].b], writes=[qeT[p_].b])
                    P.op("dve", f_tt(keT[p_].ap, qkv[:, 2:4, :], E2[p_].ap, ALU.mult),
                         reads=[ps_kl.b, E2[p_].b], writes=[keT[p_].b])
                    P.op("dve", f_tt(kd[p_].ap, ps_qk.ap[:, 256:512], E3[p_].ap, ALU.mult),
                         reads=[ps_qk.b, E3[p_].b], writes=[kd[p_].b])
                    P.op("pe", f_mm([(ps_a.ap[:, h * 128:(h + 1) * 128], keT[p_].ap[:, h, :], qeT[p_].ap[:, h, :], True, True)
                                     for h in range(2)]),
                         reads=[keT[p_].b, qeT[p_].b], writes=[ps_a.b])
                    P.op("dve", f_tt(aTm[p_].ap, ps_a.ap[:, 0:256].rearrange("p (h c) -> p h c", h=2),
                                     MI.ap.unsqueeze(1).broadcast_to([128, 2, 128]), ALU.mult),
                         reads=[ps_a.b, MI.b], writes=[aTm[p_].b])
                    grp = []
                    for h in range(2):
                        for vb in range(2):
                            blk = h * 2 + vb
                            o_ = ps_o.ap[:, blk * 128:(blk + 1) * 128]
                            grp.append((o_, vt[p_].ap[:, h * 256 + vb * 128:h * 256 + (vb + 1) * 128], aTm[p_].ap[:, h, :],
                                        True, False))
                            grp.append((o_, Sb.ap[:, h, vb * 128:(vb + 1) * 128], qeT[p_].ap[:, h, :], False, True))
                    P.op("pe", f_mm(grp), reads=[vt[p_].b, aTm[p_].b, Sb.b, qeT[p_].b], writes=[ps_o.b])
                    P.op("pe", f_mm([(ps_ckv.ap[:, h * 256:(h + 1) * 256], kd[p_].ap[:, h * 128:(h + 1) * 128],
                                      vt[p_].ap[:, h * 256:(h + 1) * 256], True, True) for h in range(2)]),
                         reads=[kd[p_].b, vt[p_].b], writes=[ps_ckv.b])
                    for h in range(2):
                        P.op("dve", f_stt(Sf.ap[:, h, :], Sf.ap[:, h, :], E1[p_].ap[:, h, lastc:lastc + 1],
                                          ps_ckv.ap[:, h * 256:(h + 1) * 256], ALU.mult, ALU.add),
                             reads=[Sf.b, E1[p_].b, ps_ckv.b], writes=[Sf.b])
                    P.op("act", f_copy_act(Sb.ap, Sf.ap), reads=[Sf.b], writes=[Sb.b])
                    ov = ps_o.ap.rearrange("p (b c) -> p b c", b=4)
                    if not final:
                        P.op("act", f_copy_act(oX.ap[:, :, ts_], ov), reads=[ps_o.b], writes=[oX_b[i]])
                        continue
                    P.op("dve", f_tt(ot.ap, ov, oX.ap[:, :, ts_], ALU.add), reads=[ps_o.b, oX_b[i]], writes=[ot.b])
                    P.op("act", f_act(osq.ap, ot.ap, AF.Square), reads=[ot.b], writes=[osq.b])
                    grp = []
                    for h in range(2):
                        grp += acc_group(ps_a.ap[:, 256 + h * 128:256 + (h + 1) * 128],
                                         [(onesb.ap, osq.ap[:, h * 2 + vb, :]) for vb in range(2)])
                    P.op("pe", f_mm(grp), reads=[osq.b, onesb.b], writes=[ps_a.b])
                    P.op("act", f_act(lnv.ap[:, 0:256], ps_a.ap[:, 256:512], AF.Ln, scale=1.0 / 256.0, bias=EPS),
                         reads=[ps_a.b], writes=[lnv.b])
                    P.op("act", f_act(rstd.ap[:, 0:256], lnv.ap[:, 0:256], AF.Exp, scale=-0.5), reads=[lnv.b], writes=[rstd.b])
                    grp = []
                    for blk in range(4):
                        grp += acc_group(ps_r.ap[:, blk * 128:(blk + 1) * 128],
                                         [(wg.ap[:, ck, 1024 + blk * 128:1024 + (blk + 1) * 128], hn.ap[:, ck, ts_])
                                          for ck in range(8)])
                    P.op("pe", f_mm(grp), reads=[wg.b, hb], writes=[ps_r.b])
                    P.op("act", f_act(sg.ap, ps_r.ap, AF.Exp, scale=-1.0), reads=[ps_r.b], writes=[sg.b])
                    P.op("act", f_act(sg.ap, sg.ap, AF.Ln, bias=1.0), reads=[sg.b], writes=[sg.b])
                    P.op("act", f_act(sg.ap, sg.ap, AF.Exp, scale=-1.0), reads=[sg.b], writes=[sg.b])
                    for blk in range(4):
                        h, vb = blk // 2, blk % 2
                        P.op("dve", f_stt(on.ap[:, blk, :], ot.ap[:, blk, :], og.ap[:, vb:vb + 1],
                                          rstd.ap[:, h * 128:(h + 1) * 128], ALU.mult, ALU.mult),
                             reads=[ot.b, og.b, rstd.b], writes=[on.b])
                    P.op("dve", f_tt(sg.ap, ps_r.ap, sg.ap, ALU.mult), reads=[ps_r.b, sg.b], writes=[sg.b])
                    P.op("dve", f_tt(oX.ap[:, :, ts_], on.ap, sg.ap.rearrange("p (b c) -> p b c", b=4), ALU.mult),
                         reads=[on.b, sg.b], writes=[oX_b[i]])

            for g in range(2):
                P.dma("pool", f_dma(wg.ap, wg_d[g], True), writes=[wg.b])
                P.dma("pool", f_dma(wo2.ap, wo2_d[g], True), writes=[wo2.b])
                P.op("dve", lambda e: e.memset(Sf.ap.rearrange("p a b -> p (a b)"), 0.0), writes=[Sf.b])
                P.op("pool", lambda e: e.memset(Sb.ap.rearrange("p a b -> p (a b)"), 0.0), writes=[Sb.b])
                one_pass(g, 0, list(range(16)), False)
                ibb, obb = Buf(), Buf()
                sfl = Sf.ap.rearrange("p a b -> p (a b)")
                P.dma("pool", f_dma(ib_t[g].ap(), sfl), reads=[Sf.b], writes=[ibb])
                P.custom("pool", (lambda g=g: (lambda e: e.collective_compute(
                    "AllGather", ALU.bypass, replica_groups=[[0, 1], [2, 3], [4, 5], [6, 7]],
                    ins=[ib_t[g].ap().opt()], outs=[ob_t[g].ap().opt()])))(), "cc%d" % g, reads=[ibb], writes=[obb])
                P.dma("pool", f_dma(rx.ap, ob_t[g].ap().rearrange("(r p) c -> p r c", p=128)), reads=[obb], writes=[rx.b])
                P.op("dve", f_ts(sfl, rx.ap[:, 0, :], mex.ap[:, 0:1], ALU.mult), reads=[rx.b, mex.b], writes=[Sf.b])
                P.op("dve", f_stt(sfl, rx.ap[:, 1, :], mex.ap[:, 1:2], sfl, ALU.mult, ALU.add),
                     reads=[rx.b, mex.b, Sf.b], writes=[Sf.b])
                P.op("act", f_copy_act(Sb.ap, Sf.ap), reads=[Sf.b], writes=[Sb.b])
                one_pass(g, 1, list(range(15, -1, -1)), True)
                k = 0
                for oc in range(8):
                    for tt in range(4):
                        tsl = slice(tt * 512, (tt + 1) * 512)
                        ps = [ps_qk, ps_kl, ps_v, ps_cr][k % 4]
                        k += 1
                        P.op("pe", f_mm(acc_group(ps.ap, [(wo2.ap[:, blk, oc * 128:(oc + 1) * 128], oX.ap[:, blk, tsl])
                                                          for blk in range(4)])),
                             reads=[wo2.b] + oX_b[tt * 4:(tt + 1) * 4], writes=[ps.b])
                        P.op("dve", f_tt(hT.ap[:, oc, tsl], hT.ap[:, oc, tsl], ps.ap, ALU.add),
                             reads=[ps.b, hT_b[oc][tt]], writes=[hT_b[oc][tt]])
                P.barrier()
            top[0] = m_phase

        def output_phase(do_norm):
            m_phase = top[0]
            lnv = alloc(F32, 512)
            rstd = alloc(F32, 512)
            sq = alloc(BF16, 8, 512)
            yo = [alloc(F32, 8, 512) for _ in range(2)]
            toks = []
            for tt in range(4):
                tsl = slice(tt * 512, (tt + 1) * 512)
                xb = [hT_b[oc][tt] for oc in range(8)]
                if do_norm:
                    y = yo[tt % 2]
                    import os
                    mode = os.environ.get("KDBG_OUT", "")
                    if mode == "B":
                        P.op("dve", f_copy(y.ap, hT.ap[:, :, tsl]), reads=xb, writes=[y.b])
                    else:
                        rmsnorm_group(hT.ap[:, :, tsl], xb, 4, y.ap, [y.b], 512, psbank(tt % 4), lnv, rstd, sq=sq.ap, sqbufs=[sq.b])
                    if mode == "A":
                        toks.append(P.dma("sp", f_dma(y_d[:, :, tsl], hT.ap[:, :, tsl]), reads=xb + [y.b]))
                    else:
                        toks.append(P.dma("sp", f_dma(y_d[:, :, tsl], y.ap), reads=[y.b]))
                else:
                    toks.append(P.dma("sp", f_dma(y_d[:, :, tsl], hT.ap[:, :, tsl]), reads=xb))
            top[0] = m_phase

        phases = [("attn", attention_phase), ("mlp1", lambda: mlp_phase(0, 1)), ("gla", gla_phase),
                  ("mlp2", lambda: mlp_phase(1, 3))]
        done = False
        if only is not None:
            for name, fn in phases:
                if name in only:
                    fn()
            output_phase("final" in only)
            done = True
        else:
            for name, fn in phases:
                fn()
                if stop_after == name:
                    output_phase(False)
                    done = True
                    break
        if not done:
            output_phase(True)
        P.barrier()

        import contextlib
        with contextlib.ExitStack() as es:
            sems = {}
            for k in P.count.keys():
                sems[k] = es.enter_context(nc.semaphore("s_" + k))
            block = es.enter_context(nc.Block())

            def replay(name, e):
                for item in P.lists[name]:
                    if item[0] == "w":
                        e.wait_ge(sems[item[1]], item[2])
                    else:
                        inst = item[1](e)
                        if item[0] == "c":
                            inst.then_inc(sems[item[2]])
                        else:
                            inst.then_inc(sems[item[2]], item[3])

            @block.tensor
            def _(e):
                replay("pe", e)

            @block.scalar
            def _(e):
                replay("act", e)

            @block.vector
            def _(e):
                replay("dve", e)

            @block.gpsimd
            def _(e):
                replay("pool", e)

            @block.sync
            def _(e):
                replay("sp", e)
    return nc


def _fm(w):
    K, N = w.shape
    return np.ascontiguousarray(w.reshape(K // 128, 128, N).transpose(1, 0, 2))


def _prep(inputs):
    f = lambda a: np.asarray(a, dtype=np.float32)
    x = f(inputs["x"])
    norm_mix, norm_mlp, final_norm = f(inputs["norm_mix"]), f(inputs["norm_mlp"]), f(inputs["final_norm"])
    wqkv = f(inputs["attn_w_qkv"])[0]
    qn, kn = f(inputs["attn_q_norm"])[0], f(inputs["attn_k_norm"])[0]
    wo = f(inputs["attn_w_o"])[0]
    gw = f(inputs["gla_w_in"])[0]
    gup, gb = f(inputs["gla_w_gate_up"])[0], f(inputs["gla_b_gate"])[0]
    gon, gwo = f(inputs["gla_out_norm"])[0], f(inputs["gla_w_o"])[0]
    mwi, mwo = f(inputs["mlp_w_in"]), f(inputs["mlp_w_out"])

    gains = np.stack([norm_mix[0], norm_mlp[0], norm_mix[1], norm_mlp[1], final_norm], 0)
    gains_l = np.ascontiguousarray(gains.reshape(5, 8, 128).transpose(2, 0, 1)).reshape(128, 40)
    qkg_l = np.ascontiguousarray(np.broadcast_to(np.concatenate([qn, kn])[None, :], (128, 128)))
    qcols = np.concatenate([np.arange(h * 64, (h + 1) * 64) for h in HO])
    wq_l = _fm(wqkv[:, :1024][:, qcols])
    wkv_l = _fm(wqkv[:, 1024:1536])
    wo_l = _fm(wo[qcols, :])
    win_l = np.stack([np.stack([_fm(mwi[l][:, G * 512:(G + 1) * 512]) for G in range(8)]) for l in range(2)])
    wout_l = np.stack([np.stack([_fm(mwo[l][G * 512:(G + 1) * 512, :]) for G in range(8)]) for l in range(2)])
    wg_l = []
    for g in range(2):
        cols = np.concatenate([np.arange(g * 256, (g + 1) * 256), 512 + np.arange(g * 256, (g + 1) * 256),
                               1024 + np.arange(g * 512, (g + 1) * 512), 2048 + np.arange(g * 512, (g + 1) * 512)])
        wg_l.append(_fm(gw[:, cols]))
    wg_l = np.stack(wg_l)
    og_l = np.ascontiguousarray(gon.reshape(2, 128).T)
    wo2_l = np.stack([_fm(gwo[g * 512:(g + 1) * 512, :]) for g in range(2)])

    maps = []
    idxs = []
    for c in range(NCORES):
        b, s = c // 2, c % 2
        if s == 0:
            own = np.arange(0, T)
            other = np.arange(T, TA)
            dirs = (0, 1)
        else:
            own = np.arange(TA - 1, T - 1, -1)
            other = np.arange(0, T)
            dirs = (1, 0)
        idx = np.concatenate([own, other])
        idxs.append(own)
        xT_l = np.ascontiguousarray(x[b][idx].T.reshape(8, 128, TA).transpose(1, 0, 2))
        pr = (idx // 64).astype(np.float32)
        pc = (idx % 64).astype(np.float32)
        pos_l = np.ascontiguousarray(np.stack([pr.reshape(32, 128).T, pc.reshape(32, 128).T], -1)).reshape(128, 64)
        zc = np.concatenate([3072 + dirs[0] * 16 + np.arange(16), 3072 + dirs[1] * 16 + np.arange(16)])
        wz_l = _fm(gw[:, zc])
        wup_l = np.zeros((33, 2, 512), np.float32)
        wup_l[0:16, 0] = gup[dirs[0]]
        wup_l[16:32, 1] = gup[dirs[1]]
        wup_l[32, 0] = gb[dirs[0]]
        wup_l[32, 1] = gb[dirs[1]]
        mex_l = np.zeros((128, 2), np.float32)
        mex_l[:, 1 - s] = 1.0
        maps.append({
            "xT": xT_l, "pos": pos_l, "gains": gains_l, "qkg": qkg_l, "wq": wq_l, "wkv": wkv_l, "wo": wo_l,
            "win": win_l, "wout": wout_l, "wg": wg_l, "wz": wz_l, "wup": wup_l, "og": og_l, "wo2": wo2_l,
            "mex": mex_l,
        })
    return maps, idxs


_NC_CACHE = {}


def run(inputs, stop_after=None, trace=False, only=None):
    maps, idxs = _prep(inputs)
    key = (stop_after, only)
    if key not in _NC_CACHE:
        _NC_CACHE[key] = build_program(stop_after, only)
    nc = _NC_CACHE[key]
    res = run_bass_kernel_spmd(nc, maps, core_ids=list(range(NCORES)), trace=trace)
    out = np.empty((4, TA, D), np.float32)
    for c in range(NCORES):
        y = np.asarray(res.results[c]["y"])
        out[c // 2, idxs[c], :] = y.transpose(2, 1, 0).reshape(T, D)
    return out, res


def kernel(**inputs):
    out, _ = run(inputs)
    return out
```

```python
import math
from functools import reduce

import numpy as np
import concourse.bass as bass
import concourse.mybir as mybir
from concourse.bass_utils import run_bass_kernel_spmd

F32 = mybir.dt.float32
BF16 = mybir.dt.bfloat16
I32 = mybir.dt.int32
AF = mybir.ActivationFunctionType
ALU = mybir.AluOpType
AX = mybir.AxisListType

NCORES = 8
D = 1024
T = 2048
TA = 4096
EPS = 1e-6
HO = [0, 4, 1, 5, 2, 6, 3, 7, 8, 12, 9, 13, 10, 14, 11, 15]
ENGS = ("pe", "act", "dve", "pool", "sp")
NDSEM = 12
SBUF_BYTES = 212800


class Buf:
    __slots__ = ("w", "r")

    def __init__(self):
        self.w = None
        self.r = {}


class Tile:
    def __init__(self, ap):
        self.ap = ap
        self.b = Buf()


class Prog:
    def __init__(self):
        self.lists = {e: [] for e in ENGS}
        self.count = {}
        self.waited = {e: {} for e in ENGS}
        self.dma_i = {"sp": 0, "pool": 0}

    def _deps(self, eng, reads, writes):
        toks = {}

        def add(t):
            if t is None:
                return
            k, v = t
            if eng == "pe" and k == "pe":
                return
            if toks.get(k, 0) < v:
                toks[k] = v

        for b in reads:
            add(b.w)
        for b in writes:
            add(b.w)
            for k, v in b.r.items():
                add((k, v))
        for k, v in toks.items():
            if self.waited[eng].get(k, 0) < v:
                self.lists[eng].append(("w", k, v))
                self.waited[eng][k] = v

    def _mark(self, tok, reads, writes):
        k, v = tok
        for b in reads:
            if b.r.get(k, 0) < v:
                b.r[k] = v
        for b in writes:
            b.w = tok
            b.r = {}

    def op(self, eng, fn, reads=(), writes=()):
        self._deps(eng, reads, writes)
        self.count[eng] = self.count.get(eng, 0) + 1
        tok = (eng, self.count[eng])
        self.lists[eng].append(("o", fn, eng, 1))
        self._mark(tok, reads, writes)
        return tok

    def dma(self, q, fn, reads=(), writes=()):
        self._deps(q, reads, writes)
        i = self.dma_i[q]
        self.dma_i[q] = i + 1
        key = "d%s%d" % (q, i % NDSEM)
        self.count[key] = self.count.get(key, 0) + 16
        tok = (key, self.count[key])
        self.lists[q].append(("o", fn, key, 16))
        self._mark(tok, reads, writes)
        return tok

    def custom(self, eng, fn, key, reads=(), writes=()):
        self._deps(eng, reads, writes)
        self.count[key] = self.count.get(key, 0) + 1
        tok = (key, self.count[key])
        self.lists[eng].append(("c", fn, key, 1))
        self._mark(tok, reads, writes)
        return tok

    def barrier(self):
        for e in ENGS:
            for k, v in self.count.items():
                if k == e and e != "pe":
                    pass
                if self.waited[e].get(k, 0) < v:
                    self.lists[e].append(("w", k, v))
                    self.waited[e][k] = v


def _prod(s):
    return reduce(lambda a, b: a * b, s, 1)


def build_program(stop_after=None, only=None):
    nc = bass.Bass("TRN2", target_bir_lowering=False)

    def din(name, shape):
        return nc.dram_tensor(name, list(shape), F32, kind="ExternalInput").ap()

    xT_d = din("xT", [128, 8, TA])
    pos_d = din("pos", [128, 64])
    gains_d = din("gains", [128, 40])
    qkg_d = din("qkg", [128, 128])
    wq_d = din("wq", [128, 8, 1024])
    wkv_d = din("wkv", [128, 8, 512])
    wo_d = din("wo", [128, 8, 1024])
    win_d = din("win", [2, 8, 128, 8, 512])
    wout_d = din("wout", [2, 8, 128, 4, 1024])
    wg_d = din("wg", [2, 128, 8, 1536])
    wz_d = din("wz", [128, 8, 32])
    wup_d = din("wup", [33, 2, 512])
    og_d = din("og", [128, 2])
    wo2_d = din("wo2", [2, 128, 4, 1024])
    mex_d = din("mex", [128, 2])
    y_d = nc.dram_tensor("y", [128, 8, T], F32, kind="ExternalOutput").ap()
    ib_t = [nc.dram_tensor("ib%d" % g, [128, 512], F32) for g in range(2)]
    ob_t = [nc.dram_tensor("ob%d" % g, [256, 512], F32) for g in range(2)]

    P = Prog()

    with (
        nc.sbuf_tensor("arena", [128, SBUF_BYTES // 4], F32) as A,
        nc.psum_tensor("psum", [128, 4096], F32) as PS,
    ):
        top = [0]

        def alloc(dtype, *shape):
            esz = 4 if dtype in (F32, I32) else 2
            nbytes = (_prod(shape) * esz + 63) // 64 * 64
            off = top[0]
            top[0] += nbytes
            assert top[0] <= SBUF_BYTES, ("SBUF overflow", top[0])
            ap = A[:, off // 4:(off + nbytes) // 4]
            if dtype != F32:
                ap = ap.bitcast(dtype)
            ap = ap[:, 0:_prod(shape)]
            if len(shape) > 1:
                names = ["a%d" % i for i in range(len(shape))]
                ap = ap.rearrange("p (%s) -> p %s" % (" ".join(names), " ".join(names)),
                                  **{n: s for n, s in zip(names[:-1], shape[:-1])})
            return Tile(ap)

        def psbank(b, dtype=F32):
            ap = PS[:, b * 512:(b + 1) * 512]
            if dtype != F32:
                ap = ap.bitcast(dtype)
            return Tile(ap)

        def f_act(out, in_, func, scale=1.0, bias=None):
            if bias is None:
                return lambda e: e.activation(out=out, in_=in_, func=func, scale=scale)
            return lambda e: e.activation(out=out, in_=in_, func=func, scale=scale, bias=bias)

        def f_tt(out, in0, in1, op):
            return lambda e: e.tensor_tensor(out=out, in0=in0, in1=in1, op=op)

        def f_stt(out, in0, scalar, in1, op0, op1):
            return lambda e: e.scalar_tensor_tensor(out=out, in0=in0, scalar=scalar, in1=in1, op0=op0, op1=op1)

        def f_ts(out, in0, s1, op0, s2=None, op1=None):
            if op1 is None:
                return lambda e: e.tensor_scalar(out=out, in0=in0, scalar1=s1, scalar2=None, op0=op0)
            return lambda e: e.tensor_scalar(out=out, in0=in0, scalar1=s1, scalar2=s2, op0=op0, op1=op1)

        def f_copy(out, in_):
            return lambda e: e.tensor_copy(out=out, in_=in_)

        def f_mm(group):
            def fn(e):
                inst = None
                for (o, l, r, st, sp) in group:
                    inst = e.matmul(o, l, r, start=st, stop=sp)
                return inst
            return fn

        def f_tr(group):
            def fn(e):
                inst = None
                for (o, i, idn) in group:
                    inst = e.transpose(o, i, idn)
                return inst
            return fn

        def f_dma(out, in_, cast=False):
            if cast:
                return lambda e: e.dma_start(out=out, in_=in_, max_dma_last_dim=4096)
            return lambda e: e.dma_start(out=out, in_=in_)

        def acc_group(out, pairs):
            n = len(pairs)
            return [(out, l, r, i == 0, i == n - 1) for i, (l, r) in enumerate(pairs)]

        hT = alloc(F32, 8, T)
        hT_b = [[Buf() for _ in range(4)] for _ in range(8)]
        hT_all = [b for row in hT_b for b in row]
        onesf = alloc(F32, 128)
        UT = alloc(F32, 128)
        LT = alloc(F32, 128)
        SL = alloc(F32, 128)
        SU = alloc(F32, 128)
        IDF = alloc(F32, 128)
        ident = alloc(BF16, 128)
        onesb = alloc(BF16, 128)
        gains = alloc(F32, 5, 8)
        qkg = alloc(F32, 2, 64)
        og = alloc(F32, 2)
        mex = alloc(F32, 2)
        pos = alloc(F32, 64)
        tab = alloc(F32, 2, 32, 2, 16)
        persist_top = top[0]

        P.dma("sp", f_dma(hT.ap, xT_d[:, :, 0:T]), writes=hT_all)
        P.dma("sp", f_dma(gains.ap, gains_d.rearrange("p (a b) -> p a b", a=5)), writes=[gains.b])
        P.dma("sp", f_dma(qkg.ap, qkg_d.rearrange("p (a b) -> p a b", a=2)), writes=[qkg.b])
        P.dma("sp", f_dma(og.ap, og_d), writes=[og.b])
        P.dma("sp", f_dma(mex.ap, mex_d), writes=[mex.b])
        P.dma("sp", f_dma(pos.ap, pos_d), writes=[pos.b])

        P.op("pool", lambda e: e.memset(onesf.ap, 1.0), writes=[onesf.b])
        P.op("pool", lambda e: e.memset(onesb.ap, 1.0), writes=[onesb.b])

        def mk_mask(dst, cm, step, cmp):
            P.op("pool", lambda e: e.affine_select(out=dst.ap, in_=onesf.ap, pattern=[[step, 128]],
                                                   compare_op=cmp, fill=0.0, base=0, channel_multiplier=cm),
                 reads=[onesf.b], writes=[dst.b])

        mk_mask(UT, -1, 1, ALU.is_ge)
        mk_mask(LT, 1, -1, ALU.is_ge)
        mk_mask(SL, 1, -1, ALU.is_gt)
        mk_mask(SU, -1, 1, ALU.is_gt)
        mk_mask(IDF, 1, -1, ALU.is_equal)
        P.op("dve", f_copy(ident.ap, IDF.ap), reads=[IDF.b], writes=[ident.b])

        m0 = top[0]
        invf = alloc(F32, 16)
        ang = alloc(F32, 64, 16)
        u = alloc(F32, 2, 1024)
        ki = alloc(I32, 2048)
        kf_ = alloc(F32, 2048)
        fr = alloc(F32, 2048)
        ng = alloc(F32, 2048)
        for f in range(16):
            val = float(10000.0 ** (-(2.0 * f) / 32.0))
            P.op("pool", (lambda f=f, val=val: (lambda e: e.memset(invf.ap[:, f:f + 1], val)))(), writes=[invf.b])
        P.op("dve", f_tt(ang.ap, pos.ap.unsqueeze(2).broadcast_to([128, 64, 16]),
                         invf.ap.unsqueeze(1).broadcast_to([128, 64, 16]), ALU.mult),
             reads=[pos.b, invf.b], writes=[ang.b])
        angf = ang.ap.rearrange("p a b -> p (a b)")
        inv2pi = float(1.0 / (2.0 * math.pi))
        P.op("dve", f_ts(u.ap[:, 0, :], angf, inv2pi, ALU.mult, 0.5, ALU.add), reads=[ang.b], writes=[u.b])
        P.op("dve", f_ts(u.ap[:, 1, :], angf, inv2pi, ALU.mult, 0.75, ALU.add), reads=[ang.b], writes=[u.b])
        uf = u.ap.rearrange("p a b -> p (a b)")
        P.op("dve", f_copy(ki.ap, uf), reads=[u.b], writes=[ki.b])
        P.op("dve", f_copy(kf_.ap, ki.ap), reads=[ki.b], writes=[kf_.b])
        P.op("dve", f_tt(fr.ap, uf, kf_.ap, ALU.subtract), reads=[u.b, kf_.b], writes=[fr.b])
        P.op("dve", lambda e: e.tensor_single_scalar(out=ng.ap, in_=fr.ap, scalar=0.0, op=ALU.is_lt),
             reads=[fr.b], writes=[ng.b])
        P.op("dve", f_tt(fr.ap, fr.ap, ng.ap, ALU.add), reads=[fr.b, ng.b], writes=[fr.b])
        P.op("dve", f_ts(fr.ap, fr.ap, float(2.0 * math.pi), ALU.mult, float(-math.pi), ALU.add),
             reads=[fr.b], writes=[fr.b])
        P.op("dve", f_ts(fr.ap, fr.ap, -3.141592, ALU.max, 3.141592, ALU.min), reads=[fr.b], writes=[fr.b])
        P.op("act", f_act(tab.ap.rearrange("p a b c d -> p (a b c d)"), fr.ap, AF.Sin), reads=[fr.b], writes=[tab.b])
        P.barrier()
        top[0] = m0

        def rmsnorm_group(xsrc, xbufs, gidx, dst, dstbufs, n, ps_ss, lnv, rstd, sq=None, sqbufs=None):
            if sq is None:
                sq, sqbufs = dst, dstbufs
            P.op("act", f_act(sq, xsrc, AF.Square), reads=xbufs, writes=sqbufs)
            P.op("pe", f_mm(acc_group(ps_ss.ap[:, 0:n], [(onesb.ap, sq[:, ck, :]) for ck in range(8)])),
                 reads=sqbufs + [onesb.b], writes=[ps_ss.b])
            P.op("act", f_act(lnv.ap[:, 0:n], ps_ss.ap[:, 0:n], AF.Ln, scale=1.0 / D, bias=EPS),
                 reads=[ps_ss.b], writes=[lnv.b])
            P.op("act", f_act(rstd.ap[:, 0:n], lnv.ap[:, 0:n], AF.Exp, scale=-0.5), reads=[lnv.b], writes=[rstd.b])
            for ck in range(8):
                P.op("dve", f_stt(dst[:, ck, :], xsrc[:, ck, :], gains.ap[:, gidx, ck:ck + 1], rstd.ap[:, 0:n],
                                  ALU.mult, ALU.mult),
                     reads=xbufs + [rstd.b, gains.b], writes=dstbufs)

        def rope(src, dst, H, i, tmps):
            sv = src.ap.rearrange("p (h r x f) -> p h r x f", h=H, r=2, x=2)
            dv = dst.ap.rearrange("p (h r x f) -> p h r x f", h=H, r=2, x=2)
            x1, x2 = sv[:, :, :, 0, :], sv[:, :, :, 1, :]
            sn = tab.ap[:, 0, i, :, :].unsqueeze(1).broadcast_to([128, H, 2, 16])
            cs = tab.ap[:, 1, i, :, :].unsqueeze(1).broadcast_to([128, H, 2, 16])
            t1, t2, t3, t4 = [t.ap[:, 0:H * 32].rearrange("p (h r f) -> p h r f", h=H, r=2) for t in tmps]
            tb = [t.b for t in tmps]
            P.op("dve", f_tt(t1, x1, cs, ALU.mult), reads=[src.b, tab.b], writes=[tb[0]])
            P.op("dve", f_tt(t2, x2, sn, ALU.mult), reads=[src.b, tab.b], writes=[tb[1]])
            P.op("dve", f_tt(dv[:, :, :, 0, :], t1, t2, ALU.subtract), reads=[tb[0], tb[1]], writes=[dst.b])
            P.op("dve", f_tt(t3, x2, cs, ALU.mult), reads=[src.b, tab.b], writes=[tb[2]])
            P.op("dve", f_tt(t4, x1, sn, ALU.mult), reads=[src.b, tab.b], writes=[tb[3]])
            P.op("dve", f_tt(dv[:, :, :, 1, :], t3, t4, ALU.add), reads=[tb[2], tb[3]], writes=[dst.b])

        def headnorm(srcf, H, gi, sqt, sst, lt, rt):
            v = srcf.ap.rearrange("p (h d) -> p h d", h=H)
            sqv = sqt.ap[:, 0:H * 64].rearrange("p (h d) -> p h d", h=H)
            P.op("dve", f_tt(sqt.ap[:, 0:H * 64], srcf.ap, srcf.ap, ALU.mult), reads=[srcf.b], writes=[sqt.b])
            P.op("dve", lambda e: e.tensor_reduce(out=sst.ap[:, 0:H], in_=sqv, axis=AX.X, op=ALU.add),
                 reads=[sqt.b], writes=[sst.b])
            P.op("act", f_act(lt.ap[:, 0:H], sst.ap[:, 0:H], AF.Ln, scale=1.0 / 64.0, bias=EPS),
                 reads=[sst.b], writes=[lt.b])
            P.op("act", f_act(rt.ap[:, 0:H], lt.ap[:, 0:H], AF.Exp, scale=-0.5), reads=[lt.b], writes=[rt.b])
            P.op("dve", f_tt(v, v, rt.ap[:, 0:H].unsqueeze(2).broadcast_to([128, H, 64]), ALU.mult),
                 reads=[srcf.b, rt.b], writes=[srcf.b])
            P.op("dve", f_tt(v, v, qkg.ap[:, gi, :].unsqueeze(1).broadcast_to([128, H, 64]), ALU.mult),
                 reads=[srcf.b, qkg.b], writes=[srcf.b])

        def attention_phase():
            m_phase = top[0]
            QT = alloc(BF16, 8, T)
            KT = alloc(BF16, 2, TA)
            VA = alloc(BF16, 32, 4, 128)
            QT_b = [Buf() for _ in range(16)]
            KT_b = [Buf() for _ in range(32)]
            VA_b = [Buf() for _ in range(32)]
            m_a = top[0]
            wq = alloc(BF16, 8, 1024)
            wkv = alloc(BF16, 8, 512)
            xo = alloc(F32, 8, 256)
            hn = alloc(BF16, 8, 256)
            lnv = alloc(F32, 256)
            rstd = alloc(F32, 256)
            kf = alloc(F32, 256)
            qf = alloc(F32, 512)
            sqt = alloc(F32, 512)
            sst = alloc(F32, 8)
            lt = alloc(F32, 8)
            rt = alloc(F32, 8)
            rtm = [alloc(F32, 256) for _ in range(4)]
            Kr = alloc(BF16, 256)
            Qr = alloc(BF16, 1024)
            ps_ss, ps_kv, ps_q, ps_kt, ps_qt = psbank(0), psbank(1), psbank(2), psbank(3, BF16), psbank(4, BF16)

            P.dma("pool", f_dma(wq.ap, wq_d, True), writes=[wq.b])
            P.dma("pool", f_dma(wkv.ap, wkv_d, True), writes=[wkv.b])
            P.op("pool", lambda e: e.memset(VA.ap.rearrange("p a b c -> p (a b c)"), 1.0), writes=VA_b)

            for gi in range(16):
                own = gi < 8
                if own:
                    xsrc = hT.ap[:, :, gi * 256:(gi + 1) * 256]
                    xb = [hT_b[oc][gi // 2] for oc in range(8)]
                else:
                    P.dma("sp", f_dma(xo.ap, xT_d[:, :, gi * 256:(gi + 1) * 256]), writes=[xo.b])
                    xsrc, xb = xo.ap, [xo.b]
                rmsnorm_group(xsrc, xb, 0, hn.ap, [hn.b], 256, ps_ss, lnv, rstd)
                for j in range(2):
                    i = gi * 2 + j
                    js = slice(j * 128, (j + 1) * 128)
                    ts_ = slice(i * 128, (i + 1) * 128)
                    P.op("pe", f_mm(acc_group(ps_kv.ap, [(hn.ap[:, ck, js], wkv.ap[:, ck, :]) for ck in range(8)])),
                         reads=[hn.b, wkv.b], writes=[ps_kv.b])
                    P.op("act", f_copy_act(VA.ap[:, i, :, 0:64],
                                           ps_kv.ap[:, 256:512].rearrange("p (m d) -> p m d", m=4)),
                         reads=[ps_kv.b], writes=[VA_b[i]])
                    P.op("act", f_copy_act(kf.ap, ps_kv.ap[:, 0:256]), reads=[ps_kv.b], writes=[kf.b])
                    headnorm(kf, 4, 1, sqt, sst, lt, rt)
                    rope(kf, Kr, 4, i, rtm)
                    P.op("pe", f_tr([(ps_kt.ap[:, pi * 128:(pi + 1) * 128], Kr.ap[:, pi * 128:(pi + 1) * 128], ident.ap)
                                     for pi in range(2)]),
                         reads=[Kr.b, ident.b], writes=[ps_kt.b])
                    P.op("act", f_copy_act(KT.ap[:, :, ts_], ps_kt.ap[:, 0:256].rearrange("p (a b) -> p a b", a=2)),
                         reads=[ps_kt.b], writes=[KT_b[i]])
                    if own:
                        for half in range(2):
                            P.op("pe", f_mm(acc_group(ps_q.ap, [(hn.ap[:, ck, js], wq.ap[:, ck, half * 512:(half + 1) * 512])
                                                               for ck in range(8)])),
                                 reads=[hn.b, wq.b], writes=[ps_q.b])
                            P.op("act", f_copy_act(qf.ap, ps_q.ap), reads=[ps_q.b], writes=[qf.b])
                            headnorm(qf, 8, 0, sqt, sst, lt, rt)
                            qdst = Tile(Qr.ap[:, half * 512:(half + 1) * 512])
                            qdst.b = Qr.b
                            rope(qf, qdst, 8, i, rtm)
                        P.op("pe", f_tr([(ps_qt.ap[:, b * 128:(b + 1) * 128], Qr.ap[:, b * 128:(b + 1) * 128], ident.ap)
                                         for b in range(8)]),
                             reads=[Qr.b, ident.b], writes=[ps_qt.b])
                        P.op("act", f_copy_act(QT.ap[:, :, ts_], ps_qt.ap.rearrange("p (a b) -> p a b", a=8)),
                             reads=[ps_qt.b], writes=[QT_b[i]])
            P.barrier()
            top[0] = m_a
            wo = alloc(BF16, 8, 1024)
            oT = alloc(BF16, 8, 512)
            Pb = [alloc(BF16, 1024) for _ in range(3)]
            tl = alloc(F32, 512)
            rr = alloc(F32, 512)
            nb = alloc(F32, 512)
            S = [Tile(PS[:, 0:1024]), Tile(PS[:, 1024:2048])]
            O = [[psbank(4), psbank(5)], [psbank(6), psbank(7)]]
            P.dma("pool", f_dma(wo.ap, wo_d, True), writes=[wo.b])
            iters = [(qt, b, kt) for qt in range(4) for b in range(8) for kt in range(32)]

            def emit_S(i):
                qt, b, kt = iters[i]
                qs = slice(qt * 512, (qt + 1) * 512)
                ks = slice(kt * 128, (kt + 1) * 128)
                pi = b // 4
                s2 = S[i % 2]
                P.op("pe", f_mm([(s2.ap[:, 0:512], KT.ap[0:64, pi, ks], QT.ap[0:64, b, qs], True, True),
                                 (s2.ap[:, 512:1024], KT.ap[64:128, pi, ks], QT.ap[64:128, b, qs], True, True)]),
                     reads=[KT_b[kt]] + QT_b[qt * 4:(qt + 1) * 4], writes=[s2.b])

            def emit_rest(i):
                qt, b, kt = iters[i]
                qs = slice(qt * 512, (qt + 1) * 512)
                pi = b // 4
                s2 = S[i % 2]
                p2 = Pb[i % 3]
                Oa, Ob = O[b % 2]
                P.op("act", f_act(p2.ap, s2.ap, AF.Exp, scale=0.125), reads=[s2.b], writes=[p2.b])
                P.op("pe", f_mm([(Oa.ap, VA.ap[:, kt, 2 * pi, :], p2.ap[:, 0:512], kt == 0, kt == 31),
                                 (Ob.ap, VA.ap[:, kt, 2 * pi + 1, :], p2.ap[:, 512:1024], kt == 0, kt == 31)]),
                     reads=[VA_b[kt], p2.b], writes=[Oa.b, Ob.b])
                if kt != 31:
                    return
                P.op("act", f_act(tl.ap[0:64, :], Oa.ap[64:128, :], AF.Ln), reads=[Oa.b], writes=[tl.b])
                P.op("act", f_act(rr.ap[0:64, :], tl.ap[0:64, :], AF.Exp, scale=-1.0), reads=[tl.b], writes=[rr.b])
                P.op("dve", f_tt(oT.ap[0:64, b, :], Oa.ap[0:64, :], rr.ap[0:64, :], ALU.mult),
                     reads=[Oa.b, rr.b], writes=[oT.b])
                P.op("act", f_act(tl.ap[64:128, :], Ob.ap[64:128, :], AF.Ln), reads=[Ob.b], writes=[tl.b])
                P.op("act", f_act(rr.ap[64:128, :], tl.ap[64:128, :], AF.Exp, scale=-1.0), reads=[tl.b], writes=[rr.b])
                P.op("act", f_copy_act(nb.ap[64:128, :], Ob.ap[0:64, :]), reads=[Ob.b], writes=[nb.b])
                P.op("dve", f_tt(oT.ap[64:128, b, :], nb.ap[64:128, :], rr.ap[64:128, :], ALU.mult),
                     reads=[nb.b, rr.b], writes=[oT.b])
                if b != 7:
                    return
                for oc in range(8):
                    ps = O[0][oc % 2]
                    P.op("pe", f_mm(acc_group(ps.ap, [(wo.ap[:, bb, oc * 128:(oc + 1) * 128], oT.ap[:, bb, :])
                                                      for bb in range(8)])),
                         reads=[wo.b, oT.b], writes=[ps.b])
                    P.op("dve", f_tt(hT.ap[:, oc, qs], hT.ap[:, oc, qs], ps.ap, ALU.add),
                         reads=[ps.b, hT_b[oc][qt]], writes=[hT_b[oc][qt]])

            emit_S(0)
            for i in range(len(iters)):
                if i + 1 < len(iters):
                    emit_S(i + 1)
                emit_rest(i)
            P.barrier()
            top[0] = m_phase

        def f_copy_act(out, in_):
            return lambda e: e.activation(out=out, in_=in_, func=AF.Copy)

        def mlp_phase(layer, gidx):
            m_phase = top[0]
            hn = alloc(BF16, 8, T)
            hn_b = [Buf() for _ in range(4)]
            lnv = alloc(F32, 512)
            rstd = alloc(F32, 512)
            h1 = [alloc(BF16, 4, T) for _ in range(2)]
            h1_b = [[Buf() for _ in range(4)] for _ in range(2)]
            win = [alloc(BF16, 8, 512) for _ in range(2)]
            wout = [alloc(BF16, 4, 1024) for _ in range(2)]
            rl = [alloc(F32, 512) for _ in range(2)]
            psI = [psbank(b) for b in range(4)]
            psO = [psbank(b) for b in range(4, 8)]
            for tt in range(4):
                tsl = slice(tt * 512, (tt + 1) * 512)
                rmsnorm_group(hT.ap[:, :, tsl], [hT_b[oc][tt] for oc in range(8)], gidx,
                              hn.ap[:, :, tsl], [hn_b[tt]], 512, psI[tt % 4], lnv, rstd)
            cnt = [0, 0]

            def load(G):
                P.dma("pool", f_dma(win[G % 2].ap, win_d[layer, G], True), writes=[win[G % 2].b])
                P.dma("pool", f_dma(wout[G % 2].ap, wout_d[layer, G], True), writes=[wout[G % 2].b])

            def stage_in(G):
                w = win[G % 2]
                for tt in range(4):
                    tsl = slice(tt * 512, (tt + 1) * 512)
                    for fb in range(4):
                        ps = psI[cnt[0] % 4]
                        r_ = rl[cnt[0] % 2]
                        cnt[0] += 1
                        P.op("pe", f_mm(acc_group(ps.ap, [(w.ap[:, ck, fb * 128:(fb + 1) * 128], hn.ap[:, ck, tsl])
                                                          for ck in range(8)])),
                             reads=[w.b, hn_b[tt]], writes=[ps.b])
                        P.op("act", f_act(r_.ap, ps.ap, AF.Relu), reads=[ps.b], writes=[r_.b])
                        P.op("act", f_act(h1[G % 2].ap[:, fb, tsl], r_.ap, AF.Square), reads=[r_.b],
                             writes=[h1_b[G % 2][tt]])

            def stage_out(G):
                w = wout[G % 2]
                for oc in range(8):
                    for tt in range(4):
                        tsl = slice(tt * 512, (tt + 1) * 512)
                        ps = psO[cnt[1] % 4]
                        cnt[1] += 1
                        P.op("pe", f_mm(acc_group(ps.ap, [(w.ap[:, fb, oc * 128:(oc + 1) * 128], h1[G % 2].ap[:, fb, tsl])
                                                          for fb in range(4)])),
                             reads=[w.b, h1_b[G % 2][tt]], writes=[ps.b])
                        P.op("dve", f_tt(hT.ap[:, oc, tsl], hT.ap[:, oc, tsl], ps.ap, ALU.add),
                             reads=[ps.b, hT_b[oc][tt]], writes=[hT_b[oc][tt]])

            load(0)
            load(1)
            stage_in(0)
            for G in range(8):
                if G + 1 < 8:
                    stage_in(G + 1)
                stage_out(G)
                if G + 2 < 8:
                    load(G + 2)
            P.barrier()
            top[0] = m_phase

        def gla_phase():
            m_phase = top[0]
            hn = alloc(BF16, 8, T)
            hn_b = [Buf() for _ in range(4)]
            zT = alloc(F32, T)
            wup = alloc(F32, 2, 512)
            wz = alloc(BF16, 8, 32)
            wg = alloc(BF16, 8, 1536)
            wo2 = alloc(BF16, 4, 1024)
            oX = alloc(BF16, 4, T)
            oX_b = [Buf() for _ in range(16)]
            Sf = alloc(F32, 2, 256)
            Sb = alloc(BF16, 2, 256)
            rx = alloc(F32, 2, 512)
            NP_ = 2
            lg = [alloc(F32, 256) for _ in range(NP_)]
            E1 = [alloc(F32, 2, 128) for _ in range(NP_)]
            E2 = [alloc(F32, 2, 128) for _ in range(NP_)]
            E3 = [alloc(F32, 256) for _ in range(NP_)]
            qeT = [alloc(BF16, 2, 128) for _ in range(NP_)]
            keT = [alloc(BF16, 2, 128) for _ in range(NP_)]
            kd = [alloc(BF16, 256) for _ in range(NP_)]
            vt = [alloc(BF16, 512) for _ in range(NP_)]
            aTm = [alloc(BF16, 2, 128) for _ in range(NP_)]
            ot = alloc(F32, 4, 128)
            osq = alloc(BF16, 4, 128)
            lnv = alloc(F32, 512)
            rstd = alloc(F32, 512)
            sg = alloc(F32, 512)
            on = alloc(F32, 4, 128)
            ps_qk, ps_kl, ps_v, ps_cr, ps_a, ps_o, ps_ckv, ps_r = [psbank(b) for b in range(8)]
            ps_klb = ps_kl.ap.bitcast(BF16)
            qkb = [alloc(BF16, 512) for _ in range(NP_)]

            P.dma("pool", f_dma(wz.ap, wz_d, True), writes=[wz.b])
            P.dma("sp", f_dma(wup.ap[0:33], wup_d), writes=[wup.b])
            for tt in range(4):
                tsl = slice(tt * 512, (tt + 1) * 512)
                rmsnorm_group(hT.ap[:, :, tsl], [hT_b[oc][tt] for oc in range(8)], 2,
                              hn.ap[:, :, tsl], [hn_b[tt]], 512, ps_qk, lnv, rstd)
            P.op("pool", lambda e: e.memset(zT.ap[32:33, :], 1.0), writes=[zT.b])
            for tt in range(4):
                tsl = slice(tt * 512, (tt + 1) * 512)
                P.op("pe", f_mm(acc_group(ps_v.ap[0:32, :], [(wz.ap[:, ck, :], hn.ap[:, ck, tsl]) for ck in range(8)])),
                     reads=[wz.b, hn_b[tt]], writes=[ps_v.b])
                P.op("act", f_copy_act(zT.ap[0:32, tsl], ps_v.ap[0:32, :]), reads=[ps_v.b], writes=[zT.b])

            QS = float(128.0 ** -0.5)

            def one_pass(g, d, order, final):
                MI = UT if d == 0 else LT
                MS = SL if d == 0 else SU
                lastc = 127 if d == 0 else 0
                for n_, i in enumerate(order):
                    p_ = n_ % NP_
                    ts_ = slice(i * 128, (i + 1) * 128)
                    hb = hn_b[i // 4]
                    P.op("pe", f_mm(acc_group(ps_qk.ap, [(hn.ap[:, ck, ts_], wg.ap[:, ck, 0:512]) for ck in range(8)])),
                         reads=[wg.b, hb], writes=[ps_qk.b])
                    P.op("act", f_copy_act(qkb[p_].ap, ps_qk.ap), reads=[ps_qk.b], writes=[qkb[p_].b])
                    P.op("pe", f_mm([(ps_kl.ap[:, 256:512], zT.ap[0:33, ts_], wup.ap[0:33, d, g * 256:(g + 1) * 256],
                                      True, True)]),
                         reads=[zT.b, wup.b], writes=[ps_kl.b])
                    P.op("pe", f_tr([(ps_klb[:, blk * 128:(blk + 1) * 128], qkb[p_].ap[:, blk * 128:(blk + 1) * 128], ident.ap)
                                     for blk in range(4)]),
                         reads=[qkb[p_].b, ident.b], writes=[ps_kl.b])
                    P.op("pe", f_mm(acc_group(ps_v.ap, [(hn.ap[:, ck, ts_], wg.ap[:, ck, 512:1024]) for ck in range(8)])),
                         reads=[wg.b, hb], writes=[ps_v.b])
                    P.op("act", f_act(lg[p_].ap, ps_kl.ap[:, 256:512], AF.Exp, scale=-1.0), reads=[ps_kl.b], writes=[lg[p_].b])
                    P.op("act", f_act(lg[p_].ap, lg[p_].ap, AF.Ln, bias=1.0), reads=[lg[p_].b], writes=[lg[p_].b])
                    P.op("act", f_copy_act(vt[p_].ap, ps_v.ap), reads=[ps_v.b], writes=[vt[p_].b])
                    P.op("pe", f_mm([(ps_cr.ap[:, h * 128:(h + 1) * 128], lg[p_].ap[:, h * 128:(h + 1) * 128], MI.ap, True, True)
                                     for h in range(2)]
                                    + [(ps_cr.ap[:, 256:512], MS.ap, lg[p_].ap, True, True)]),
                         reads=[lg[p_].b, MI.b, MS.b], writes=[ps_cr.b])
                    csv = ps_cr.ap[:, 0:256].rearrange("p (h c) -> p h c", h=2)
                    P.op("act", f_act(E1[p_].ap, csv, AF.Exp, scale=-1.0 / 16.0), reads=[ps_cr.b], writes=[E1[p_].b])
                    P.op("act", f_act(E2[p_].ap, csv, AF.Exp, scale=1.0 / 16.0), reads=[ps_cr.b], writes=[E2[p_].b])
                    P.op("act", f_act(E3[p_].ap, ps_cr.ap[:, 256:512], AF.Exp, scale=-1.0 / 16.0),
                         reads=[ps_cr.b], writes=[E3[p_].b])
                    qkv = ps_klb[:, 0:512].rearrange("p (b c) -> p b c", b=4)
                    P.op("dve", f_stt(qeT[p_].ap, qkv[:, 0:2, :], QS, E1[p_].ap, ALU.mult, ALU.mult),
                         reads=[ps_kl.b, E1[p_].b], writes=[qeT[p_].b])
                    P.op("dve", f_tt(keT[p_].ap, qkv[:, 2:4, :], E2[p_].ap, ALU.mult),
                         reads=[ps_kl.b, E2[p_].b], writes=[keT[p_].b])
                    P.op("dve", f_tt(kd[p_].ap, ps_qk.ap[:, 256:512], E3[p_].ap, ALU.mult),
                         reads=[ps_qk.b, E3[p_].b], writes=[kd[p_].b])
                    P.op("pe", f_mm([(ps_a.ap[:, h * 128:(h + 1) * 128], keT[p_].ap[:, h, :], qeT[p_].ap[:, h, :], True, True)
                                     for h in range(2)]),
                         reads=[keT[p_].b, qeT[p_].b], writes=[ps_a.b])
                    P.op("dve", f_tt(aTm[p_].ap, ps_a.ap[:, 0:256].rearrange("p (h c) -> p h c", h=2),
                                     MI.ap.unsqueeze(1).broadcast_to([128, 2, 128]), ALU.mult),
                         reads=[ps_a.b, MI.b], writes=[aTm[p_].b])
                    grp = []
                    for h in range(2):
                        for vb in range(2):
                            blk = h * 2 + vb
                            o_ = ps_o.ap[:, blk * 128:(blk + 1) * 128]
                            grp.append((o_, vt[p_].ap[:, h * 256 + vb * 128:h * 256 + (vb + 1) * 128], aTm[p_].ap[:, h, :],
                                        True, False))
                            grp.append((o_, Sb.ap[:, h, vb * 128:(vb + 1) * 128], qeT[p_].ap[:, h, :], False, True))
                    P.op("pe", f_mm(grp), reads=[vt[p_].b, aTm[p_].b, Sb.b, qeT[p_].b], writes=[ps_o.b])
                    P.op("pe", f_mm([(ps_ckv.ap[:, h * 256:(h + 1) * 256], kd[p_].ap[:, h * 128:(h + 1) * 128],
                                      vt[p_].ap[:, h * 256:(h + 1) * 256], True, True) for h in range(2)]),
                         reads=[kd[p_].b, vt[p_].b], writes=[ps_ckv.b])
                    for h in range(2):
                        P.op("dve", f_stt(Sf.ap[:, h, :], Sf.ap[:, h, :], E1[p_].ap[:, h, lastc:lastc + 1],
                                          ps_ckv.ap[:, h * 256:(h + 1) * 256], ALU.mult, ALU.add),
                             reads=[Sf.b, E1[p_].b, ps_ckv.b], writes=[Sf.b])
                    P.op("act", f_copy_act(Sb.ap, Sf.ap), reads=[Sf.b], writes=[Sb.b])
                    ov = ps_o.ap.rearrange("p (b c) -> p b c", b=4)
                    if not final:
                        P.op("act", f_copy_act(oX.ap[:, :, ts_], ov), reads=[ps_o.b], writes=[oX_b[i]])
                        continue
                    P.op("dve", f_tt(ot.ap, ov, oX.ap[:, :, ts_], ALU.add), reads=[ps_o.b, oX_b[i]], writes=[ot.b])
                    P.op("act", f_act(osq.ap, ot.ap, AF.Square), reads=[ot.b], writes=[osq.b])
                    grp = []
                    for h in range(2):
                        grp += acc_group(ps_a.ap[:, 256 + h * 128:256 + (h + 1) * 128],
                                         [(onesb.ap, osq.ap[:, h * 2 + vb, :]) for vb in range(2)])
                    P.op("pe", f_mm(grp), reads=[osq.b, onesb.b], writes=[ps_a.b])
                    P.op("act", f_act(lnv.ap[:, 0:256], ps_a.ap[:, 256:512], AF.Ln, scale=1.0 / 256.0, bias=EPS),
                         reads=[ps_a.b], writes=[lnv.b])
                    P.op("act", f_act(rstd.ap[:, 0:256], lnv.ap[:, 0:256], AF.Exp, scale=-0.5), reads=[lnv.b], writes=[rstd.b])
                    grp = []
                    for blk in range(4):
                        grp += acc_group(ps_r.ap[:, blk * 128:(blk + 1) * 128],
                                         [(wg.ap[:, ck, 1024 + blk * 128:1024 + (blk + 1) * 128], hn.ap[:, ck, ts_])
                                          for ck in range(8)])
                    P.op("pe", f_mm(grp), reads=[wg.b, hb], writes=[ps_r.b])
                    P.op("act", f_act(sg.ap, ps_r.ap, AF.Exp, scale=-1.0), reads=[ps_r.b], writes=[sg.b])
                    P.op("act", f_act(sg.ap, sg.ap, AF.Ln, bias=1.0), reads=[sg.b], writes=[sg.b])
                    P.op("act", f_act(sg.ap, sg.ap, AF.Exp, scale=-1.0), reads=[sg.b], writes=[sg.b])
                    for blk in range(4):
                        h, vb = blk // 2, blk % 2
                        P.op("dve", f_stt(on.ap[:, blk, :], ot.ap[:, blk, :], og.ap[:, vb:vb + 1],
                                          rstd.ap[:, h * 128:(h + 1) * 128], ALU.mult, ALU.mult),
                             reads=[ot.b, og.b, rstd.b], writes=[on.b])
                    P.op("dve", f_tt(sg.ap, ps_r.ap, sg.ap, ALU.mult), reads=[ps_r.b, sg.b], writes=[sg.b])
                    P.op("dve", f_tt(oX.ap[:, :, ts_], on.ap, sg.ap.rearrange("p (b c) -> p b c", b=4), ALU.mult),
                         reads=[on.b, sg.b], writes=[oX_b[i]])

            for g in range(2):
                P.dma("pool", f_dma(wg.ap, wg_d[g], True), writes=[wg.b])
                P.dma("pool", f_dma(wo2.ap, wo2_d[g], True), writes=[wo2.b])
                P.op("dve", lambda e: e.memset(Sf.ap.rearrange("p a b -> p (a b)"), 0.0), writes=[Sf.b])
                P.op("pool", lambda e: e.memset(Sb.ap.rearrange("p a b -> p (a b)"), 0.0), writes=[Sb.b])
                one_pass(g, 0, list(range(16)), False)
                ibb, obb = Buf(), Buf()
                sfl = Sf.ap.rearrange("p a b -> p (a b)")
                P.dma("pool", f_dma(ib_t[g].ap(), sfl), reads=[Sf.b], writes=[ibb])
                P.custom("pool", (lambda g=g: (lambda e: e.collective_compute(
                    "AllGather", ALU.bypass, replica_groups=[[0, 1], [2, 3], [4, 5], [6, 7]],
                    ins=[ib_t[g].ap().opt()], outs=[ob_t[g].ap().opt()])))(), "cc%d" % g, reads=[ibb], writes=[obb])
                P.dma("pool", f_dma(rx.ap, ob_t[g].ap().rearrange("(r p) c -> p r c", p=128)), reads=[obb], writes=[rx.b])
                P.op("dve", f_ts(sfl, rx.ap[:, 0, :], mex.ap[:, 0:1], ALU.mult), reads=[rx.b, mex.b], writes=[Sf.b])
                P.op("dve", f_stt(sfl, rx.ap[:, 1, :], mex.ap[:, 1:2], sfl, ALU.mult, ALU.add),
                     reads=[rx.b, mex.b, Sf.b], writes=[Sf.b])
                P.op("act", f_copy_act(Sb.ap, Sf.ap), reads=[Sf.b], writes=[Sb.b])
                one_pass(g, 1, list(range(15, -1, -1)), True)
                k = 0
                for oc in range(8):
                    for tt in range(4):
                        tsl = slice(tt * 512, (tt + 1) * 512)
                        ps = [ps_qk, ps_kl, ps_v, ps_cr][k % 4]
                        k += 1
                        P.op("pe", f_mm(acc_group(ps.ap, [(wo2.ap[:, blk, oc * 128:(oc + 1) * 128], oX.ap[:, blk, tsl])
                                                          for blk in range(4)])),
                             reads=[wo2.b] + oX_b[tt * 4:(tt + 1) * 4], writes=[ps.b])
                        P.op("dve", f_tt(hT.ap[:, oc, tsl], hT.ap[:, oc, tsl], ps.ap, ALU.add),
                             reads=[ps.b, hT_b[oc][tt]], writes=[hT_b[oc][tt]])
                P.barrier()
            top[0] = m_phase

        def output_phase(do_norm):
            m_phase = top[0]
            lnv = alloc(F32, 512)
            rstd = alloc(F32, 512)
            sq = alloc(BF16, 8, 512)
            yo = [alloc(F32, 8, 512) for _ in range(2)]
            toks = []
            for tt in range(4):
                tsl = slice(tt * 512, (tt + 1) * 512)
                xb = [hT_b[oc][tt] for oc in range(8)]
                if do_norm:
                    y = yo[tt % 2]
                    import os
                    mode = os.environ.get("KDBG_OUT", "")
                    if mode == "B":
                        P.op("dve", f_copy(y.ap, hT.ap[:, :, tsl]), reads=xb, writes=[y.b])
                    else:
                        rmsnorm_group(hT.ap[:, :, tsl], xb, 4, y.ap, [y.b], 512, psbank(tt % 4), lnv, rstd, sq=sq.ap, sqbufs=[sq.b])
                    if mode == "A":
                        toks.append(P.dma("sp", f_dma(y_d[:, :, tsl], hT.ap[:, :, tsl]), reads=xb + [y.b]))
                    else:
                        toks.append(P.dma("sp", f_dma(y_d[:, :, tsl], y.ap), reads=[y.b]))
                else:
                    toks.append(P.dma("sp", f_dma(y_d[:, :, tsl], hT.ap[:, :, tsl]), reads=xb))
            top[0] = m_phase

        phases = [("attn", attention_phase), ("mlp1", lambda: mlp_phase(0, 1)), ("gla", gla_phase),
                  ("mlp2", lambda: mlp_phase(1, 3))]
        done = False
        if only is not None:
            for name, fn in phases:
                if name in only:
                    fn()
            output_phase("final" in only)
            done = True
        else:
            for name, fn in phases:
                fn()
                if stop_after == name:
                    output_phase(False)
                    done = True
                    break
        if not done:
            output_phase(True)
        P.barrier()

        import contextlib
        with contextlib.ExitStack() as es:
            sems = {}
            for k in P.count.keys():
                sems[k] = es.enter_context(nc.semaphore("s_" + k))
            block = es.enter_context(nc.Block())

            def replay(name, e):
                for item in P.lists[name]:
                    if item[0] == "w":
                        e.wait_ge(sems[item[1]], item[2])
                    else:
                        inst = item[1](e)
                        if item[0] == "c":
                            inst.then_inc(sems[item[2]])
                        else:
                            inst.then_inc(sems[item[2]], item[3])

            @block.tensor
            def _(e):
                replay("pe", e)

            @block.scalar
            def _(e):
                replay("act", e)

            @block.vector
            def _(e):
                replay("dve", e)

            @block.gpsimd
            def _(e):
                replay("pool", e)

            @block.sync
            def _(e):
                replay("sp", e)
    return nc


def _fm(w):
    K, N = w.shape
    return np.ascontiguousarray(w.reshape(K // 128, 128, N).transpose(1, 0, 2))


def _prep(inputs):
    f = lambda a: np.asarray(a, dtype=np.float32)
    x = f(inputs["x"])
    norm_mix, norm_mlp, final_norm = f(inputs["norm_mix"]), f(inputs["norm_mlp"]), f(inputs["final_norm"])
    wqkv = f(inputs["attn_w_qkv"])[0]
    qn, kn = f(inputs["attn_q_norm"])[0], f(inputs["attn_k_norm"])[0]
    wo = f(inputs["attn_w_o"])[0]
    gw = f(inputs["gla_w_in"])[0]
    gup, gb = f(inputs["gla_w_gate_up"])[0], f(inputs["gla_b_gate"])[0]
    gon, gwo = f(inputs["gla_out_norm"])[0], f(inputs["gla_w_o"])[0]
    mwi, mwo = f(inputs["mlp_w_in"]), f(inputs["mlp_w_out"])

    gains = np.stack([norm_mix[0], norm_mlp[0], norm_mix[1], norm_mlp[1], final_norm], 0)
    gains_l = np.ascontiguousarray(gains.reshape(5, 8, 128).transpose(2, 0, 1)).reshape(128, 40)
    qkg_l = np.ascontiguousarray(np.broadcast_to(np.concatenate([qn, kn])[None, :], (128, 128)))
    qcols = np.concatenate([np.arange(h * 64, (h + 1) * 64) for h in HO])
    wq_l = _fm(wqkv[:, :1024][:, qcols])
    wkv_l = _fm(wqkv[:, 1024:1536])
    wo_l = _fm(wo[qcols, :])
    win_l = np.stack([np.stack([_fm(mwi[l][:, G * 512:(G + 1) * 512]) for G in range(8)]) for l in range(2)])
    wout_l = np.stack([np.stack([_fm(mwo[l][G * 512:(G + 1) * 512, :]) for G in range(8)]) for l in range(2)])
    wg_l = []
    for g in range(2):
        cols = np.concatenate([np.arange(g * 256, (g + 1) * 256), 512 + np.arange(g * 256, (g + 1) * 256),
                               1024 + np.arange(g * 512, (g + 1) * 512), 2048 + np.arange(g * 512, (g + 1) * 512)])
        wg_l.append(_fm(gw[:, cols]))
    wg_l = np.stack(wg_l)
    og_l = np.ascontiguousarray(gon.reshape(2, 128).T)
    wo2_l = np.stack([_fm(gwo[g * 512:(g + 1) * 512, :]) for g in range(2)])

    maps = []
    idxs = []
    for c in range(NCORES):
        b, s = c // 2, c % 2
        if s == 0:
            own = np.arange(0, T)
            other = np.arange(T, TA)
            dirs = (0, 1)
        else:
            own = np.arange(TA - 1, T - 1, -1)
            other = np.arange(0, T)
            dirs = (1, 0)
        idx = np.concatenate([own, other])
        idxs.append(own)
        xT_l = np.ascontiguousarray(x[b][idx].T.reshape(8, 128, TA).transpose(1, 0, 2))
        pr = (idx // 64).astype(np.float32)
        pc = (idx % 64).astype(np.float32)
        pos_l = np.ascontiguousarray(np.stack([pr.reshape(32, 128).T, pc.reshape(32, 128).T], -1)).reshape(128, 64)
        zc = np.concatenate([3072 + dirs[0] * 16 + np.arange(16), 3072 + dirs[1] * 16 + np.arange(16)])
        wz_l = _fm(gw[:, zc])
        wup_l = np.zeros((33, 2, 512), np.float32)
        wup_l[0:16, 0] = gup[dirs[0]]
        wup_l[16:32, 1] = gup[dirs[1]]
        wup_l[32, 0] = gb[dirs[0]]
        wup_l[32, 1] = gb[dirs[1]]
        mex_l = np.zeros((128, 2), np.float32)
        mex_l[:, 1 - s] = 1.0
        maps.append({
            "xT": xT_l, "pos": pos_l, "gains": gains_l, "qkg": qkg_l, "wq": wq_l, "wkv": wkv_l, "wo": wo_l,
            "win": win_l, "wout": wout_l, "wg": wg_l, "wz": wz_l, "wup": wup_l, "og": og_l, "wo2": wo2_l,
            "mex": mex_l,
        })
    return maps, idxs


_NC_CACHE = {}


def run(inputs, stop_after=None, trace=False, only=None):
    maps, idxs = _prep(inputs)
    key = (stop_after, only)
    if key not in _NC_CACHE:
        _NC_CACHE[key] = build_program(stop_after, only)
    nc = _NC_CACHE[key]
    res = run_bass_kernel_spmd(nc, maps, core_ids=list(range(NCORES)), trace=trace)
    out = np.empty((4, TA, D), np.float32)
    for c in range(NCORES):
        y = np.asarray(res.results[c]["y"])
        out[c // 2, idxs[c], :] = y.transpose(2, 1, 0).reshape(T, D)
    return out, res


def kernel(**inputs):
    out, _ = run(inputs)
    return out
```

```python
import math
from functools import reduce

import numpy as np
import concourse.bass as bass
import concourse.mybir as mybir
from concourse.bass_utils import run_bass_kernel_spmd

F32 = mybir.dt.float32
BF16 = mybir.dt.bfloat16
I32 = mybir.dt.int32
AF = mybir.ActivationFunctionType
ALU = mybir.AluOpType
AX = mybir.AxisListType

NCORES = 8
D = 1024
T = 2048
TA = 4096
EPS = 1e-6
HO = [0, 4, 1, 5, 2, 6, 3, 7, 8, 12, 9, 13, 10, 14, 11, 15]
ENGS = ("pe", "act", "dve", "pool", "sp")
NDSEM = 12
SBUF_BYTES = 212800


class Buf:
    __slots__ = ("w", "r")

    def __init__(self):
        self.w = None
        self.r = {}


class Tile:
    def __init__(self, ap):
        self.ap = ap
        self.b = Buf()


class Prog:
    def __init__(self):
        self.lists = {e: [] for e in ENGS}
        self.count = {}
        self.waited = {e: {} for e in ENGS}
        self.dma_i = {"sp": 0, "pool": 0}

    def _deps(self, eng, reads, writes):
        toks = {}

        def add(t):
            if t is None:
                return
            k, v = t
            if eng == "pe" and k == "pe":
                return
            if toks.get(k, 0) < v:
                toks[k] = v

        for b in reads:
            add(b.w)
        for b in writes:
            add(b.w)
            for k, v in b.r.items():
                add((k, v))
        for k, v in toks.items():
            if self.waited[eng].get(k, 0) < v:
                self.lists[eng].append(("w", k, v))
                self.waited[eng][k] = v

    def _mark(self, tok, reads, writes):
        k, v = tok
        for b in reads:
            if b.r.get(k, 0) < v:
                b.r[k] = v
        for b in writes:
            b.w = tok
            b.r = {}

    def op(self, eng, fn, reads=(), writes=()):
        self._deps(eng, reads, writes)
        self.count[eng] = self.count.get(eng, 0) + 1
        tok = (eng, self.count[eng])
        self.lists[eng].append(("o", fn, eng, 1))
        self._mark(tok, reads, writes)
        return tok

    def dma(self, q, fn, reads=(), writes=()):
        self._deps(q, reads, writes)
        i = self.dma_i[q]
        self.dma_i[q] = i + 1
        key = "d%s%d" % (q, i % NDSEM)
        self.count[key] = self.count.get(key, 0) + 16
        tok = (key, self.count[key])
        self.lists[q].append(("o", fn, key, 16))
        self._mark(tok, reads, writes)
        return tok

    def custom(self, eng, fn, key, reads=(), writes=()):
        self._deps(eng, reads, writes)
        self.count[key] = self.count.get(key, 0) + 1
        tok = (key, self.count[key])
        self.lists[eng].append(("c", fn, key, 1))
        self._mark(tok, reads, writes)
        return tok

    def barrier(self):
        for e in ENGS:
            for k, v in self.count.items():
                if k == e and e != "pe":
                    pass
                if self.waited[e].get(k, 0) < v:
                    self.lists[e].append(("w", k, v))
                    self.waited[e][k] = v


def _prod(s):
    return reduce(lambda a, b: a * b, s, 1)


def build_program(stop_after=None, only=None):
    nc = bass.Bass("TRN2", target_bir_lowering=False)

    def din(name, shape):
        return nc.dram_tensor(name, list(shape), F32, kind="ExternalInput").ap()

    xT_d = din("xT", [128, 8, TA])
    pos_d = din("pos", [128, 64])
    gains_d = din("gains", [128, 40])
    qkg_d = din("qkg", [128, 128])
    wq_d = din("wq", [128, 8, 1024])
    wkv_d = din("wkv", [128, 8, 512])
    wo_d = din("wo", [128, 8, 1024])
    win_d = din("win", [2, 8, 128, 8, 512])
    wout_d = din("wout", [2, 8, 128, 4, 1024])
    wg_d = din("wg", [2, 128, 8, 1536])
    wz_d = din("wz", [128, 8, 32])
    wup_d = din("wup", [33, 2, 512])
    og_d = din("og", [128, 2])
    wo2_d = din("wo2", [2, 128, 4, 1024])
    mex_d = din("mex", [128, 2])
    y_d = nc.dram_tensor("y", [128, 8, T], F32, kind="ExternalOutput").ap()
    ib_t = [nc.dram_tensor("ib%d" % g, [128, 512], F32) for g in range(2)]
    ob_t = [nc.dram_tensor("ob%d" % g, [256, 512], F32) for g in range(2)]

    P = Prog()

    with (
        nc.sbuf_tensor("arena", [128, SBUF_BYTES // 4], F32) as A,
        nc.psum_tensor("psum", [128, 4096], F32) as PS,
    ):
        top = [0]

        def alloc(dtype, *shape):
            esz = 4 if dtype in (F32, I32) else 2
            nbytes = (_prod(shape) * esz + 63) // 64 * 64
            off = top[0]
            top[0] += nbytes
            assert top[0] <= SBUF_BYTES, ("SBUF overflow", top[0])
            ap = A[:, off // 4:(off + nbytes) // 4]
            if dtype != F32:
                ap = ap.bitcast(dtype)
            ap = ap[:, 0:_prod(shape)]
            if len(shape) > 1:
                names = ["a%d" % i for i in range(len(shape))]
                ap = ap.rearrange("p (%s) -> p %s" % (" ".join(names), " ".join(names)),
                                  **{n: s for n, s in zip(names[:-1], shape[:-1])})
            return Tile(ap)

        def psbank(b, dtype=F32):
            ap = PS[:, b * 512:(b + 1) * 512]
            if dtype != F32:
                ap = ap.bitcast(dtype)
            return Tile(ap)

        def f_act(out, in_, func, scale=1.0, bias=None):
            if bias is None:
                return lambda e: e.activation(out=out, in_=in_, func=func, scale=scale)
            return lambda e: e.activation(out=out, in_=in_, func=func, scale=scale, bias=bias)

        def f_tt(out, in0, in1, op):
            return lambda e: e.tensor_tensor(out=out, in0=in0, in1=in1, op=op)

        def f_stt(out, in0, scalar, in1, op0, op1):
            return lambda e: e.scalar_tensor_tensor(out=out, in0=in0, scalar=scalar, in1=in1, op0=op0, op1=op1)

        def f_ts(out, in0, s1, op0, s2=None, op1=None):
            if op1 is None:
                return lambda e: e.tensor_scalar(out=out, in0=in0, scalar1=s1, scalar2=None, op0=op0)
            return lambda e: e.tensor_scalar(out=out, in0=in0, scalar1=s1, scalar2=s2, op0=op0, op1=op1)

        def f_copy(out, in_):
            return lambda e: e.tensor_copy(out=out, in_=in_)

        def f_mm(group):
            def fn(e):
                inst = None
                for (o, l, r, st, sp) in group:
                    inst = e.matmul(o, l, r, start=st, stop=sp)
                return inst
            return fn

        def f_tr(group):
            def fn(e):
                inst = None
                for (o, i, idn) in group:
                    inst = e.transpose(o, i, idn)
                return inst
            return fn

        def f_dma(out, in_, cast=False):
            if cast:
                return lambda e: e.dma_start(out=out, in_=in_, max_dma_last_dim=4096)
            return lambda e: e.dma_start(out=out, in_=in_)

        def acc_group(out, pairs):
            n = len(pairs)
            return [(out, l, r, i == 0, i == n - 1) for i, (l, r) in enumerate(pairs)]

        hT = alloc(F32, 8, T)
        hT_b = [[Buf() for _ in range(4)] for _ in range(8)]
        hT_all = [b for row in hT_b for b in row]
        onesf = alloc(F32, 128)
        UT = alloc(F32, 128)
        LT = alloc(F32, 128)
        SL = alloc(F32, 128)
        SU = alloc(F32, 128)
        IDF = alloc(F32, 128)
        ident = alloc(BF16, 128)
        onesb = alloc(BF16, 128)
        gains = alloc(F32, 5, 8)
        qkg = alloc(F32, 2, 64)
        og = alloc(F32, 2)
        mex = alloc(F32, 2)
        pos = alloc(F32, 64)
        tab = alloc(F32, 2, 32, 2, 16)
        persist_top = top[0]

        for tt in range(4):
            P.dma("sp", f_dma(hT.ap[:, :, tt * 512:(tt + 1) * 512], xT_d[:, :, tt * 512:(tt + 1) * 512]),
                  writes=[hT_b[oc][tt] for oc in range(8)])
        P.dma("sp", f_dma(gains.ap, gains_d.rearrange("p (a b) -> p a b", a=5)), writes=[gains.b])
        P.dma("sp", f_dma(qkg.ap, qkg_d.rearrange("p (a b) -> p a b", a=2)), writes=[qkg.b])
        P.dma("sp", f_dma(og.ap, og_d), writes=[og.b])
        P.dma("sp", f_dma(mex.ap, mex_d), writes=[mex.b])
        P.dma("sp", f_dma(pos.ap, pos_d), writes=[pos.b])

        P.op("pool", lambda e: e.memset(onesf.ap, 1.0), writes=[onesf.b])
        P.op("pool", lambda e: e.memset(onesb.ap, 1.0), writes=[onesb.b])

        def mk_mask(dst, cm, step, cmp):
            P.op("pool", lambda e: e.affine_select(out=dst.ap, in_=onesf.ap, pattern=[[step, 128]],
                                                   compare_op=cmp, fill=0.0, base=0, channel_multiplier=cm),
                 reads=[onesf.b], writes=[dst.b])

        mk_mask(UT, -1, 1, ALU.is_ge)
        mk_mask(LT, 1, -1, ALU.is_ge)
        mk_mask(SL, 1, -1, ALU.is_gt)
        mk_mask(SU, -1, 1, ALU.is_gt)
        mk_mask(IDF, 1, -1, ALU.is_equal)
        P.op("dve", f_copy(ident.ap, IDF.ap), reads=[IDF.b], writes=[ident.b])

        m0 = top[0]
        invf = alloc(F32, 16)
        ang = alloc(F32, 64, 16)
        u = alloc(F32, 2, 1024)
        ki = alloc(I32, 2048)
        kf_ = alloc(F32, 2048)
        fr = alloc(F32, 2048)
        ng = alloc(F32, 2048)
        for f in range(16):
            val = float(10000.0 ** (-(2.0 * f) / 32.0))
            P.op("pool", (lambda f=f, val=val: (lambda e: e.memset(invf.ap[:, f:f + 1], val)))(), writes=[invf.b])
        P.op("dve", f_tt(ang.ap, pos.ap.unsqueeze(2).broadcast_to([128, 64, 16]),
                         invf.ap.unsqueeze(1).broadcast_to([128, 64, 16]), ALU.mult),
             reads=[pos.b, invf.b], writes=[ang.b])
        angf = ang.ap.rearrange("p a b -> p (a b)")
        inv2pi = float(1.0 / (2.0 * math.pi))
        P.op("dve", f_ts(u.ap[:, 0, :], angf, inv2pi, ALU.mult, 0.5, ALU.add), reads=[ang.b], writes=[u.b])
        P.op("dve", f_ts(u.ap[:, 1, :], angf, inv2pi, ALU.mult, 0.75, ALU.add), reads=[ang.b], writes=[u.b])
        uf = u.ap.rearrange("p a b -> p (a b)")
        P.op("dve", f_copy(ki.ap, uf), reads=[u.b], writes=[ki.b])
        P.op("dve", f_copy(kf_.ap, ki.ap), reads=[ki.b], writes=[kf_.b])
        P.op("dve", f_tt(fr.ap, uf, kf_.ap, ALU.subtract), reads=[u.b, kf_.b], writes=[fr.b])
        P.op("dve", lambda e: e.tensor_single_scalar(out=ng.ap, in_=fr.ap, scalar=0.0, op=ALU.is_lt),
             reads=[fr.b], writes=[ng.b])
        P.op("dve", f_tt(fr.ap, fr.ap, ng.ap, ALU.add), reads=[fr.b, ng.b], writes=[fr.b])
        P.op("dve", f_ts(fr.ap, fr.ap, float(2.0 * math.pi), ALU.mult, float(-math.pi), ALU.add),
             reads=[fr.b], writes=[fr.b])
        P.op("dve", f_ts(fr.ap, fr.ap, -3.141592, ALU.max, 3.141592, ALU.min), reads=[fr.b], writes=[fr.b])
        P.op("act", f_act(tab.ap.rearrange("p a b c d -> p (a b c d)"), fr.ap, AF.Sin), reads=[fr.b], writes=[tab.b])
        P.barrier()
        top[0] = m0

        def rmsnorm_group(xsrc, xbufs, gidx, dst, dstbufs, n, ps_ss, lnv, rstd, sq=None, sqbufs=None):
            if sq is None:
                sq, sqbufs = dst, dstbufs
            P.op("act", f_act(sq, xsrc, AF.Square), reads=xbufs, writes=sqbufs)
            P.op("pe", f_mm(acc_group(ps_ss.ap[:, 0:n], [(onesb.ap, sq[:, ck, :]) for ck in range(8)])),
                 reads=sqbufs + [onesb.b], writes=[ps_ss.b])
            P.op("act", f_act(lnv.ap[:, 0:n], ps_ss.ap[:, 0:n], AF.Ln, scale=1.0 / D, bias=EPS),
                 reads=[ps_ss.b], writes=[lnv.b])
            P.op("act", f_act(rstd.ap[:, 0:n], lnv.ap[:, 0:n], AF.Exp, scale=-0.5), reads=[lnv.b], writes=[rstd.b])
            fns = [f_stt(dst[:, ck, :], xsrc[:, ck, :], gains.ap[:, gidx, ck:ck + 1], rstd.ap[:, 0:n],
                         ALU.mult, ALU.mult) for ck in range(8)]

            def multi(e, fns=fns):
                inst = None
                for f in fns:
                    inst = f(e)
                return inst
            P.op("dve", multi, reads=xbufs + [rstd.b, gains.b], writes=dstbufs)

        def rope(src, dst, H, i, tmps):
            sv = src.ap.rearrange("p (h r x f) -> p h r x f", h=H, r=2, x=2)
            dv = dst.ap.rearrange("p (h r x f) -> p h r x f", h=H, r=2, x=2)
            x1, x2 = sv[:, :, :, 0, :], sv[:, :, :, 1, :]
            sn = tab.ap[:, 0, i, :, :].unsqueeze(1).broadcast_to([128, H, 2, 16])
            cs = tab.ap[:, 1, i, :, :].unsqueeze(1).broadcast_to([128, H, 2, 16])
            t1, t2, t3, t4 = [t.ap[:, 0:H * 32].rearrange("p (h r f) -> p h r f", h=H, r=2) for t in tmps]
            tb = [t.b for t in tmps]
            P.op("dve", f_tt(t1, x1, cs, ALU.mult), reads=[src.b, tab.b], writes=[tb[0]])
            P.op("dve", f_tt(t2, x2, sn, ALU.mult), reads=[src.b, tab.b], writes=[tb[1]])
            P.op("dve", f_tt(t3, x2, cs, ALU.mult), reads=[src.b, tab.b], writes=[tb[2]])
            P.op("dve", f_tt(t4, x1, sn, ALU.mult), reads=[src.b, tab.b], writes=[tb[3]])
            P.op("dve", f_tt(dv[:, :, :, 0, :], t1, t2, ALU.subtract), reads=[tb[0], tb[1]], writes=[dst.b])
            P.op("dve", f_tt(dv[:, :, :, 1, :], t3, t4, ALU.add), reads=[tb[2], tb[3]], writes=[dst.b])

        def norm_rope(srcf, H, gi, dst, i, sqt, sst, lt, rt, tmps, ro):
            v = srcf.ap.rearrange("p (h d) -> p h d", h=H)
            sqv = sqt.ap[:, 0:H * 64].rearrange("p (h d) -> p h d", h=H)
            P.op("dve", f_tt(sqt.ap[:, 0:H * 64], srcf.ap, srcf.ap, ALU.mult), reads=[srcf.b], writes=[sqt.b])
            P.op("dve", lambda e: e.tensor_reduce(out=sst.ap[:, 0:H], in_=sqv, axis=AX.X, op=ALU.add),
                 reads=[sqt.b], writes=[sst.b])
            P.op("act", f_act(lt.ap[:, 0:H], sst.ap[:, 0:H], AF.Ln, scale=1.0 / 64.0, bias=EPS),
                 reads=[sst.b], writes=[lt.b])
            P.op("act", f_act(rt.ap[:, 0:H], lt.ap[:, 0:H], AF.Exp, scale=-0.5), reads=[lt.b], writes=[rt.b])
            P.op("dve", f_tt(v, v, qkg.ap[:, gi, :].unsqueeze(1).broadcast_to([128, H, 64]), ALU.mult),
                 reads=[srcf.b, qkg.b], writes=[srcf.b])
            rot = Tile(ro.ap[:, 0:H * 64])
            rot.b = ro.b
            rope(srcf, rot, H, i, tmps)
            P.op("dve", f_tt(dst.ap.rearrange("p (h d) -> p h d", h=H), rot.ap.rearrange("p (h d) -> p h d", h=H),
                             rt.ap[:, 0:H].unsqueeze(2).broadcast_to([128, H, 64]), ALU.mult),
                 reads=[ro.b, rt.b], writes=[dst.b])

        def headnorm(srcf, H, gi, sqt, sst, lt, rt):
            v = srcf.ap.rearrange("p (h d) -> p h d", h=H)
            sqv = sqt.ap[:, 0:H * 64].rearrange("p (h d) -> p h d", h=H)
            P.op("dve", f_tt(sqt.ap[:, 0:H * 64], srcf.ap, srcf.ap, ALU.mult), reads=[srcf.b], writes=[sqt.b])
            P.op("dve", lambda e: e.tensor_reduce(out=sst.ap[:, 0:H], in_=sqv, axis=AX.X, op=ALU.add),
                 reads=[sqt.b], writes=[sst.b])
            P.op("act", f_act(lt.ap[:, 0:H], sst.ap[:, 0:H], AF.Ln, scale=1.0 / 64.0, bias=EPS),
                 reads=[sst.b], writes=[lt.b])
            P.op("act", f_act(rt.ap[:, 0:H], lt.ap[:, 0:H], AF.Exp, scale=-0.5), reads=[lt.b], writes=[rt.b])
            P.op("dve", f_tt(v, v, rt.ap[:, 0:H].unsqueeze(2).broadcast_to([128, H, 64]), ALU.mult),
                 reads=[srcf.b, rt.b], writes=[srcf.b])
            P.op("dve", f_tt(v, v, qkg.ap[:, gi, :].unsqueeze(1).broadcast_to([128, H, 64]), ALU.mult),
                 reads=[srcf.b, qkg.b], writes=[srcf.b])

        def attention_phase():
            m_phase = top[0]
            QT = alloc(BF16, 8, T)
            KT = alloc(BF16, 2, TA)
            VA = alloc(BF16, 32, 4, 128)
            QT_b = [Buf() for _ in range(16)]
            KT_b = [Buf() for _ in range(32)]
            VA_b = [Buf() for _ in range(32)]
            m_a = top[0]
            wq = alloc(BF16, 8, 1024)
            wkv = alloc(BF16, 8, 512)
            xo = alloc(F32, 8, 256)
            xof = xo.ap.rearrange("p a b -> p (a b)")
            ksq = Tile(xof[:, 0:256]); ksst = Tile(xof[:, 256:264]); klt = Tile(xof[:, 264:272]); krt = Tile(xof[:, 272:280])
            krtm = [Tile(xof[:, 512 + k * 256:768 + k * 256]) for k in range(4)]
            kpriv = [ksq.b, ksst.b, klt.b, krt.b] + [t.b for t in krtm]
            hn = alloc(BF16, 8, 256)
            lnv = alloc(F32, 256)
            rstd = alloc(F32, 256)
            kf = alloc(F32, 256)
            qf = alloc(F32, 512)
            sqt = alloc(F32, 512)
            sst = alloc(F32, 8)
            lt = alloc(F32, 8)
            rt = alloc(F32, 8)
            rtm = [alloc(F32, 256) for _ in range(4)]
            Kr = alloc(BF16, 256)
            Qr = alloc(BF16, 1024)
            ro = sqt
            ps_ss, ps_kv, ps_q, ps_kt, ps_qt = psbank(0), psbank(1), psbank(2), psbank(3, BF16), psbank(4, BF16)

            P.dma("pool", f_dma(wq.ap, wq_d, True), writes=[wq.b])
            P.dma("pool", f_dma(wkv.ap, wkv_d, True), writes=[wkv.b])
            P.op("pool", lambda e: e.memset(VA.ap.rearrange("p a b c -> p (a b c)"), 1.0), writes=VA_b)

            for gi in range(16):
                own = gi < 8
                if own:
                    xsrc = hT.ap[:, :, gi * 256:(gi + 1) * 256]
                    xb = [hT_b[oc][gi // 2] for oc in range(8)]
                else:
                    P.dma("sp", f_dma(xo.ap, xT_d[:, :, gi * 256:(gi + 1) * 256]), writes=[xo.b] + kpriv)
                    xsrc, xb = xo.ap, [xo.b]
                rmsnorm_group(xsrc, xb, 0, hn.ap, [hn.b], 256, ps_ss, lnv, rstd)
                for j in range(2):
                    i = gi * 2 + j
                    js = slice(j * 128, (j + 1) * 128)
                    ts_ = slice(i * 128, (i + 1) * 128)
                    P.op("pe", f_mm(acc_group(ps_kv.ap, [(hn.ap[:, ck, js], wkv.ap[:, ck, :]) for ck in range(8)])),
                         reads=[hn.b, wkv.b], writes=[ps_kv.b])
                    P.op("act", f_copy_act(VA.ap[:, i, :, 0:64],
                                           ps_kv.ap[:, 256:512].rearrange("p (m d) -> p m d", m=4)),
                         reads=[ps_kv.b], writes=[VA_b[i]])
                    P.op("act", f_copy_act(kf.ap, ps_kv.ap[:, 0:256]), reads=[ps_kv.b], writes=[kf.b])
                    if own:
                        norm_rope(kf, 4, 1, Kr, i, ksq, ksst, klt, krt, krtm, ksq)
                    else:
                        norm_rope(kf, 4, 1, Kr, i, sqt, sst, lt, rt, rtm, ro)
                    P.op("pe", f_tr([(ps_kt.ap[:, pi * 128:(pi + 1) * 128], Kr.ap[:, pi * 128:(pi + 1) * 128], ident.ap)
                                     for pi in range(2)]),
                         reads=[Kr.b, ident.b], writes=[ps_kt.b])
                    P.op("act", f_copy_act(KT.ap[:, :, ts_], ps_kt.ap[:, 0:256].rearrange("p (a b) -> p a b", a=2)),
                         reads=[ps_kt.b], writes=[KT_b[i]])
                    if own:
                        for half in range(2):
                            P.op("pe", f_mm(acc_group(ps_q.ap, [(hn.ap[:, ck, js], wq.ap[:, ck, half * 512:(half + 1) * 512])
                                                               for ck in range(8)])),
                                 reads=[hn.b, wq.b], writes=[ps_q.b])
                            P.op("act", f_copy_act(qf.ap, ps_q.ap), reads=[ps_q.b], writes=[qf.b])
                            qdst = Tile(Qr.ap[:, half * 512:(half + 1) * 512])
                            qdst.b = Qr.b
                            norm_rope(qf, 8, 0, qdst, i, sqt, sst, lt, rt, rtm, ro)
                        P.op("pe", f_tr([(ps_qt.ap[:, b * 128:(b + 1) * 128], Qr.ap[:, b * 128:(b + 1) * 128], ident.ap)
                                         for b in range(8)]),
                             reads=[Qr.b, ident.b], writes=[ps_qt.b])
                        P.op("act", f_copy_act(QT.ap[:, :, ts_], ps_qt.ap.rearrange("p (a b) -> p a b", a=8)),
                             reads=[ps_qt.b], writes=[QT_b[i]])
            P.barrier()
            top[0] = m_a
            wo = alloc(BF16, 8, 1024)
            oT = alloc(BF16, 8, 512)
            Pb = [alloc(BF16, 1024) for _ in range(3)]
            tl = alloc(F32, 512)
            rr = alloc(F32, 512)
            nb = alloc(F32, 512)
            nb0 = alloc(F32, 512)
            t2 = alloc(F32, 512)
            S = [Tile(PS[:, 0:1024]), Tile(PS[:, 1024:2048])]
            O = [[psbank(4), psbank(5)], [psbank(6), psbank(7)]]
            P.dma("pool", f_dma(wo.ap, wo_d, True), writes=[wo.b])
            iters = [(qt, b, kt) for qt in range(4) for b in range(8) for kt in range(32)]

            def emit_S(i):
                qt, b, kt = iters[i]
                qs = slice(qt * 512, (qt + 1) * 512)
                ks = slice(kt * 128, (kt + 1) * 128)
                pi = b // 4
                s2 = S[i % 2]
                P.op("pe", f_mm([(s2.ap[:, 0:512], KT.ap[0:64, pi, ks], QT.ap[0:64, b, qs], True, True),
                                 (s2.ap[:, 512:1024], KT.ap[64:128, pi, ks], QT.ap[64:128, b, qs], True, True)]),
                     reads=[KT_b[kt]] + QT_b[qt * 4:(qt + 1) * 4], writes=[s2.b])

            def emit_rest(i):
                qt, b, kt = iters[i]
                qs = slice(qt * 512, (qt + 1) * 512)
                pi = b // 4
                s2 = S[i % 2]
                p2 = Pb[i % 3]
                Oa, Ob = O[b % 2]
                P.op("act", f_act(p2.ap, s2.ap, AF.Exp, scale=0.125), reads=[s2.b], writes=[p2.b])
                P.op("pe", f_mm([(Oa.ap, VA.ap[:, kt, 2 * pi, :], p2.ap[:, 0:512], kt == 0, kt == 31),
                                 (Ob.ap, VA.ap[:, kt, 2 * pi + 1, :], p2.ap[:, 512:1024], kt == 0, kt == 31)]),
                     reads=[VA_b[kt], p2.b], writes=[Oa.b, Ob.b])
                if kt != 31:
                    return
                f_rcp = lambda o, i_: (lambda e: e.reciprocal(out=o, in_=i_))
                P.op("dve", f_copy(tl.ap[64:128, :], Oa.ap[64:128, :]), reads=[Oa.b], writes=[tl.b])
                P.op("dve", f_rcp(t2.ap[64:128, :], tl.ap[64:128, :]), reads=[tl.b], writes=[t2.b])
                P.op("dve", f_copy(rr.ap[0:64, :], t2.ap[64:128, :]), reads=[t2.b], writes=[rr.b])
                P.op("dve", f_tt(oT.ap[0:64, b, :], Oa.ap[0:64, :], rr.ap[0:64, :], ALU.mult),
                     reads=[Oa.b, rr.b], writes=[oT.b])
                P.op("dve", f_copy(tl.ap[64:128, :], Ob.ap[64:128, :]), reads=[Ob.b], writes=[tl.b])
                P.op("dve", f_rcp(t2.ap[64:128, :], tl.ap[64:128, :]), reads=[tl.b], writes=[t2.b])
                P.op("dve", f_copy(nb0.ap[0:64, :], Ob.ap[0:64, :]), reads=[Ob.b], writes=[nb0.b])
                P.op("dve", f_copy(nb.ap[64:128, :], nb0.ap[0:64, :]), reads=[nb0.b], writes=[nb.b])
                P.op("dve", f_tt(oT.ap[64:128, b, :], nb.ap[64:128, :], t2.ap[64:128, :], ALU.mult),
                     reads=[nb.b, t2.b], writes=[oT.b])
                if b != 7:
                    return
                for oc in range(8):
                    ps = O[0][oc % 2]
                    P.op("pe", f_mm(acc_group(ps.ap, [(wo.ap[:, bb, oc * 128:(oc + 1) * 128], oT.ap[:, bb, :])
                                                      for bb in range(8)])),
                         reads=[wo.b, oT.b], writes=[ps.b])
                    P.op("dve", f_tt(hT.ap[:, oc, qs], hT.ap[:, oc, qs], ps.ap, ALU.add),
                         reads=[ps.b, hT_b[oc][qt]], writes=[hT_b[oc][qt]])

            emit_S(0)
            for i in range(len(iters)):
                if i + 1 < len(iters):
                    emit_S(i + 1)
                emit_rest(i)
            P.barrier()
            top[0] = m_phase

        def f_copy_act(out, in_):
            return lambda e: e.activation(out=out, in_=in_, func=AF.Copy)

        def mlp_phase(layer, gidx):
            m_phase = top[0]
            hn = alloc(BF16, 8, T)
            hn_b = [Buf() for _ in range(4)]
            lnv = alloc(F32, 512)
            rstd = alloc(F32, 512)
            h1 = [alloc(BF16, 4, T) for _ in range(2)]
            h1_b = [[Buf() for _ in range(4)] for _ in range(2)]
            win = [alloc(BF16, 8, 512) for _ in range(2)]
            wout = [alloc(BF16, 4, 1024) for _ in range(2)]
            rl = [alloc(F32, 512) for _ in range(2)]
            psI = [psbank(b) for b in range(4)]
            psO = [psbank(b) for b in range(4, 8)]
            for tt in range(4):
                tsl = slice(tt * 512, (tt + 1) * 512)
                rmsnorm_group(hT.ap[:, :, tsl], [hT_b[oc][tt] for oc in range(8)], gidx,
                              hn.ap[:, :, tsl], [hn_b[tt]], 512, psI[tt % 4], lnv, rstd)
            cnt = [0, 0]

            def load(G):
                P.dma("pool", f_dma(win[G % 2].ap, win_d[layer, G], True), writes=[win[G % 2].b])
                P.dma("pool", f_dma(wout[G % 2].ap, wout_d[layer, G], True), writes=[wout[G % 2].b])

            def stage_in(G):
                w = win[G % 2]
                for tt in range(4):
                    tsl = slice(tt * 512, (tt + 1) * 512)
                    for fb in range(4):
                        ps = psI[cnt[0] % 4]
                        r_ = rl[cnt[0] % 2]
                        cnt[0] += 1
                        P.op("pe", f_mm(acc_group(ps.ap, [(w.ap[:, ck, fb * 128:(fb + 1) * 128], hn.ap[:, ck, tsl])
                                                          for ck in range(8)])),
                             reads=[w.b, hn_b[tt]], writes=[ps.b])
                        P.op("act", f_act(r_.ap, ps.ap, AF.Relu), reads=[ps.b], writes=[r_.b])
                        P.op("act", f_act(h1[G % 2].ap[:, fb, tsl], r_.ap, AF.Square), reads=[r_.b],
                             writes=[h1_b[G % 2][tt]])

            def stage_out(G):
                w = wout[G % 2]
                for oc in range(8):
                    for tt in range(4):
                        tsl = slice(tt * 512, (tt + 1) * 512)
                        ps = psO[cnt[1] % 4]
                        cnt[1] += 1
                        P.op("pe", f_mm(acc_group(ps.ap, [(w.ap[:, fb, oc * 128:(oc + 1) * 128], h1[G % 2].ap[:, fb, tsl])
                                                          for fb in range(4)])),
                             reads=[w.b, h1_b[G % 2][tt]], writes=[ps.b])
                        P.op("dve", f_tt(hT.ap[:, oc, tsl], hT.ap[:, oc, tsl], ps.ap, ALU.add),
                             reads=[ps.b, hT_b[oc][tt]], writes=[hT_b[oc][tt]])

            load(0)
            load(1)
            stage_in(0)
            for G in range(8):
                if G + 1 < 8:
                    stage_in(G + 1)
                stage_out(G)
                if G + 2 < 8:
                    load(G + 2)
            P.barrier()
            top[0] = m_phase

        def gla_phase():
            m_phase = top[0]
            hn = alloc(BF16, 8, T)
            hn_b = [Buf() for _ in range(4)]
            zT = alloc(F32, T)
            wup = alloc(F32, 2, 512)
            wz = alloc(BF16, 8, 32)
            wg = alloc(BF16, 8, 1536)
            wo2 = alloc(BF16, 4, 1024)
            oX = alloc(BF16, 4, T)
            oX_b = [Buf() for _ in range(16)]
            Sf = alloc(F32, 2, 256)
            Sb = alloc(BF16, 2, 256)
            rx = alloc(F32, 2, 512)
            NP_ = 2
            lg = [alloc(F32, 256) for _ in range(NP_)]
            E1 = [alloc(F32, 2, 128) for _ in range(NP_)]
            E2 = [alloc(F32, 2, 128) for _ in range(NP_)]
            E3 = [alloc(F32, 256) for _ in range(NP_)]
            qeT = [alloc(BF16, 2, 128) for _ in range(NP_)]
            keT = [alloc(BF16, 2, 128) for _ in range(NP_)]
            kd = [alloc(BF16, 256) for _ in range(NP_)]
            vt = [alloc(BF16, 512) for _ in range(NP_)]
            aTm = [alloc(BF16, 2, 128) for _ in range(NP_)]
            ot = alloc(F32, 4, 128)
            osq = alloc(BF16, 4, 128)
            lnv = alloc(F32, 512)
            rstd = alloc(F32, 512)
            sg = alloc(F32, 512)
            on = alloc(F32, 4, 128)
            ps_qk, ps_kl, ps_v, ps_cr, ps_a, ps_o, ps_ckv, ps_r = [psbank(b) for b in range(8)]
            ps_klb = ps_kl.ap.bitcast(BF16)
            qkb = [alloc(BF16, 512) for _ in range(NP_)]

            P.dma("pool", f_dma(wz.ap, wz_d, True), writes=[wz.b])
            P.dma("sp", f_dma(wup.ap[0:33], wup_d), writes=[wup.b])
            for tt in range(4):
                tsl = slice(tt * 512, (tt + 1) * 512)
                rmsnorm_group(hT.ap[:, :, tsl], [hT_b[oc][tt] for oc in range(8)], 2,
                              hn.ap[:, :, tsl], [hn_b[tt]], 512, ps_qk, lnv, rstd)
            P.op("pool", lambda e: e.memset(zT.ap[32:33, :], 1.0), writes=[zT.b])
            for tt in range(4):
                tsl = slice(tt * 512, (tt + 1) * 512)
                P.op("pe", f_mm(acc_group(ps_v.ap[0:32, :], [(wz.ap[:, ck, :], hn.ap[:, ck, tsl]) for ck in range(8)])),
                     reads=[wz.b, hn_b[tt]], writes=[ps_v.b])
                P.op("act", f_copy_act(zT.ap[0:32, tsl], ps_v.ap[0:32, :]), reads=[ps_v.b], writes=[zT.b])

            QS = float(128.0 ** -0.5)

            def one_pass(g, d, order, final):
                MI = UT if d == 0 else LT
                MS = SL if d == 0 else SU
                lastc = 127 if d == 0 else 0
                def pre(n_, i):
                    p_ = n_ % NP_
                    ts_ = slice(i * 128, (i + 1) * 128)
                    hb = hn_b[i // 4]
                    P.op("pe", f_mm(acc_group(ps_qk.ap, [(hn.ap[:, ck, ts_], wg.ap[:, ck, 0:512]) for ck in range(8)])),
                         reads=[wg.b, hb], writes=[ps_qk.b])
                    P.op("act", f_copy_act(qkb[p_].ap, ps_qk.ap), reads=[ps_qk.b], writes=[qkb[p_].b])
                    P.op("pe", f_mm([(ps_kl.ap[:, 256:512], zT.ap[0:33, ts_], wup.ap[0:33, d, g * 256:(g + 1) * 256],
                                      True, True)]),
                         reads=[zT.b, wup.b], writes=[ps_kl.b])
                    P.op("pe", f_tr([(ps_klb[:, blk * 128:(blk + 1) * 128], qkb[p_].ap[:, blk * 128:(blk + 1) * 128], ident.ap)
                                     for blk in range(4)]),
                         reads=[qkb[p_].b, ident.b], writes=[ps_kl.b])
                    P.op("pe", f_mm(acc_group(ps_v.ap, [(hn.ap[:, ck, ts_], wg.ap[:, ck, 512:1024]) for ck in range(8)])),
                         reads=[wg.b, hb], writes=[ps_v.b])
                    P.op("act", f_act(lg[p_].ap, ps_kl.ap[:, 256:512], AF.Exp, scale=-1.0), reads=[ps_kl.b], writes=[lg[p_].b])
                    P.op("act", f_act(lg[p_].ap, lg[p_].ap, AF.Ln, bias=1.0), reads=[lg[p_].b], writes=[lg[p_].b])
                    P.op("act", f_copy_act(vt[p_].ap, ps_v.ap), reads=[ps_v.b], writes=[vt[p_].b])
                    P.op("pe", f_mm([(ps_cr.ap[:, h * 128:(h + 1) * 128], lg[p_].ap[:, h * 128:(h + 1) * 128], MI.ap, True, True)
                                     for h in range(2)]
                                    + [(ps_cr.ap[:, 256:512], MS.ap, lg[p_].ap, True, True)]),
                         reads=[lg[p_].b, MI.b, MS.b], writes=[ps_cr.b])
                    csv = ps_cr.ap[:, 0:256].rearrange("p (h c) -> p h c", h=2)
                    P.op("act", f_act(E1[p_].ap, csv, AF.Exp, scale=-1.0 / 16.0), reads=[ps_cr.b], writes=[E1[p_].b])
                    P.op("act", f_act(E2[p_].ap, csv, AF.Exp, scale=1.0 / 16.0), reads=[ps_cr.b], writes=[E2[p_].b])
                    P.op("act", f_act(E3[p_].ap, ps_cr.ap[:, 256:512], AF.Exp, scale=-1.0 / 16.0),
                         reads=[ps_cr.b], writes=[E3[p_].b])
                    qkv = ps_klb[:, 0:512].rearrange("p (b c) -> p b c", b=4)
                    P.op("dve", f_stt(qeT[p_].ap, qkv[:, 0:2, :], QS, E1[p_].ap, ALU.mult, ALU.mult),
                         reads=[ps_kl.b, E1[p_].b], writes=[qeT[p_].b])
                    P.op("dve", f_tt(keT[p_].ap, qkv[:, 2:4, :], E2[p_].ap, ALU.mult),
                         reads=[ps_kl.b, E2[p_].b], writes=[keT[p_].b])
                    P.op("dve", f_tt(kd[p_].ap, ps_qk.ap[:, 256:512], E3[p_].ap, ALU.mult),
                         reads=[ps_qk.b, E3[p_].b], writes=[kd[p_].b])
                    P.op("pe", f_mm([(ps_a.ap[:, h * 128:(h + 1) * 128], keT[p_].ap[:, h, :], qeT[p_].ap[:, h, :], True, True)
                                     for h in range(2)]),
                         reads=[keT[p_].b, qeT[p_].b], writes=[ps_a.b])
                    P.op("dve", f_tt(aTm[p_].ap, ps_a.ap[:, 0:256].rearrange("p (h c) -> p h c", h=2),
                                     MI.ap.unsqueeze(1).broadcast_to([128, 2, 128]), ALU.mult),
                         reads=[ps_a.b, MI.b], writes=[aTm[p_].b])
                def post(n_, i):
                    p_ = n_ % NP_
                    ts_ = slice(i * 128, (i + 1) * 128)
                    hb = hn_b[i // 4]
                    grp = []
                    for h in range(2):
                        for vb in range(2):
                            blk = h * 2 + vb
                            o_ = ps_o.ap[:, blk * 128:(blk + 1) * 128]
                            grp.append((o_, vt[p_].ap[:, h * 256 + vb * 128:h * 256 + (vb + 1) * 128], aTm[p_].ap[:, h, :],
                                        True, False))
                            grp.append((o_, Sb.ap[:, h, vb * 128:(vb + 1) * 128], qeT[p_].ap[:, h, :], False, True))
                    P.op("pe", f_mm(grp), reads=[vt[p_].b, aTm[p_].b, Sb.b, qeT[p_].b], writes=[ps_o.b])
                    P.op("pe", f_mm([(ps_ckv.ap[:, h * 256:(h + 1) * 256], kd[p_].ap[:, h * 128:(h + 1) * 128],
                                      vt[p_].ap[:, h * 256:(h + 1) * 256], True, True) for h in range(2)]),
                         reads=[kd[p_].b, vt[p_].b], writes=[ps_ckv.b])
                    for h in range(2):
                        P.op("dve", f_stt(Sf.ap[:, h, :], Sf.ap[:, h, :], E1[p_].ap[:, h, lastc:lastc + 1],
                                          ps_ckv.ap[:, h * 256:(h + 1) * 256], ALU.mult, ALU.add),
                             reads=[Sf.b, E1[p_].b, ps_ckv.b], writes=[Sf.b])
                    P.op("act", f_copy_act(Sb.ap, Sf.ap), reads=[Sf.b], writes=[Sb.b])
                    ov = ps_o.ap.rearrange("p (b c) -> p b c", b=4)
                    if not final:
                        P.op("act", f_copy_act(oX.ap[:, :, ts_], ov), reads=[ps_o.b], writes=[oX_b[i]])
                        return
                    P.op("dve", f_tt(ot.ap, ov, oX.ap[:, :, ts_], ALU.add), reads=[ps_o.b, oX_b[i]], writes=[ot.b])
                    P.op("act", f_act(osq.ap, ot.ap, AF.Square), reads=[ot.b], writes=[osq.b])
                    grp = []
                    for h in range(2):
                        grp += acc_group(ps_a.ap[:, 256 + h * 128:256 + (h + 1) * 128],
                                         [(onesb.ap, osq.ap[:, h * 2 + vb, :]) for vb in range(2)])
                    P.op("pe", f_mm(grp), reads=[osq.b, onesb.b], writes=[ps_a.b])
                    P.op("act", f_act(lnv.ap[:, 0:256], ps_a.ap[:, 256:512], AF.Ln, scale=1.0 / 256.0, bias=EPS),
                         reads=[ps_a.b], writes=[lnv.b])
                    P.op("act", f_act(rstd.ap[:, 0:256], lnv.ap[:, 0:256], AF.Exp, scale=-0.5), reads=[lnv.b], writes=[rstd.b])
                    grp = []
                    for blk in range(4):
                        grp += acc_group(ps_r.ap[:, blk * 128:(blk + 1) * 128],
                                         [(wg.ap[:, ck, 1024 + blk * 128:1024 + (blk + 1) * 128], hn.ap[:, ck, ts_])
                                          for ck in range(8)])
                    P.op("pe", f_mm(grp), reads=[wg.b, hb], writes=[ps_r.b])
                    P.op("act", f_act(sg.ap, ps_r.ap, AF.Exp, scale=-1.0), reads=[ps_r.b], writes=[sg.b])
                    P.op("act", f_act(sg.ap, sg.ap, AF.Ln, bias=1.0), reads=[sg.b], writes=[sg.b])
                    P.op("act", f_act(sg.ap, sg.ap, AF.Exp, scale=-1.0), reads=[sg.b], writes=[sg.b])
                    for blk in range(4):
                        h, vb = blk // 2, blk % 2
                        P.op("dve", f_stt(on.ap[:, blk, :], ot.ap[:, blk, :], og.ap[:, vb:vb + 1],
                                          rstd.ap[:, h * 128:(h + 1) * 128], ALU.mult, ALU.mult),
                             reads=[ot.b, og.b, rstd.b], writes=[on.b])
                    P.op("dve", f_tt(sg.ap, ps_r.ap, sg.ap, ALU.mult), reads=[ps_r.b, sg.b], writes=[sg.b])
                    P.op("dve", f_tt(oX.ap[:, :, ts_], on.ap, sg.ap.rearrange("p (b c) -> p b c", b=4), ALU.mult),
                         reads=[on.b, sg.b], writes=[oX_b[i]])

                pre(0, order[0])
                for n_, i in enumerate(order):
                    if n_ + 1 < len(order):
                        pre(n_ + 1, order[n_ + 1])
                    post(n_, i)

            for g in range(2):
                P.dma("pool", f_dma(wg.ap, wg_d[g], True), writes=[wg.b])
                P.dma("pool", f_dma(wo2.ap, wo2_d[g], True), writes=[wo2.b])
                P.op("dve", lambda e: e.memset(Sf.ap.rearrange("p a b -> p (a b)"), 0.0), writes=[Sf.b])
                P.op("pool", lambda e: e.memset(Sb.ap.rearrange("p a b -> p (a b)"), 0.0), writes=[Sb.b])
                one_pass(g, 0, list(range(16)), False)
                ibb, obb = Buf(), Buf()
                sfl = Sf.ap.rearrange("p a b -> p (a b)")
                P.dma("pool", f_dma(ib_t[g].ap(), sfl), reads=[Sf.b], writes=[ibb])
                P.custom("pool", (lambda g=g: (lambda e: e.collective_compute(
                    "AllGather", ALU.bypass, replica_groups=[[0, 1], [2, 3], [4, 5], [6, 7]],
                    ins=[ib_t[g].ap().opt()], outs=[ob_t[g].ap().opt()])))(), "cc%d" % g, reads=[ibb], writes=[obb])
                P.dma("pool", f_dma(rx.ap, ob_t[g].ap().rearrange("(r p) c -> p r c", p=128)), reads=[obb], writes=[rx.b])
                P.op("dve", f_ts(sfl, rx.ap[:, 0, :], mex.ap[:, 0:1], ALU.mult), reads=[rx.b, mex.b], writes=[Sf.b])
                P.op("dve", f_stt(sfl, rx.ap[:, 1, :], mex.ap[:, 1:2], sfl, ALU.mult, ALU.add),
                     reads=[rx.b, mex.b, Sf.b], writes=[Sf.b])
                P.op("act", f_copy_act(Sb.ap, Sf.ap), reads=[Sf.b], writes=[Sb.b])
                one_pass(g, 1, list(range(15, -1, -1)), True)
                k = 0
                for oc in range(8):
                    for tt in range(4):
                        tsl = slice(tt * 512, (tt + 1) * 512)
                        ps = [ps_qk, ps_kl, ps_v, ps_cr][k % 4]
                        k += 1
                        P.op("pe", f_mm(acc_group(ps.ap, [(wo2.ap[:, blk, oc * 128:(oc + 1) * 128], oX.ap[:, blk, tsl])
                                                          for blk in range(4)])),
                             reads=[wo2.b] + oX_b[tt * 4:(tt + 1) * 4], writes=[ps.b])
                        P.op("dve", f_tt(hT.ap[:, oc, tsl], hT.ap[:, oc, tsl], ps.ap, ALU.add),
                             reads=[ps.b, hT_b[oc][tt]], writes=[hT_b[oc][tt]])
                P.barrier()
            top[0] = m_phase

        def output_phase(do_norm):
            m_phase = top[0]
            lnv = alloc(F32, 512)
            rstd = alloc(F32, 512)
            sq = alloc(BF16, 8, 512)
            yo = [alloc(F32, 8, 512) for _ in range(2)]
            toks = []
            for tt in range(4):
                tsl = slice(tt * 512, (tt + 1) * 512)
                xb = [hT_b[oc][tt] for oc in range(8)]
                if do_norm:
                    y = yo[tt % 2]
                    import os
                    mode = os.environ.get("KDBG_OUT", "")
                    if mode == "B":
                        P.op("dve", f_copy(y.ap, hT.ap[:, :, tsl]), reads=xb, writes=[y.b])
                    else:
                        rmsnorm_group(hT.ap[:, :, tsl], xb, 4, y.ap, [y.b], 512, psbank(tt % 4), lnv, rstd, sq=sq.ap, sqbufs=[sq.b])
                    if mode == "A":
                        toks.append(P.dma("sp", f_dma(y_d[:, :, tsl], hT.ap[:, :, tsl]), reads=xb + [y.b]))
                    else:
                        toks.append(P.dma("sp", f_dma(y_d[:, :, tsl], y.ap), reads=[y.b]))
                else:
                    toks.append(P.dma("sp", f_dma(y_d[:, :, tsl], hT.ap[:, :, tsl]), reads=xb))
            top[0] = m_phase

        phases = [("attn", attention_phase), ("mlp1", lambda: mlp_phase(0, 1)), ("gla", gla_phase),
                  ("mlp2", lambda: mlp_phase(1, 3))]
        done = False
        if only is not None:
            for name, fn in phases:
                if name in only:
                    fn()
            output_phase("final" in only)
            done = True
        else:
            for name, fn in phases:
                fn()
                if stop_after == name:
                    output_phase(False)
                    done = True
                    break
        if not done:
            output_phase(True)
        P.barrier()

        import contextlib
        with contextlib.ExitStack() as es:
            sems = {}
            for k in P.count.keys():
                sems[k] = es.enter_context(nc.semaphore("s_" + k))
            block = es.enter_context(nc.Block())

            def replay(name, e):
                for item in P.lists[name]:
                    if item[0] == "w":
                        e.wait_ge(sems[item[1]], item[2])
                    else:
                        inst = item[1](e)
                        if item[0] == "c":
                            inst.then_inc(sems[item[2]])
                        else:
                            inst.then_inc(sems[item[2]], item[3])

            @block.tensor
            def _(e):
                replay("pe", e)

            @block.scalar
            def _(e):
                replay("act", e)

            @block.vector
            def _(e):
                replay("dve", e)

            @block.gpsimd
            def _(e):
                replay("pool", e)

            @block.sync
            def _(e):
                replay("sp", e)
    return nc


def _fm(w):
    K, N = w.shape
    return np.ascontiguousarray(w.reshape(K // 128, 128, N).transpose(1, 0, 2))


def _prep(inputs):
    f = lambda a: np.asarray(a, dtype=np.float32)
    x = f(inputs["x"])
    norm_mix, norm_mlp, final_norm = f(inputs["norm_mix"]), f(inputs["norm_mlp"]), f(inputs["final_norm"])
    wqkv = f(inputs["attn_w_qkv"])[0]
    qn, kn = f(inputs["attn_q_norm"])[0], f(inputs["attn_k_norm"])[0]
    wo = f(inputs["attn_w_o"])[0]
    gw = f(inputs["gla_w_in"])[0]
    gup, gb = f(inputs["gla_w_gate_up"])[0], f(inputs["gla_b_gate"])[0]
    gon, gwo = f(inputs["gla_out_norm"])[0], f(inputs["gla_w_o"])[0]
    mwi, mwo = f(inputs["mlp_w_in"]), f(inputs["mlp_w_out"])

    gains = np.stack([norm_mix[0], norm_mlp[0], norm_mix[1], norm_mlp[1], final_norm], 0)
    gains_l = np.ascontiguousarray(gains.reshape(5, 8, 128).transpose(2, 0, 1)).reshape(128, 40)
    qkg_l = np.ascontiguousarray(np.broadcast_to(np.concatenate([qn, kn])[None, :], (128, 128)))
    qcols = np.concatenate([np.arange(h * 64, (h + 1) * 64) for h in HO])
    wq_l = _fm(wqkv[:, :1024][:, qcols])
    wkv_l = _fm(wqkv[:, 1024:1536])
    wo_l = _fm(wo[qcols, :])
    win_l = np.stack([np.stack([_fm(mwi[l][:, G * 512:(G + 1) * 512]) for G in range(8)]) for l in range(2)])
    wout_l = np.stack([np.stack([_fm(mwo[l][G * 512:(G + 1) * 512, :]) for G in range(8)]) for l in range(2)])
    wg_l = []
    for g in range(2):
        cols = np.concatenate([np.arange(g * 256, (g + 1) * 256), 512 + np.arange(g * 256, (g + 1) * 256),
                               1024 + np.arange(g * 512, (g + 1) * 512), 2048 + np.arange(g * 512, (g + 1) * 512)])
        wg_l.append(_fm(gw[:, cols]))
    wg_l = np.stack(wg_l)
    og_l = np.ascontiguousarray(gon.reshape(2, 128).T)
    wo2_l = np.stack([_fm(gwo[g * 512:(g + 1) * 512, :]) for g in range(2)])

    maps = []
    idxs = []
    for c in range(NCORES):
        b, s = c // 2, c % 2
        if s == 0:
            own = np.arange(0, T)
            other = np.arange(T, TA)
            dirs = (0, 1)
        else:
            own = np.arange(TA - 1, T - 1, -1)
            other = np.arange(0, T)
            dirs = (1, 0)
        idx = np.concatenate([own, other])
        idxs.append(own)
        xT_l = np.ascontiguousarray(x[b][idx].T.reshape(8, 128, TA).transpose(1, 0, 2))
        pr = (idx // 64).astype(np.float32)
        pc = (idx % 64).astype(np.float32)
        pos_l = np.ascontiguousarray(np.stack([pr.reshape(32, 128).T, pc.reshape(32, 128).T], -1)).reshape(128, 64)
        zc = np.concatenate([3072 + dirs[0] * 16 + np.arange(16), 3072 + dirs[1] * 16 + np.arange(16)])
        wz_l = _fm(gw[:, zc])
        wup_l = np.zeros((33, 2, 512), np.float32)
        wup_l[0:16, 0] = gup[dirs[0]]
        wup_l[16:32, 1] = gup[dirs[1]]
        wup_l[32, 0] = gb[dirs[0]]
        wup_l[32, 1] = gb[dirs[1]]
        mex_l = np.zeros((128, 2), np.float32)
        mex_l[:, 1 - s] = 1.0
        maps.append({
            "xT": xT_l, "pos": pos_l, "gains": gains_l, "qkg": qkg_l, "wq": wq_l, "wkv": wkv_l, "wo": wo_l,
            "win": win_l, "wout": wout_l, "wg": wg_l, "wz": wz_l, "wup": wup_l, "og": og_l, "wo2": wo2_l,
            "mex": mex_l,
        })
    return maps, idxs


_NC_CACHE = {}


def run(inputs, stop_after=None, trace=False, only=None):
    maps, idxs = _prep(inputs)
    key = (stop_after, only)
    if key not in _NC_CACHE:
        _NC_CACHE[key] = build_program(stop_after, only)
    nc = _NC_CACHE[key]
    res = run_bass_kernel_spmd(nc, maps, core_ids=list(range(NCORES)), trace=trace)
    out = np.empty((4, TA, D), np.float32)
    for c in range(NCORES):
        y = np.asarray(res.results[c]["y"])
        out[c // 2, idxs[c], :] = y.transpose(2, 1, 0).reshape(T, D)
    return out, res


def kernel(**inputs):
    out, _ = run(inputs)
    return out
```

```python
import math
from functools import reduce

import numpy as np
import concourse.bass as bass
import concourse.mybir as mybir
from concourse.bass_utils import run_bass_kernel_spmd

F32 = mybir.dt.float32
BF16 = mybir.dt.bfloat16
I32 = mybir.dt.int32
AF = mybir.ActivationFunctionType
ALU = mybir.AluOpType
AX = mybir.AxisListType

NCORES = 8
D = 1024
T = 2048
TA = 4096
EPS = 1e-6
HO = [0, 4, 1, 5, 2, 6, 3, 7, 8, 12, 9, 13, 10, 14, 11, 15]
ENGS = ("pe", "act", "dve", "pool", "sp")
NDSEM = 12
SBUF_BYTES = 212800


class Buf:
    __slots__ = ("w", "r")

    def __init__(self):
        self.w = None
        self.r = {}


class Tile:
    def __init__(self, ap):
        self.ap = ap
        self.b = Buf()


class Prog:
    def __init__(self):
        self.lists = {e: [] for e in ENGS}
        self.count = {}
        self.waited = {e: {} for e in ENGS}
        self.dma_i = {"sp": 0, "pool": 0}

    def _deps(self, eng, reads, writes):
        toks = {}

        def add(t):
            if t is None:
                return
            k, v = t
            if eng == "pe" and k == "pe":
                return
            if toks.get(k, 0) < v:
                toks[k] = v

        for b in reads:
            add(b.w)
        for b in writes:
            add(b.w)
            for k, v in b.r.items():
                add((k, v))
        for k, v in toks.items():
            if self.waited[eng].get(k, 0) < v:
                self.lists[eng].append(("w", k, v))
                self.waited[eng][k] = v

    def _mark(self, tok, reads, writes):
        k, v = tok
        for b in reads:
            if b.r.get(k, 0) < v:
                b.r[k] = v
        for b in writes:
            b.w = tok
            b.r = {}

    def op(self, eng, fn, reads=(), writes=()):
        self._deps(eng, reads, writes)
        self.count[eng] = self.count.get(eng, 0) + 1
        tok = (eng, self.count[eng])
        self.lists[eng].append(("o", fn, eng, 1))
        self._mark(tok, reads, writes)
        return tok

    def dma(self, q, fn, reads=(), writes=()):
        self._deps(q, reads, writes)
        i = self.dma_i[q]
        self.dma_i[q] = i + 1
        key = "d%s%d" % (q, i % NDSEM)
        self.count[key] = self.count.get(key, 0) + 16
        tok = (key, self.count[key])
        self.lists[q].append(("o", fn, key, 16))
        self._mark(tok, reads, writes)
        return tok

    def custom(self, eng, fn, key, reads=(), writes=()):
        self._deps(eng, reads, writes)
        self.count[key] = self.count.get(key, 0) + 1
        tok = (key, self.count[key])
        self.lists[eng].append(("c", fn, key, 1))
        self._mark(tok, reads, writes)
        return tok

    def barrier(self):
        for e in ENGS:
            for k, v in self.count.items():
                if k == e and e != "pe":
                    pass
                if self.waited[e].get(k, 0) < v:
                    self.lists[e].append(("w", k, v))
                    self.waited[e][k] = v


def _prod(s):
    return reduce(lambda a, b: a * b, s, 1)


def build_program(stop_after=None, only=None):
    nc = bass.Bass("TRN2", target_bir_lowering=False)

    def din(name, shape):
        return nc.dram_tensor(name, list(shape), F32, kind="ExternalInput").ap()

    xT_d = din("xT", [128, 8, TA])
    pos_d = din("pos", [128, 64])
    gains_d = din("gains", [128, 40])
    qkg_d = din("qkg", [128, 128])
    wq_d = din("wq", [128, 8, 1024])
    wkv_d = din("wkv", [128, 8, 512])
    wo_d = din("wo", [128, 8, 1024])
    win_d = din("win", [2, 8, 128, 8, 512])
    wout_d = din("wout", [2, 8, 128, 4, 1024])
    wg_d = din("wg", [2, 128, 8, 1536])
    wz_d = din("wz", [128, 8, 32])
    wup_d = din("wup", [33, 2, 512])
    og_d = din("og", [128, 2])
    wo2_d = din("wo2", [2, 128, 4, 1024])
    mex_d = din("mex", [128, 2])
    y_d = nc.dram_tensor("y", [128, 8, T], F32, kind="ExternalOutput").ap()
    ib_t = [nc.dram_tensor("ib%d" % g, [128, 512], F32) for g in range(2)]
    ob_t = [nc.dram_tensor("ob%d" % g, [256, 512], F32) for g in range(2)]

    P = Prog()

    with (
        nc.sbuf_tensor("arena", [128, SBUF_BYTES // 4], F32) as A,
        nc.psum_tensor("psum", [128, 4096], F32) as PS,
    ):
        top = [0]

        def alloc(dtype, *shape):
            esz = 4 if dtype in (F32, I32) else 2
            nbytes = (_prod(shape) * esz + 63) // 64 * 64
            off = top[0]
            top[0] += nbytes
            assert top[0] <= SBUF_BYTES, ("SBUF overflow", top[0])
            ap = A[:, off // 4:(off + nbytes) // 4]
            if dtype != F32:
                ap = ap.bitcast(dtype)
            ap = ap[:, 0:_prod(shape)]
            if len(shape) > 1:
                names = ["a%d" % i for i in range(len(shape))]
                ap = ap.rearrange("p (%s) -> p %s" % (" ".join(names), " ".join(names)),
                                  **{n: s for n, s in zip(names[:-1], shape[:-1])})
            return Tile(ap)

        def psbank(b, dtype=F32):
            ap = PS[:, b * 512:(b + 1) * 512]
            if dtype != F32:
                ap = ap.bitcast(dtype)
            return Tile(ap)

        def f_act(out, in_, func, scale=1.0, bias=None):
            if bias is None:
                return lambda e: e.activation(out=out, in_=in_, func=func, scale=scale)
            return lambda e: e.activation(out=out, in_=in_, func=func, scale=scale, bias=bias)

        def f_tt(out, in0, in1, op):
            return lambda e: e.tensor_tensor(out=out, in0=in0, in1=in1, op=op)

        def f_stt(out, in0, scalar, in1, op0, op1):
            return lambda e: e.scalar_tensor_tensor(out=out, in0=in0, scalar=scalar, in1=in1, op0=op0, op1=op1)

        def f_ts(out, in0, s1, op0, s2=None, op1=None):
            if op1 is None:
                return lambda e: e.tensor_scalar(out=out, in0=in0, scalar1=s1, scalar2=None, op0=op0)
            return lambda e: e.tensor_scalar(out=out, in0=in0, scalar1=s1, scalar2=s2, op0=op0, op1=op1)

        def f_copy(out, in_):
            return lambda e: e.tensor_copy(out=out, in_=in_)

        def f_mm(group):
            def fn(e):
                inst = None
                for (o, l, r, st, sp) in group:
                    inst = e.matmul(o, l, r, start=st, stop=sp)
                return inst
            return fn

        def f_tr(group):
            def fn(e):
                inst = None
                for (o, i, idn) in group:
                    inst = e.transpose(o, i, idn)
                return inst
            return fn

        def f_dma(out, in_, cast=False):
            if cast:
                return lambda e: e.dma_start(out=out, in_=in_, max_dma_last_dim=4096)
            return lambda e: e.dma_start(out=out, in_=in_)

        def acc_group(out, pairs):
            n = len(pairs)
            return [(out, l, r, i == 0, i == n - 1) for i, (l, r) in enumerate(pairs)]

        hT = alloc(F32, 8, T)
        hT_b = [[Buf() for _ in range(4)] for _ in range(8)]
        hT_all = [b for row in hT_b for b in row]
        onesf = alloc(F32, 128)
        UT = alloc(F32, 128)
        LT = alloc(F32, 128)
        SL = alloc(F32, 128)
        SU = alloc(F32, 128)
        IDF = alloc(F32, 128)
        ident = alloc(BF16, 128)
        onesb = alloc(BF16, 128)
        gains = alloc(F32, 5, 8)
        qkg = alloc(F32, 2, 64)
        og = alloc(F32, 2)
        mex = alloc(F32, 2)
        pos = alloc(F32, 64)
        tab = alloc(F32, 2, 32, 2, 16)
        persist_top = top[0]

        for tt in range(4):
            P.dma("sp", f_dma(hT.ap[:, :, tt * 512:(tt + 1) * 512], xT_d[:, :, tt * 512:(tt + 1) * 512]),
                  writes=[hT_b[oc][tt] for oc in range(8)])
        P.dma("sp", f_dma(gains.ap, gains_d.rearrange("p (a b) -> p a b", a=5)), writes=[gains.b])
        P.dma("sp", f_dma(qkg.ap, qkg_d.rearrange("p (a b) -> p a b", a=2)), writes=[qkg.b])
        P.dma("sp", f_dma(og.ap, og_d), writes=[og.b])
        P.dma("sp", f_dma(mex.ap, mex_d), writes=[mex.b])
        P.dma("sp", f_dma(pos.ap, pos_d), writes=[pos.b])

        P.op("pool", lambda e: e.memset(onesf.ap, 1.0), writes=[onesf.b])
        P.op("pool", lambda e: e.memset(onesb.ap, 1.0), writes=[onesb.b])

        def mk_mask(dst, cm, step, cmp):
            P.op("pool", lambda e: e.affine_select(out=dst.ap, in_=onesf.ap, pattern=[[step, 128]],
                                                   compare_op=cmp, fill=0.0, base=0, channel_multiplier=cm),
                 reads=[onesf.b], writes=[dst.b])

        mk_mask(UT, -1, 1, ALU.is_ge)
        mk_mask(LT, 1, -1, ALU.is_ge)
        mk_mask(SL, 1, -1, ALU.is_gt)
        mk_mask(SU, -1, 1, ALU.is_gt)
        mk_mask(IDF, 1, -1, ALU.is_equal)
        P.op("dve", f_copy(ident.ap, IDF.ap), reads=[IDF.b], writes=[ident.b])

        m0 = top[0]
        invf = alloc(F32, 16)
        ang = alloc(F32, 64, 16)
        u = alloc(F32, 2, 1024)
        ki = alloc(I32, 2048)
        kf_ = alloc(F32, 2048)
        fr = alloc(F32, 2048)
        ng = alloc(F32, 2048)
        for f in range(16):
            val = float(10000.0 ** (-(2.0 * f) / 32.0))
            P.op("pool", (lambda f=f, val=val: (lambda e: e.memset(invf.ap[:, f:f + 1], val)))(), writes=[invf.b])
        P.op("dve", f_tt(ang.ap, pos.ap.unsqueeze(2).broadcast_to([128, 64, 16]),
                         invf.ap.unsqueeze(1).broadcast_to([128, 64, 16]), ALU.mult),
             reads=[pos.b, invf.b], writes=[ang.b])
        angf = ang.ap.rearrange("p a b -> p (a b)")
        inv2pi = float(1.0 / (2.0 * math.pi))
        P.op("dve", f_ts(u.ap[:, 0, :], angf, inv2pi, ALU.mult, 0.5, ALU.add), reads=[ang.b], writes=[u.b])
        P.op("dve", f_ts(u.ap[:, 1, :], angf, inv2pi, ALU.mult, 0.75, ALU.add), reads=[ang.b], writes=[u.b])
        uf = u.ap.rearrange("p a b -> p (a b)")
        P.op("dve", f_copy(ki.ap, uf), reads=[u.b], writes=[ki.b])
        P.op("dve", f_copy(kf_.ap, ki.ap), reads=[ki.b], writes=[kf_.b])
        P.op("dve", f_tt(fr.ap, uf, kf_.ap, ALU.subtract), reads=[u.b, kf_.b], writes=[fr.b])
        P.op("dve", lambda e: e.tensor_single_scalar(out=ng.ap, in_=fr.ap, scalar=0.0, op=ALU.is_lt),
             reads=[fr.b], writes=[ng.b])
        P.op("dve", f_tt(fr.ap, fr.ap, ng.ap, ALU.add), reads=[fr.b, ng.b], writes=[fr.b])
        P.op("dve", f_ts(fr.ap, fr.ap, float(2.0 * math.pi), ALU.mult, float(-math.pi), ALU.add),
             reads=[fr.b], writes=[fr.b])
        P.op("dve", f_ts(fr.ap, fr.ap, -3.141592, ALU.max, 3.141592, ALU.min), reads=[fr.b], writes=[fr.b])
        P.op("act", f_act(tab.ap.rearrange("p a b c d -> p (a b c d)"), fr.ap, AF.Sin), reads=[fr.b], writes=[tab.b])
        P.barrier()
        top[0] = m0

        def rmsnorm_group(xsrc, xbufs, gidx, dst, dstbufs, n, ps_ss, lnv, rstd, sq=None, sqbufs=None):
            if sq is None:
                sq, sqbufs = dst, dstbufs
            P.op("act", f_act(sq, xsrc, AF.Square), reads=xbufs, writes=sqbufs)
            P.op("pe", f_mm(acc_group(ps_ss.ap[:, 0:n], [(onesb.ap, sq[:, ck, :]) for ck in range(8)])),
                 reads=sqbufs + [onesb.b], writes=[ps_ss.b])
            P.op("act", f_act(lnv.ap[:, 0:n], ps_ss.ap[:, 0:n], AF.Ln, scale=1.0 / D, bias=EPS),
                 reads=[ps_ss.b], writes=[lnv.b])
            P.op("act", f_act(rstd.ap[:, 0:n], lnv.ap[:, 0:n], AF.Exp, scale=-0.5), reads=[lnv.b], writes=[rstd.b])
            fns = [f_stt(dst[:, ck, :], xsrc[:, ck, :], gains.ap[:, gidx, ck:ck + 1], rstd.ap[:, 0:n],
                         ALU.mult, ALU.mult) for ck in range(8)]

            def multi(e, fns=fns):
                inst = None
                for f in fns:
                    inst = f(e)
                return inst
            P.op("dve", multi, reads=xbufs + [rstd.b, gains.b], writes=dstbufs)

        def rope(src, dst, H, i, tmps):
            sv = src.ap.rearrange("p (h r x f) -> p h r x f", h=H, r=2, x=2)
            dv = dst.ap.rearrange("p (h r x f) -> p h r x f", h=H, r=2, x=2)
            x1, x2 = sv[:, :, :, 0, :], sv[:, :, :, 1, :]
            sn = tab.ap[:, 0, i, :, :].unsqueeze(1).broadcast_to([128, H, 2, 16])
            cs = tab.ap[:, 1, i, :, :].unsqueeze(1).broadcast_to([128, H, 2, 16])
            t1, t2, t3, t4 = [t.ap[:, 0:H * 32].rearrange("p (h r f) -> p h r f", h=H, r=2) for t in tmps]
            tb = [t.b for t in tmps]
            P.op("dve", f_tt(t1, x1, cs, ALU.mult), reads=[src.b, tab.b], writes=[tb[0]])
            P.op("dve", f_tt(t2, x2, sn, ALU.mult), reads=[src.b, tab.b], writes=[tb[1]])
            P.op("dve", f_tt(t3, x2, cs, ALU.mult), reads=[src.b, tab.b], writes=[tb[2]])
            P.op("dve", f_tt(t4, x1, sn, ALU.mult), reads=[src.b, tab.b], writes=[tb[3]])
            P.op("dve", f_tt(dv[:, :, :, 0, :], t1, t2, ALU.subtract), reads=[tb[0], tb[1]], writes=[dst.b])
            P.op("dve", f_tt(dv[:, :, :, 1, :], t3, t4, ALU.add), reads=[tb[2], tb[3]], writes=[dst.b])

        def norm_rope(srcf, H, gi, dst, i, sqt, sst, lt, rt, tmps, ro):
            v = srcf.ap.rearrange("p (h d) -> p h d", h=H)
            sqv = sqt.ap[:, 0:H * 64].rearrange("p (h d) -> p h d", h=H)
            P.op("dve", f_tt(sqt.ap[:, 0:H * 64], srcf.ap, srcf.ap, ALU.mult), reads=[srcf.b], writes=[sqt.b])
            P.op("dve", lambda e: e.tensor_reduce(out=sst.ap[:, 0:H], in_=sqv, axis=AX.X, op=ALU.add),
                 reads=[sqt.b], writes=[sst.b])
            P.op("act", f_act(lt.ap[:, 0:H], sst.ap[:, 0:H], AF.Ln, scale=1.0 / 64.0, bias=EPS),
                 reads=[sst.b], writes=[lt.b])
            P.op("act", f_act(rt.ap[:, 0:H], lt.ap[:, 0:H], AF.Exp, scale=-0.5), reads=[lt.b], writes=[rt.b])
            P.op("dve", f_tt(v, v, qkg.ap[:, gi, :].unsqueeze(1).broadcast_to([128, H, 64]), ALU.mult),
                 reads=[srcf.b, qkg.b], writes=[srcf.b])
            rot = Tile(ro.ap[:, 0:H * 64])
            rot.b = ro.b
            rope(srcf, rot, H, i, tmps)
            P.op("dve", f_tt(dst.ap.rearrange("p (h d) -> p h d", h=H), rot.ap.rearrange("p (h d) -> p h d", h=H),
                             rt.ap[:, 0:H].unsqueeze(2).broadcast_to([128, H, 64]), ALU.mult),
                 reads=[ro.b, rt.b], writes=[dst.b])

        def headnorm(srcf, H, gi, sqt, sst, lt, rt):
            v = srcf.ap.rearrange("p (h d) -> p h d", h=H)
            sqv = sqt.ap[:, 0:H * 64].rearrange("p (h d) -> p h d", h=H)
            P.op("dve", f_tt(sqt.ap[:, 0:H * 64], srcf.ap, srcf.ap, ALU.mult), reads=[srcf.b], writes=[sqt.b])
            P.op("dve", lambda e: e.tensor_reduce(out=sst.ap[:, 0:H], in_=sqv, axis=AX.X, op=ALU.add),
                 reads=[sqt.b], writes=[sst.b])
            P.op("act", f_act(lt.ap[:, 0:H], sst.ap[:, 0:H], AF.Ln, scale=1.0 / 64.0, bias=EPS),
                 reads=[sst.b], writes=[lt.b])
            P.op("act", f_act(rt.ap[:, 0:H], lt.ap[:, 0:H], AF.Exp, scale=-0.5), reads=[lt.b], writes=[rt.b])
            P.op("dve", f_tt(v, v, rt.ap[:, 0:H].unsqueeze(2).broadcast_to([128, H, 64]), ALU.mult),
                 reads=[srcf.b, rt.b], writes=[srcf.b])
            P.op("dve", f_tt(v, v, qkg.ap[:, gi, :].unsqueeze(1).broadcast_to([128, H, 64]), ALU.mult),
                 reads=[srcf.b, qkg.b], writes=[srcf.b])

        def attention_phase():
            m_phase = top[0]
            QT = alloc(BF16, 8, T)
            KT = alloc(BF16, 2, TA)
            VA = alloc(BF16, 32, 4, 128)
            QT_b = [Buf() for _ in range(16)]
            KT_b = [Buf() for _ in range(32)]
            VA_b = [Buf() for _ in range(32)]
            m_a = top[0]
            wq = alloc(BF16, 8, 1024)
            wkv = alloc(BF16, 8, 512)
            xo = alloc(F32, 8, 256)
            xof = xo.ap.rearrange("p a b -> p (a b)")
            ksq = Tile(xof[:, 0:256]); ksst = Tile(xof[:, 256:264]); klt = Tile(xof[:, 264:272]); krt = Tile(xof[:, 272:280])
            krtm = [Tile(xof[:, 512 + k * 256:768 + k * 256]) for k in range(4)]
            qf_b = Tile(xof[:, 1536:2048])
            kpriv = [ksq.b, ksst.b, klt.b, krt.b, qf_b.b] + [t.b for t in krtm]
            hn = alloc(BF16, 8, 256)
            lnv = alloc(F32, 256)
            rstd = alloc(F32, 256)
            kf = alloc(F32, 256)
            qf2 = [alloc(F32, 512), qf_b]
            sqt = alloc(F32, 512)
            sst = alloc(F32, 8)
            lt = alloc(F32, 8)
            rt = alloc(F32, 8)
            rtm = [alloc(F32, 256) for _ in range(4)]
            Kr = alloc(BF16, 256)
            Qr = alloc(BF16, 1024)
            ro = sqt
            ps_ss, ps_kv, ps_kt, ps_qt = psbank(0), psbank(1), psbank(3, BF16), psbank(4, BF16)
            ps_q2 = [psbank(2), psbank(5)]

            P.dma("pool", f_dma(wq.ap, wq_d, True), writes=[wq.b])
            P.dma("pool", f_dma(wkv.ap, wkv_d, True), writes=[wkv.b])
            P.op("pool", lambda e: e.memset(VA.ap.rearrange("p a b c -> p (a b c)"), 1.0), writes=VA_b)

            for gi in range(16):
                own = gi < 8
                if own:
                    xsrc = hT.ap[:, :, gi * 256:(gi + 1) * 256]
                    xb = [hT_b[oc][gi // 2] for oc in range(8)]
                else:
                    P.dma("sp", f_dma(xo.ap, xT_d[:, :, gi * 256:(gi + 1) * 256]), writes=[xo.b] + kpriv)
                    xsrc, xb = xo.ap, [xo.b]
                rmsnorm_group(xsrc, xb, 0, hn.ap, [hn.b], 256, ps_ss, lnv, rstd)
                for j in range(2):
                    i = gi * 2 + j
                    js = slice(j * 128, (j + 1) * 128)
                    ts_ = slice(i * 128, (i + 1) * 128)
                    P.op("pe", f_mm(acc_group(ps_kv.ap, [(hn.ap[:, ck, js], wkv.ap[:, ck, :]) for ck in range(8)])),
                         reads=[hn.b, wkv.b], writes=[ps_kv.b])
                    P.op("act", f_copy_act(VA.ap[:, i, :, 0:64],
                                           ps_kv.ap[:, 256:512].rearrange("p (m d) -> p m d", m=4)),
                         reads=[ps_kv.b], writes=[VA_b[i]])
                    P.op("act", f_copy_act(kf.ap, ps_kv.ap[:, 0:256]), reads=[ps_kv.b], writes=[kf.b])
                    if own:
                        norm_rope(kf, 4, 1, Kr, i, ksq, ksst, klt, krt, krtm, ksq)
                    else:
                        norm_rope(kf, 4, 1, Kr, i, sqt, sst, lt, rt, rtm, ro)
                    P.op("pe", f_tr([(ps_kt.ap[:, pi * 128:(pi + 1) * 128], Kr.ap[:, pi * 128:(pi + 1) * 128], ident.ap)
                                     for pi in range(2)]),
                         reads=[Kr.b, ident.b], writes=[ps_kt.b])
                    P.op("act", f_copy_act(KT.ap[:, :, ts_], ps_kt.ap[:, 0:256].rearrange("p (a b) -> p a b", a=2)),
                         reads=[ps_kt.b], writes=[KT_b[i]])
                    if own:
                        for half in range(2):
                            qf, ps_q = qf2[half], ps_q2[half]
                            P.op("pe", f_mm(acc_group(ps_q.ap, [(hn.ap[:, ck, js], wq.ap[:, ck, half * 512:(half + 1) * 512])
                                                               for ck in range(8)])),
                                 reads=[hn.b, wq.b], writes=[ps_q.b])
                            P.op("act", f_copy_act(qf.ap, ps_q.ap), reads=[ps_q.b], writes=[qf.b])
                            qdst = Tile(Qr.ap[:, half * 512:(half + 1) * 512])
                            qdst.b = Qr.b
                            norm_rope(qf, 8, 0, qdst, i, sqt, sst, lt, rt, rtm, ro)
                        P.op("pe", f_tr([(ps_qt.ap[:, b * 128:(b + 1) * 128], Qr.ap[:, b * 128:(b + 1) * 128], ident.ap)
                                         for b in range(8)]),
                             reads=[Qr.b, ident.b], writes=[ps_qt.b])
                        P.op("act", f_copy_act(QT.ap[:, :, ts_], ps_qt.ap.rearrange("p (a b) -> p a b", a=8)),
                             reads=[ps_qt.b], writes=[QT_b[i]])
            P.barrier()
            top[0] = m_a
            wo = alloc(BF16, 8, 1024)
            oT = alloc(BF16, 8, 512)
            Pb = [alloc(BF16, 1024) for _ in range(3)]
            tl = alloc(F32, 512)
            rr = alloc(F32, 512)
            nb = alloc(F32, 512)
            nb0 = alloc(F32, 512)
            t2 = alloc(F32, 512)
            S = [Tile(PS[:, 0:1024]), Tile(PS[:, 1024:2048])]
            O = [[psbank(4), psbank(5)], [psbank(6), psbank(7)]]
            P.dma("pool", f_dma(wo.ap, wo_d, True), writes=[wo.b])
            iters = [(qt, b, kt) for qt in range(4) for b in range(8) for kt in range(32)]

            def emit_S(i):
                qt, b, kt = iters[i]
                qs = slice(qt * 512, (qt + 1) * 512)
                ks = slice(kt * 128, (kt + 1) * 128)
                pi = b // 4
                s2 = S[i % 2]
                P.op("pe", f_mm([(s2.ap[:, 0:512], KT.ap[0:64, pi, ks], QT.ap[0:64, b, qs], True, True),
                                 (s2.ap[:, 512:1024], KT.ap[64:128, pi, ks], QT.ap[64:128, b, qs], True, True)]),
                     reads=[KT_b[kt]] + QT_b[qt * 4:(qt + 1) * 4], writes=[s2.b])

            def emit_rest(i):
                qt, b, kt = iters[i]
                qs = slice(qt * 512, (qt + 1) * 512)
                pi = b // 4
                s2 = S[i % 2]
                p2 = Pb[i % 3]
                Oa, Ob = O[b % 2]
                P.op("act", f_act(p2.ap, s2.ap, AF.Exp, scale=0.125), reads=[s2.b], writes=[p2.b])
                P.op("pe", f_mm([(Oa.ap, VA.ap[:, kt, 2 * pi, :], p2.ap[:, 0:512], kt == 0, kt == 31),
                                 (Ob.ap, VA.ap[:, kt, 2 * pi + 1, :], p2.ap[:, 512:1024], kt == 0, kt == 31)]),
                     reads=[VA_b[kt], p2.b], writes=[Oa.b, Ob.b])
                if kt != 31:
                    return
                f_rcp = lambda o, i_: (lambda e: e.reciprocal(out=o, in_=i_))
                P.op("dve", f_copy(tl.ap[64:128, :], Oa.ap[64:128, :]), reads=[Oa.b], writes=[tl.b])
                P.op("dve", f_rcp(t2.ap[64:128, :], tl.ap[64:128, :]), reads=[tl.b], writes=[t2.b])
                P.op("dve", f_copy(rr.ap[0:64, :], t2.ap[64:128, :]), reads=[t2.b], writes=[rr.b])
                P.op("dve", f_tt(oT.ap[0:64, b, :], Oa.ap[0:64, :], rr.ap[0:64, :], ALU.mult),
                     reads=[Oa.b, rr.b], writes=[oT.b])
                P.op("dve", f_copy(tl.ap[64:128, :], Ob.ap[64:128, :]), reads=[Ob.b], writes=[tl.b])
                P.op("dve", f_rcp(t2.ap[64:128, :], tl.ap[64:128, :]), reads=[tl.b], writes=[t2.b])
                P.op("dve", f_copy(nb0.ap[0:64, :], Ob.ap[0:64, :]), reads=[Ob.b], writes=[nb0.b])
                P.op("dve", f_copy(nb.ap[64:128, :], nb0.ap[0:64, :]), reads=[nb0.b], writes=[nb.b])
                P.op("dve", f_tt(oT.ap[64:128, b, :], nb.ap[64:128, :], t2.ap[64:128, :], ALU.mult),
                     reads=[nb.b, t2.b], writes=[oT.b])
                if b != 7:
                    return
                for oc in range(8):
                    ps = O[0][oc % 2]
                    P.op("pe", f_mm(acc_group(ps.ap, [(wo.ap[:, bb, oc * 128:(oc + 1) * 128], oT.ap[:, bb, :])
                                                      for bb in range(8)])),
                         reads=[wo.b, oT.b], writes=[ps.b])
                    P.op("dve", f_tt(hT.ap[:, oc, qs], hT.ap[:, oc, qs], ps.ap, ALU.add),
                         reads=[ps.b, hT_b[oc][qt]], writes=[hT_b[oc][qt]])

            emit_S(0)
            for i in range(len(iters)):
                if i + 1 < len(iters):
                    emit_S(i + 1)
                emit_rest(i)
            P.barrier()
            top[0] = m_phase

        def f_copy_act(out, in_):
            return lambda e: e.activation(out=out, in_=in_, func=AF.Copy)

        def mlp_phase(layer, gidx):
            m_phase = top[0]
            hn = alloc(BF16, 8, T)
            hn_b = [Buf() for _ in range(4)]
            lnv = alloc(F32, 512)
            rstd = alloc(F32, 512)
            h1 = [alloc(BF16, 4, T) for _ in range(2)]
            h1_b = [[Buf() for _ in range(4)] for _ in range(2)]
            win = [alloc(BF16, 8, 512) for _ in range(2)]
            wout = [alloc(BF16, 4, 1024) for _ in range(2)]
            rl = [alloc(F32, 512) for _ in range(2)]
            psI = [psbank(b) for b in range(4)]
            psO = [psbank(b) for b in range(4, 8)]
            for tt in range(4):
                tsl = slice(tt * 512, (tt + 1) * 512)
                rmsnorm_group(hT.ap[:, :, tsl], [hT_b[oc][tt] for oc in range(8)], gidx,
                              hn.ap[:, :, tsl], [hn_b[tt]], 512, psI[tt % 4], lnv, rstd)
            cnt = [0, 0]

            def load(G):
                P.dma("pool", f_dma(win[G % 2].ap, win_d[layer, G], True), writes=[win[G % 2].b])
                P.dma("pool", f_dma(wout[G % 2].ap, wout_d[layer, G], True), writes=[wout[G % 2].b])

            def stage_in(G):
                w = win[G % 2]
                for tt in range(4):
                    tsl = slice(tt * 512, (tt + 1) * 512)
                    for fb in range(4):
                        ps = psI[cnt[0] % 4]
                        r_ = rl[cnt[0] % 2]
                        cnt[0] += 1
                        P.op("pe", f_mm(acc_group(ps.ap, [(w.ap[:, ck, fb * 128:(fb + 1) * 128], hn.ap[:, ck, tsl])
                                                          for ck in range(8)])),
                             reads=[w.b, hn_b[tt]], writes=[ps.b])
                        P.op("act", f_act(r_.ap, ps.ap, AF.Relu), reads=[ps.b], writes=[r_.b])
                        P.op("act", f_act(h1[G % 2].ap[:, fb, tsl], r_.ap, AF.Square), reads=[r_.b],
                             writes=[h1_b[G % 2][tt]])

            def stage_out(G):
                w = wout[G % 2]
                for oc in range(8):
                    for tt in range(4):
                        tsl = slice(tt * 512, (tt + 1) * 512)
                        ps = psO[cnt[1] % 4]
                        cnt[1] += 1
                        P.op("pe", f_mm(acc_group(ps.ap, [(w.ap[:, fb, oc * 128:(oc + 1) * 128], h1[G % 2].ap[:, fb, tsl])
                                                          for fb in range(4)])),
                             reads=[w.b, h1_b[G % 2][tt]], writes=[ps.b])
                        P.op("dve", f_tt(hT.ap[:, oc, tsl], hT.ap[:, oc, tsl], ps.ap, ALU.add),
                             reads=[ps.b, hT_b[oc][tt]], writes=[hT_b[oc][tt]])

            load(0)
            load(1)
            stage_in(0)
            for G in range(8):
                if G + 1 < 8:
                    stage_in(G + 1)
                stage_out(G)
                if G + 2 < 8:
                    load(G + 2)
            P.barrier()
            top[0] = m_phase

        def gla_phase():
            m_phase = top[0]
            hn = alloc(BF16, 8, T)
            hn_b = [Buf() for _ in range(4)]
            zT = alloc(F32, T)
            wup = alloc(F32, 2, 512)
            wz = alloc(BF16, 8, 32)
            wg = alloc(BF16, 8, 1536)
            wo2 = alloc(BF16, 4, 1024)
            oX = alloc(BF16, 4, T)
            oX_b = [Buf() for _ in range(16)]
            Sf = alloc(F32, 2, 256)
            Sb = alloc(BF16, 2, 256)
            rx = alloc(F32, 2, 512)
            NP_ = 2
            lg = [alloc(F32, 256) for _ in range(NP_)]
            E1 = [alloc(F32, 2, 128) for _ in range(NP_)]
            E2 = [alloc(F32, 2, 128) for _ in range(NP_)]
            E3 = [alloc(F32, 256) for _ in range(NP_)]
            qeT = [alloc(BF16, 2, 128) for _ in range(NP_)]
            keT = [alloc(BF16, 2, 128) for _ in range(NP_)]
            kd = [alloc(BF16, 256) for _ in range(NP_)]
            vt = [alloc(BF16, 512) for _ in range(NP_)]
            aTm = [alloc(BF16, 2, 128) for _ in range(NP_)]
            ot = alloc(F32, 4, 128)
            osq = alloc(BF16, 4, 128)
            lnv = alloc(F32, 512)
            rstd = alloc(F32, 512)
            sg = alloc(F32, 512)
            on = alloc(F32, 4, 128)
            ps_qk, ps_kl, ps_v, ps_cr, ps_a, ps_o, ps_ckv, ps_r = [psbank(b) for b in range(8)]
            ps_klb = ps_kl.ap.bitcast(BF16)
            qkb = [alloc(BF16, 512) for _ in range(NP_)]

            P.dma("pool", f_dma(wz.ap, wz_d, True), writes=[wz.b])
            P.dma("sp", f_dma(wup.ap[0:33], wup_d), writes=[wup.b])
            for tt in range(4):
                tsl = slice(tt * 512, (tt + 1) * 512)
                rmsnorm_group(hT.ap[:, :, tsl], [hT_b[oc][tt] for oc in range(8)], 2,
                              hn.ap[:, :, tsl], [hn_b[tt]], 512, ps_qk, lnv, rstd)
            P.op("pool", lambda e: e.memset(zT.ap[32:33, :], 1.0), writes=[zT.b])
            for tt in range(4):
                tsl = slice(tt * 512, (tt + 1) * 512)
                P.op("pe", f_mm(acc_group(ps_v.ap[0:32, :], [(wz.ap[:, ck, :], hn.ap[:, ck, tsl]) for ck in range(8)])),
                     reads=[wz.b, hn_b[tt]], writes=[ps_v.b])
                P.op("act", f_copy_act(zT.ap[0:32, tsl], ps_v.ap[0:32, :]), reads=[ps_v.b], writes=[zT.b])

            QS = float(128.0 ** -0.5)

            def one_pass(g, d, order, final):
                MI = UT if d == 0 else LT
                MS = SL if d == 0 else SU
                lastc = 127 if d == 0 else 0
                def pre(n_, i):
                    p_ = n_ % NP_
                    ts_ = slice(i * 128, (i + 1) * 128)
                    hb = hn_b[i // 4]
                    P.op("pe", f_mm(acc_group(ps_qk.ap, [(hn.ap[:, ck, ts_], wg.ap[:, ck, 0:512]) for ck in range(8)])),
                         reads=[wg.b, hb], writes=[ps_qk.b])
                    P.op("act", f_copy_act(qkb[p_].ap, ps_qk.ap), reads=[ps_qk.b], writes=[qkb[p_].b])
                    P.op("pe", f_mm([(ps_kl.ap[:, 256:512], zT.ap[0:33, ts_], wup.ap[0:33, d, g * 256:(g + 1) * 256],
                                      True, True)]),
                         reads=[zT.b, wup.b], writes=[ps_kl.b])
                    P.op("pe", f_tr([(ps_klb[:, blk * 128:(blk + 1) * 128], qkb[p_].ap[:, blk * 128:(blk + 1) * 128], ident.ap)
                                     for blk in range(4)]),
                         reads=[qkb[p_].b, ident.b], writes=[ps_kl.b])
                    P.op("pe", f_mm(acc_group(ps_v.ap, [(hn.ap[:, ck, ts_], wg.ap[:, ck, 512:1024]) for ck in range(8)])),
                         reads=[wg.b, hb], writes=[ps_v.b])
                    P.op("act", f_act(lg[p_].ap, ps_kl.ap[:, 256:512], AF.Exp, scale=-1.0), reads=[ps_kl.b], writes=[lg[p_].b])
                    P.op("act", f_act(lg[p_].ap, lg[p_].ap, AF.Ln, bias=1.0), reads=[lg[p_].b], writes=[lg[p_].b])
                    P.op("act", f_copy_act(vt[p_].ap, ps_v.ap), reads=[ps_v.b], writes=[vt[p_].b])
                    P.op("pe", f_mm([(ps_cr.ap[:, h * 128:(h + 1) * 128], lg[p_].ap[:, h * 128:(h + 1) * 128], MI.ap, True, True)
                                     for h in range(2)]
                                    + [(ps_cr.ap[:, 256:512], MS.ap, lg[p_].ap, True, True)]),
                         reads=[lg[p_].b, MI.b, MS.b], writes=[ps_cr.b])
                    csv = ps_cr.ap[:, 0:256].rearrange("p (h c) -> p h c", h=2)
                    P.op("act", f_act(E1[p_].ap, csv, AF.Exp, scale=-1.0 / 16.0), reads=[ps_cr.b], writes=[E1[p_].b])
                    P.op("act", f_act(E2[p_].ap, csv, AF.Exp, scale=1.0 / 16.0), reads=[ps_cr.b], writes=[E2[p_].b])
                    P.op("act", f_act(E3[p_].ap, ps_cr.ap[:, 256:512], AF.Exp, scale=-1.0 / 16.0),
                         reads=[ps_cr.b], writes=[E3[p_].b])
                    qkv = ps_klb[:, 0:512].rearrange("p (b c) -> p b c", b=4)
                    P.op("dve", f_stt(qeT[p_].ap, qkv[:, 0:2, :], QS, E1[p_].ap, ALU.mult, ALU.mult),
                         reads=[ps_kl.b, E1[p_].b], writes=[qeT[p_].b])
                    P.op("dve", f_tt(keT[p_].ap, qkv[:, 2:4, :], E2[p_].ap, ALU.mult),
                         reads=[ps_kl.b, E2[p_].b], writes=[keT[p_].b])
                    P.op("dve", f_tt(kd[p_].ap, ps_qk.ap[:, 256:512], E3[p_].ap, ALU.mult),
                         reads=[ps_qk.b, E3[p_].b], writes=[kd[p_].b])
                    P.op("pe", f_mm([(ps_a.ap[:, h * 128:(h + 1) * 128], keT[p_].ap[:, h, :], qeT[p_].ap[:, h, :], True, True)
                                     for h in range(2)]),
                         reads=[keT[p_].b, qeT[p_].b], writes=[ps_a.b])
                    P.op("dve", f_tt(aTm[p_].ap, ps_a.ap[:, 0:256].rearrange("p (h c) -> p h c", h=2),
                                     MI.ap.unsqueeze(1).broadcast_to([128, 2, 128]), ALU.mult),
                         reads=[ps_a.b, MI.b], writes=[aTm[p_].b])
                def post(n_, i):
                    p_ = n_ % NP_
                    ts_ = slice(i * 128, (i + 1) * 128)
                    hb = hn_b[i // 4]
                    grp = []
                    for h in range(2):
                        for vb in range(2):
                            blk = h * 2 + vb
                            o_ = ps_o.ap[:, blk * 128:(blk + 1) * 128]
                            grp.append((o_, vt[p_].ap[:, h * 256 + vb * 128:h * 256 + (vb + 1) * 128], aTm[p_].ap[:, h, :],
                                        True, False))
                            grp.append((o_, Sb.ap[:, h, vb * 128:(vb + 1) * 128], qeT[p_].ap[:, h, :], False, True))
                    P.op("pe", f_mm(grp), reads=[vt[p_].b, aTm[p_].b, Sb.b, qeT[p_].b], writes=[ps_o.b])
                    P.op("pe", f_mm([(ps_ckv.ap[:, h * 256:(h + 1) * 256], kd[p_].ap[:, h * 128:(h + 1) * 128],
                                      vt[p_].ap[:, h * 256:(h + 1) * 256], True, True) for h in range(2)]),
                         reads=[kd[p_].b, vt[p_].b], writes=[ps_ckv.b])
                    if final:
                        grp = []
                        for blk in range(4):
                            grp += acc_group(ps_r.ap[:, blk * 128:(blk + 1) * 128],
                                             [(wg.ap[:, ck, 1024 + blk * 128:1024 + (blk + 1) * 128], hn.ap[:, ck, ts_])
                                              for ck in range(8)])
                        P.op("pe", f_mm(grp), reads=[wg.b, hb], writes=[ps_r.b])
                    for h in range(2):
                        P.op("dve", f_stt(Sf.ap[:, h, :], Sf.ap[:, h, :], E1[p_].ap[:, h, lastc:lastc + 1],
                                          ps_ckv.ap[:, h * 256:(h + 1) * 256], ALU.mult, ALU.add),
                             reads=[Sf.b, E1[p_].b, ps_ckv.b], writes=[Sf.b])
                    P.op("act", f_copy_act(Sb.ap, Sf.ap), reads=[Sf.b], writes=[Sb.b])
                    ov = ps_o.ap.rearrange("p (b c) -> p b c", b=4)
                    if not final:
                        P.op("act", f_copy_act(oX.ap[:, :, ts_], ov), reads=[ps_o.b], writes=[oX_b[i]])
                        return
                    P.op("dve", f_tt(ot.ap, ov, oX.ap[:, :, ts_], ALU.add), reads=[ps_o.b, oX_b[i]], writes=[ot.b])
                    P.op("act", f_act(osq.ap, ot.ap, AF.Square), reads=[ot.b], writes=[osq.b])
                    P.op("act", f_act(sg.ap, ps_r.ap, AF.Exp, scale=-1.0), reads=[ps_r.b], writes=[sg.b])
                    P.op("act", f_act(sg.ap, sg.ap, AF.Ln, bias=1.0), reads=[sg.b], writes=[sg.b])
                    P.op("act", f_act(sg.ap, sg.ap, AF.Exp, scale=-1.0), reads=[sg.b], writes=[sg.b])
                    grp = []
                    for h in range(2):
                        grp += acc_group(ps_a.ap[:, 256 + h * 128:256 + (h + 1) * 128],
                                         [(onesb.ap, osq.ap[:, h * 2 + vb, :]) for vb in range(2)])
                    P.op("pe", f_mm(grp), reads=[osq.b, onesb.b], writes=[ps_a.b])
                    P.op("act", f_act(lnv.ap[:, 0:256], ps_a.ap[:, 256:512], AF.Ln, scale=1.0 / 256.0, bias=EPS),
                         reads=[ps_a.b], writes=[lnv.b])
                    P.op("act", f_act(rstd.ap[:, 0:256], lnv.ap[:, 0:256], AF.Exp, scale=-0.5), reads=[lnv.b], writes=[rstd.b])
                    for blk in range(4):
                        h, vb = blk // 2, blk % 2
                        P.op("dve", f_stt(on.ap[:, blk, :], ot.ap[:, blk, :], og.ap[:, vb:vb + 1],
                                          rstd.ap[:, h * 128:(h + 1) * 128], ALU.mult, ALU.mult),
                             reads=[ot.b, og.b, rstd.b], writes=[on.b])
                    P.op("dve", f_tt(sg.ap, ps_r.ap, sg.ap, ALU.mult), reads=[ps_r.b, sg.b], writes=[sg.b])
                    P.op("dve", f_tt(oX.ap[:, :, ts_], on.ap, sg.ap.rearrange("p (b c) -> p b c", b=4), ALU.mult),
                         reads=[on.b, sg.b], writes=[oX_b[i]])

                pre(0, order[0])
                for n_, i in enumerate(order):
                    if n_ + 1 < len(order):
                        pre(n_ + 1, order[n_ + 1])
                    post(n_, i)

            for g in range(2):
                P.dma("pool", f_dma(wg.ap, wg_d[g], True), writes=[wg.b])
                P.dma("pool", f_dma(wo2.ap, wo2_d[g], True), writes=[wo2.b])
                P.op("dve", lambda e: e.memset(Sf.ap.rearrange("p a b -> p (a b)"), 0.0), writes=[Sf.b])
                P.op("pool", lambda e: e.memset(Sb.ap.rearrange("p a b -> p (a b)"), 0.0), writes=[Sb.b])
                one_pass(g, 0, list(range(16)), False)
                ibb, obb = Buf(), Buf()
                sfl = Sf.ap.rearrange("p a b -> p (a b)")
                P.dma("pool", f_dma(ib_t[g].ap(), sfl), reads=[Sf.b], writes=[ibb])
                P.custom("pool", (lambda g=g: (lambda e: e.collective_compute(
                    "AllGather", ALU.bypass, replica_groups=[[0, 1], [2, 3], [4, 5], [6, 7]],
                    ins=[ib_t[g].ap().opt()], outs=[ob_t[g].ap().opt()])))(), "cc%d" % g, reads=[ibb], writes=[obb])
                P.dma("pool", f_dma(rx.ap, ob_t[g].ap().rearrange("(r p) c -> p r c", p=128)), reads=[obb], writes=[rx.b])
                P.op("dve", f_ts(sfl, rx.ap[:, 0, :], mex.ap[:, 0:1], ALU.mult), reads=[rx.b, mex.b], writes=[Sf.b])
                P.op("dve", f_stt(sfl, rx.ap[:, 1, :], mex.ap[:, 1:2], sfl, ALU.mult, ALU.add),
                     reads=[rx.b, mex.b, Sf.b], writes=[Sf.b])
                P.op("act", f_copy_act(Sb.ap, Sf.ap), reads=[Sf.b], writes=[Sb.b])
                one_pass(g, 1, list(range(15, -1, -1)), True)
                k = 0
                for oc in range(8):
                    for tt in range(4):
                        tsl = slice(tt * 512, (tt + 1) * 512)
                        ps = [ps_qk, ps_kl, ps_v, ps_cr][k % 4]
                        k += 1
                        P.op("pe", f_mm(acc_group(ps.ap, [(wo2.ap[:, blk, oc * 128:(oc + 1) * 128], oX.ap[:, blk, tsl])
                                                          for blk in range(4)])),
                             reads=[wo2.b] + oX_b[tt * 4:(tt + 1) * 4], writes=[ps.b])
                        P.op("dve", f_tt(hT.ap[:, oc, tsl], hT.ap[:, oc, tsl], ps.ap, ALU.add),
                             reads=[ps.b, hT_b[oc][tt]], writes=[hT_b[oc][tt]])
                P.barrier()
            top[0] = m_phase

        def output_phase(do_norm):
            m_phase = top[0]
            lnv = alloc(F32, 512)
            rstd = alloc(F32, 512)
            sq = alloc(BF16, 8, 512)
            yo = [alloc(F32, 8, 512) for _ in range(2)]
            toks = []
            for tt in range(4):
                tsl = slice(tt * 512, (tt + 1) * 512)
                xb = [hT_b[oc][tt] for oc in range(8)]
                if do_norm:
                    y = yo[tt % 2]
                    import os
                    mode = os.environ.get("KDBG_OUT", "")
                    if mode == "B":
                        P.op("dve", f_copy(y.ap, hT.ap[:, :, tsl]), reads=xb, writes=[y.b])
                    else:
                        rmsnorm_group(hT.ap[:, :, tsl], xb, 4, y.ap, [y.b], 512, psbank(tt % 4), lnv, rstd, sq=sq.ap, sqbufs=[sq.b])
                    if mode == "A":
                        toks.append(P.dma("sp", f_dma(y_d[:, :, tsl], hT.ap[:, :, tsl]), reads=xb + [y.b]))
                    else:
                        toks.append(P.dma("sp", f_dma(y_d[:, :, tsl], y.ap), reads=[y.b]))
                else:
                    toks.append(P.dma("sp", f_dma(y_d[:, :, tsl], hT.ap[:, :, tsl]), reads=xb))
            top[0] = m_phase

        phases = [("attn", attention_phase), ("mlp1", lambda: mlp_phase(0, 1)), ("gla", gla_phase),
                  ("mlp2", lambda: mlp_phase(1, 3))]
        done = False
        if only is not None:
            for name, fn in phases:
                if name in only:
                    fn()
            output_phase("final" in only)
            done = True
        else:
            for name, fn in phases:
                fn()
                if stop_after == name:
                    output_phase(False)
                    done = True
                    break
        if not done:
            output_phase(True)
        P.barrier()

        import contextlib
        with contextlib.ExitStack() as es:
            sems = {}
            for k in P.count.keys():
                sems[k] = es.enter_context(nc.semaphore("s_" + k))
            block = es.enter_context(nc.Block())

            def replay(name, e):
                for item in P.lists[name]:
                    if item[0] == "w":
                        e.wait_ge(sems[item[1]], item[2])
                    else:
                        inst = item[1](e)
                        if item[0] == "c":
                            inst.then_inc(sems[item[2]])
                        else:
                            inst.then_inc(sems[item[2]], item[3])

            @block.tensor
            def _(e):
                replay("pe", e)

            @block.scalar
            def _(e):
                replay("act", e)

            @block.vector
            def _(e):
                replay("dve", e)

            @block.gpsimd
            def _(e):
                replay("pool", e)

            @block.sync
            def _(e):
                replay("sp", e)
    return nc


def _fm(w):
    K, N = w.shape
    return np.ascontiguousarray(w.reshape(K // 128, 128, N).transpose(1, 0, 2))


def _prep(inputs):
    f = lambda a: np.asarray(a, dtype=np.float32)
    x = f(inputs["x"])
    norm_mix, norm_mlp, final_norm = f(inputs["norm_mix"]), f(inputs["norm_mlp"]), f(inputs["final_norm"])
    wqkv = f(inputs["attn_w_qkv"])[0]
    qn, kn = f(inputs["attn_q_norm"])[0], f(inputs["attn_k_norm"])[0]
    wo = f(inputs["attn_w_o"])[0]
    gw = f(inputs["gla_w_in"])[0]
    gup, gb = f(inputs["gla_w_gate_up"])[0], f(inputs["gla_b_gate"])[0]
    gon, gwo = f(inputs["gla_out_norm"])[0], f(inputs["gla_w_o"])[0]
    mwi, mwo = f(inputs["mlp_w_in"]), f(inputs["mlp_w_out"])

    gains = np.stack([norm_mix[0], norm_mlp[0], norm_mix[1], norm_mlp[1], final_norm], 0)
    gains_l = np.ascontiguousarray(gains.reshape(5, 8, 128).transpose(2, 0, 1)).reshape(128, 40)
    qkg_l = np.ascontiguousarray(np.broadcast_to(np.concatenate([qn, kn])[None, :], (128, 128)))
    qcols = np.concatenate([np.arange(h * 64, (h + 1) * 64) for h in HO])
    wq_l = _fm(wqkv[:, :1024][:, qcols])
    wkv_l = _fm(wqkv[:, 1024:1536])
    wo_l = _fm(wo[qcols, :])
    win_l = np.stack([np.stack([_fm(mwi[l][:, G * 512:(G + 1) * 512]) for G in range(8)]) for l in range(2)])
    wout_l = np.stack([np.stack([_fm(mwo[l][G * 512:(G + 1) * 512, :]) for G in range(8)]) for l in range(2)])
    wg_l = []
    for g in range(2):
        cols = np.concatenate([np.arange(g * 256, (g + 1) * 256), 512 + np.arange(g * 256, (g + 1) * 256),
                               1024 + np.arange(g * 512, (g + 1) * 512), 2048 + np.arange(g * 512, (g + 1) * 512)])
        wg_l.append(_fm(gw[:, cols]))
    wg_l = np.stack(wg_l)
    og_l = np.ascontiguousarray(gon.reshape(2, 128).T)
    wo2_l = np.stack([_fm(gwo[g * 512:(g + 1) * 512, :]) for g in range(2)])

    maps = []
    idxs = []
    for c in range(NCORES):
        b, s = c // 2, c % 2
        if s == 0:
            own = np.arange(0, T)
            other = np.arange(T, TA)
            dirs = (0, 1)
        else:
            own = np.arange(TA - 1, T - 1, -1)
            other = np.arange(0, T)
            dirs = (1, 0)
        idx = np.concatenate([own, other])
        idxs.append(own)
        xT_l = np.ascontiguousarray(x[b][idx].T.reshape(8, 128, TA).transpose(1, 0, 2))
        pr = (idx // 64).astype(np.float32)
        pc = (idx % 64).astype(np.float32)
        pos_l = np.ascontiguousarray(np.stack([pr.reshape(32, 128).T, pc.reshape(32, 128).T], -1)).reshape(128, 64)
        zc = np.concatenate([3072 + dirs[0] * 16 + np.arange(16), 3072 + dirs[1] * 16 + np.arange(16)])
        wz_l = _fm(gw[:, zc])
        wup_l = np.zeros((33, 2, 512), np.float32)
        wup_l[0:16, 0] = gup[dirs[0]]
        wup_l[16:32, 1] = gup[dirs[1]]
        wup_l[32, 0] = gb[dirs[0]]
        wup_l[32, 1] = gb[dirs[1]]
        mex_l = np.zeros((128, 2), np.float32)
        mex_l[:, 1 - s] = 1.0
        maps.append({
            "xT": xT_l, "pos": pos_l, "gains": gains_l, "qkg": qkg_l, "wq": wq_l, "wkv": wkv_l, "wo": wo_l,
            "win": win_l, "wout": wout_l, "wg": wg_l, "wz": wz_l, "wup": wup_l, "og": og_l, "wo2": wo2_l,
            "mex": mex_l,
        })
    return maps, idxs


_NC_CACHE = {}


def run(inputs, stop_after=None, trace=False, only=None):
    maps, idxs = _prep(inputs)
    key = (stop_after, only)
    if key not in _NC_CACHE:
        _NC_CACHE[key] = build_program(stop_after, only)
    nc = _NC_CACHE[key]
    res = run_bass_kernel_spmd(nc, maps, core_ids=list(range(NCORES)), trace=trace)
    out = np.empty((4, TA, D), np.float32)
    for c in range(NCORES):
        y = np.asarray(res.results[c]["y"])
        out[c // 2, idxs[c], :] = y.transpose(2, 1, 0).reshape(T, D)
    return out, res


def kernel(**inputs):
    out, _ = run(inputs)
    return out
```
